# Optimizing a Trainium2 kernel written in Bass

```python
import math
import jax, jax.numpy as jnp
from jax import lax
import numpy as np

D_MODEL = 2048
BATCH = 2
SEQ = 4096
DEPTH = 4
DEC_BATCH = 8
DEC_SEQ = 1
PAST_LEN = 16384
PAGE_SIZE = 128

CONV_WIDTH_A = 31
W_A = D_MODEL // 2
N_GROUPS_A = 8
SB_HEAD_DIM = 128
SB_HEADS = (D_MODEL // 2) // SB_HEAD_DIM
W_B = SB_HEADS * SB_HEAD_DIM
SB_BLOCK = 128
SB_BIAS_INIT = -7.0
IN_EVEN = 3 * W_A + 4 * W_B
GDN_HEAD_DIM = 128
GDN_QK_HEADS = D_MODEL // GDN_HEAD_DIM
GDN_V_HEADS = 2 * GDN_QK_HEADS
GDN_QK_W = GDN_QK_HEADS * GDN_HEAD_DIM
GDN_V_W = GDN_V_HEADS * GDN_HEAD_DIM
GDN_CONV_WIDTH = 4
GDN_CONV_DIM = 2 * GDN_QK_W + GDN_V_W
GDN_CHUNK = 64
IN_ODD = GDN_CONV_DIM + GDN_V_W + 2 * GDN_V_HEADS
N_EVEN = (DEPTH + 1) // 2
N_ODD = DEPTH // 2
DEEPNORM_ALPHA = (2 * DEPTH) ** 0.25
DEEPNORM_BETA = (8 * DEPTH) ** -0.25
LN_EPS = 1e-5
RMS_EPS = 1e-6
F32 = jnp.float32

kernel_name = "hybrid_conv_stickbreak_gdn_decoder_step"


def _layer_norm(x, g, b):
    xf = x.astype(F32)
    mu = jnp.mean(xf, axis=-1, keepdims=True)
    var = jnp.mean(jnp.square(xf - mu), axis=-1, keepdims=True)
    y = (xf - mu) * lax.rsqrt(var + LN_EPS)
    return (y * g.astype(F32) + b.astype(F32)).astype(x.dtype)


def _group_norm_tokens(x, g, b):
    shp = x.shape
    xf = x.astype(F32).reshape(shp[:-1] + (N_GROUPS_A, shp[-1] // N_GROUPS_A))
    mu = jnp.mean(xf, axis=-1, keepdims=True)
    var = jnp.mean(jnp.square(xf - mu), axis=-1, keepdims=True)
    y = ((xf - mu) * lax.rsqrt(var + LN_EPS)).reshape(shp)
    return (y * g.astype(F32) + b.astype(F32)).astype(x.dtype)


def _l2_normalize(x):
    return x * lax.rsqrt(jnp.sum(jnp.square(x), axis=-1, keepdims=True) + RMS_EPS)


def _causal_depthwise_conv(x, buf, w):
    full = jnp.concatenate([buf.astype(x.dtype), x], axis=1)
    y = lax.conv_general_dilated(full, w.astype(x.dtype)[:, None, :], window_strides=(1,), padding='VALID',
                                 dimension_numbers=('NWC', 'WIO', 'NWC'), feature_group_count=x.shape[-1])
    return y, full[:, full.shape[1] - (w.shape[0] - 1):]


def _adaln(c, w_ada, b_ada):
    mod = jax.nn.silu(c) @ w_ada + b_ada
    shift, scale, gate = jnp.split(mod, 3, axis=-1)
    return shift[:, None, :], scale[:, None, :], gate[:, None, :]


def _stick_breaking_block(q, q_pos, k, v, k_pos, bias):
    z = jnp.einsum('bqhd,bkhd->bhqk', q.astype(F32), k.astype(F32)) * (SB_HEAD_DIM ** -0.5)
    z = z + bias.astype(F32)[None, :, None, None]
    visible = k_pos[None, :] < q_pos[:, None]
    neg_log_keep = jnp.where(visible, jax.nn.softplus(z), 0.0)
    later = lax.cumsum(neg_log_keep, axis=3, reverse=True) - neg_log_keep
    weights = jnp.where(visible, jnp.exp(jax.nn.log_sigmoid(z) - later), 0.0)
    return jnp.einsum('bhqk,bkhd->bqhd', weights, v.astype(F32))


def _stick_breaking_prompt(q, k, v, bias):
    b, t, h, d = q.shape
    nb = t // SB_BLOCK
    pos = jnp.arange(t)
    qb = jnp.moveaxis(q.reshape(b, nb, SB_BLOCK, h, d), 1, 0)
    pb = pos.reshape(nb, SB_BLOCK)
    ob = lax.map(lambda a: _stick_breaking_block(a[0], a[1], k, v, pos, bias), (qb, pb))
    return jnp.moveaxis(ob, 0, 1).reshape(b, t, h, d)


def _stick_breaking_with_past(k_past, v_past):
    def attend(q, k, v, bias):
        past = k_past.shape[1]
        kk = jnp.concatenate([k_past.astype(k.dtype), k], axis=1)
        vv = jnp.concatenate([v_past.astype(v.dtype), v], axis=1)
        q_pos = past + jnp.arange(q.shape[1])
        return _stick_breaking_block(q, q_pos, kk, vv, jnp.arange(kk.shape[1]), bias)
    return attend


def _even_mixer(u, conv_buf, attend, w_in, conv_w, gn_g, gn_b, sb_bias, w_out):
    b, t, _ = u.shape
    h = u @ w_in
    a_val, a_glu, a_gate, q, k, v, b_gate = jnp.split(
        h, [W_A, 2 * W_A, 3 * W_A, 3 * W_A + W_B, 3 * W_A + 2 * W_B, 3 * W_A + 3 * W_B], axis=-1)
    glu = a_val * jax.nn.sigmoid(a_glu)
    y_a, new_buf = _causal_depthwise_conv(glu, conv_buf, conv_w)
    y_a = jax.nn.silu(_group_norm_tokens(y_a, gn_g, gn_b)) * jax.nn.silu(a_gate)
    q = q.reshape(b, t, SB_HEADS, SB_HEAD_DIM)
    k = k.reshape(b, t, SB_HEADS, SB_HEAD_DIM)
    v = v.reshape(b, t, SB_HEADS, SB_HEAD_DIM)
    o_b = attend(q, k, v, sb_bias)
    y_b = o_b.reshape(b, t, W_B).astype(u.dtype) * jax.nn.silu(b_gate)
    out = jnp.concatenate([y_a, y_b], axis=-1) @ w_out
    return out, new_buf, k, v


def _gated_delta_chunked(q, k, v, g, beta, s0):
    b, t, h, _ = q.shape
    dv = v.shape[-1]
    c = GDN_CHUNK
    n = t // c

    def chunks(x):
        return jnp.moveaxis(x, 2, 1).reshape((b, h, n, c) + x.shape[3:])

    qc, kc, vc = chunks(q), chunks(k), chunks(v)
    bc = chunks(beta)
    gc = jnp.cumsum(chunks(g), axis=-1)
    idx = jnp.arange(c)
    incl = idx[:, None] >= idx[None, :]
    strict = idx[:, None] > idx[None, :]
    diff = gc[..., :, None] - gc[..., None, :]
    decay = jnp.where(incl, jnp.exp(jnp.where(incl, diff, 0.0)), 0.0)
    kb = kc * bc[..., None]
    lmat = jnp.where(strict, jnp.einsum('bhnid,bhnjd->bhnij', kb, kc) * decay, 0.0)
    eye = jnp.eye(c, dtype=F32)
    tmat = lax.linalg.triangular_solve(lmat + eye, jnp.broadcast_to(eye, lmat.shape),
                                       left_side=True, lower=True, unit_diagonal=True)
    u_val = tmat @ (vc * bc[..., None])
    w_key = tmat @ (kb * jnp.exp(gc)[..., None])
    intra = jnp.where(incl, jnp.einsum('bhnid,bhnjd->bhnij', qc, kc) * decay, 0.0)
    q_dec = qc * jnp.exp(gc)[..., None]
    k_dec = kc * jnp.exp(gc[..., -1:] - gc)[..., None]
    g_last = jnp.exp(gc[..., -1])

    def step(s, xs):
        u_n, w_n, qd_n, kd_n, a_n, gl_n = xs
        v_new = u_n - w_n @ s
        o_n = qd_n @ s + a_n @ v_new
        s = s * gl_n[..., None, None] + jnp.swapaxes(kd_n, -1, -2) @ v_new
        return s, o_n

    xs = tuple(jnp.moveaxis(x, 2, 0) for x in (u_val, w_key, q_dec, k_dec, intra, g_last))
    s_fin, o = lax.scan(step, s0, xs)
    o = jnp.moveaxis(o, 0, 2).reshape(b, h, t, dv)
    return jnp.moveaxis(o, 1, 2), s_fin


def _gated_delta_recurrent(q, k, v, g, beta, s0):
    def step(s, xs):
        q_t, k_t, v_t, g_t, b_t = xs
        s = s * jnp.exp(g_t)[..., None, None]
        delta = (v_t - jnp.einsum('bhkv,bhk->bhv', s, k_t)) * b_t[..., None]
        s = s + jnp.einsum('bhk,bhv->bhkv', k_t, delta)
        return s, jnp.einsum('bhkv,bhk->bhv', s, q_t)

    xs = tuple(jnp.moveaxis(x, 1, 0) for x in (q, k, v, g, beta))
    s_fin, o = lax.scan(step, s0, xs)
    return jnp.moveaxis(o, 0, 1), s_fin


def _odd_mixer(u, conv_buf, s0, recurrent, w_in, conv_w, a_log, dt_bias, gnorm_w, w_out):
    b, t, _ = u.shape
    h = u @ w_in
    qkv, z, beta_raw, a_raw = jnp.split(
        h, [GDN_CONV_DIM, GDN_CONV_DIM + GDN_V_W, GDN_CONV_DIM + GDN_V_W + GDN_V_HEADS], axis=-1)
    qkv, new_buf = _causal_depthwise_conv(qkv, conv_buf, conv_w)
    qkv = jax.nn.silu(qkv)
    q, k, v = jnp.split(qkv, [GDN_QK_W, 2 * GDN_QK_W], axis=-1)
    rep = GDN_V_HEADS // GDN_QK_HEADS
    q = _l2_normalize(q.astype(F32).reshape(b, t, GDN_QK_HEADS, GDN_HEAD_DIM))
    k = _l2_normalize(k.astype(F32).reshape(b, t, GDN_QK_HEADS, GDN_HEAD_DIM))
    q = jnp.repeat(q, rep, axis=2) * (GDN_HEAD_DIM ** -0.5)
    k = jnp.repeat(k, rep, axis=2)
    v = v.astype(F32).reshape(b, t, GDN_V_HEADS, GDN_HEAD_DIM)
    beta = jax.nn.sigmoid(beta_raw.astype(F32))
    g = -jnp.exp(a_log.astype(F32)) * jax.nn.softplus(a_raw.astype(F32) + dt_bias.astype(F32))
    if recurrent:
        o, s_fin = _gated_delta_recurrent(q, k, v, g, beta, s0)
    else:
        o, s_fin = _gated_delta_chunked(q, k, v, g, beta, s0)
    o = o * lax.rsqrt(jnp.mean(jnp.square(o), axis=-1, keepdims=True) + RMS_EPS) * gnorm_w.astype(F32)
    o = o * jax.nn.silu(z.astype(F32).reshape(b, t, GDN_V_HEADS, GDN_HEAD_DIM))
    out = o.reshape(b, t, GDN_V_W).astype(u.dtype) @ w_out
    return out, new_buf, s_fin


def setup_inputs(seed: int = 0) -> dict:
    key = jax.random.key(seed)
    ks = jax.random.split(key, 26)
    n_pages = PAST_LEN // PAGE_SIZE
    n_used = DEC_BATCH * n_pages
    n_pool = n_used + (n_used + 3) // 4

    def nrm(k, shape, s):
        return jax.random.normal(k, shape, F32) * s

    x_prompt = nrm(ks[0], (BATCH, SEQ, D_MODEL), 1.0)
    x_sample = nrm(ks[1], (DEC_BATCH, DEC_SEQ, D_MODEL), 1.0)
    c_prompt = nrm(ks[2], (BATCH, D_MODEL), 1.0)
    c_sample = nrm(ks[3], (DEC_BATCH, D_MODEL), 1.0)
    cache_k = nrm(ks[4], (N_EVEN, n_pool, PAGE_SIZE, SB_HEADS, SB_HEAD_DIM), 1.0)
    cache_v = nrm(ks[5], (N_EVEN, n_pool, PAGE_SIZE, SB_HEADS, SB_HEAD_DIM), 1.0)
    page_table = jax.random.permutation(ks[6], n_pool)[:n_used].reshape(DEC_BATCH, n_pages).astype(jnp.int32)
    state_conv_a = nrm(ks[7], (N_EVEN, DEC_BATCH, CONV_WIDTH_A - 1, W_A), 0.5)
    state_conv_c = nrm(ks[8], (N_ODD, DEC_BATCH, GDN_CONV_WIDTH - 1, GDN_CONV_DIM), 1.0)
    state_delta = nrm(ks[9], (N_ODD, DEC_BATCH, GDN_V_HEADS, GDN_HEAD_DIM, GDN_HEAD_DIM), 0.3)
    w_ada = nrm(ks[10], (DEPTH, D_MODEL, 3 * D_MODEL), 0.5 * D_MODEL ** -0.5)
    b_ada = nrm(ks[11], (DEPTH, 3 * D_MODEL), 0.01)
    ln_g = 1.0 + nrm(ks[12], (DEPTH, D_MODEL), 0.02)
    ln_b = nrm(ks[13], (DEPTH, D_MODEL), 0.02)
    w_in_even = nrm(ks[14], (N_EVEN, D_MODEL, IN_EVEN), D_MODEL ** -0.5)
    conv_w_a = nrm(ks[15], (N_EVEN, CONV_WIDTH_A, W_A), CONV_WIDTH_A ** -0.5)
    gn_g_a = 1.0 + nrm(ks[16], (N_EVEN, W_A), 0.02)
    gn_b_a = nrm(ks[17], (N_EVEN, W_A), 0.02)
    sb_bias = SB_BIAS_INIT + nrm(ks[25], (N_EVEN, SB_HEADS), 0.5)
    w_out_even = nrm(ks[18], (N_EVEN, W_A + W_B, D_MODEL), DEEPNORM_BETA * (W_A + W_B) ** -0.5)
    w_in_odd = nrm(ks[19], (N_ODD, D_MODEL, IN_ODD), D_MODEL ** -0.5)
    conv_w_c = nrm(ks[20], (N_ODD, GDN_CONV_WIDTH, GDN_CONV_DIM), GDN_CONV_WIDTH ** -0.5)
    a_log_c = jnp.log(jax.random.uniform(ks[21], (N_ODD, GDN_V_HEADS), F32, 1.0, 16.0))
    dt = jnp.exp(jax.random.uniform(ks[22], (N_ODD, GDN_V_HEADS), F32, math.log(1e-3), math.log(1e-1)))
    dt_bias_c = dt + jnp.log(-jnp.expm1(-dt))
    gnorm_w_c = 1.0 + nrm(ks[23], (N_ODD, GDN_HEAD_DIM), 0.02)
    w_out_odd = nrm(ks[24], (N_ODD, GDN_V_W, D_MODEL), DEEPNORM_BETA * GDN_V_W ** -0.5)
    return {"x_prompt": x_prompt, "x_sample": x_sample, "c_prompt": c_prompt, "c_sample": c_sample,
            "cache_k": cache_k, "cache_v": cache_v, "page_table": page_table,
            "state_conv_a": state_conv_a, "state_conv_c": state_conv_c, "state_delta": state_delta,
            "w_ada": w_ada, "b_ada": b_ada, "ln_g": ln_g, "ln_b": ln_b,
            "w_in_even": w_in_even, "conv_w_a": conv_w_a, "gn_g_a": gn_g_a, "gn_b_a": gn_b_a,
            "sb_bias": sb_bias, "w_out_even": w_out_even, "w_in_odd": w_in_odd, "conv_w_c": conv_w_c,
            "a_log_c": a_log_c, "dt_bias_c": dt_bias_c, "gnorm_w_c": gnorm_w_c, "w_out_odd": w_out_odd}


def reference(x_prompt, x_sample, c_prompt, c_sample, cache_k, cache_v, page_table,
              state_conv_a, state_conv_c, state_delta, w_ada, b_ada, ln_g, ln_b,
              w_in_even, conv_w_a, gn_g_a, gn_b_a, sb_bias, w_out_even,
              w_in_odd, conv_w_c, a_log_c, dt_bias_c, gnorm_w_c, w_out_odd):
    n_seq, n_pages = page_table.shape
    past_len = n_pages * cache_k.shape[2]
    bp = x_prompt.shape[0]
    xp, xs = x_prompt, x_sample
    kp_l, vp_l, ks_l, vs_l = [], [], [], []
    cap_l, cas_l, ccp_l, ccs_l, dp_l, ds_l = [], [], [], [], [], []
    for layer in range(DEPTH):
        sh_p, sc_p, gt_p = _adaln(c_prompt, w_ada[layer], b_ada[layer])
        sh_s, sc_s, gt_s = _adaln(c_sample, w_ada[layer], b_ada[layer])
        up = xp * (1 + sc_p) + sh_p
        us = xs * (1 + sc_s) + sh_s
        j = layer // 2
        if layer % 2 == 0:
            prm = (w_in_even[j], conv_w_a[j], gn_g_a[j], gn_b_a[j], sb_bias[j], w_out_even[j])
            buf0 = jnp.zeros((bp, CONV_WIDTH_A - 1, W_A), xp.dtype)
            hp, buf_p, k_p, v_p = _even_mixer(up, buf0, _stick_breaking_prompt, *prm)
            k_past = cache_k[j][page_table].reshape(n_seq, past_len, SB_HEADS, SB_HEAD_DIM)
            v_past = cache_v[j][page_table].reshape(n_seq, past_len, SB_HEADS, SB_HEAD_DIM)
            hs, buf_s, k_s, v_s = _even_mixer(us, state_conv_a[j], _stick_breaking_with_past(k_past, v_past), *prm)
            kp_l.append(k_p)
            vp_l.append(v_p)
            ks_l.append(k_s)
            vs_l.append(v_s)
            cap_l.append(buf_p)
            cas_l.append(buf_s.astype(state_conv_a.dtype))
        else:
            prm = (w_in_odd[j], conv_w_c[j], a_log_c[j], dt_bias_c[j], gnorm_w_c[j], w_out_odd[j])
            buf0 = jnp.zeros((bp, GDN_CONV_WIDTH - 1, GDN_CONV_DIM), xp.dtype)
            s0 = jnp.zeros((bp, GDN_V_HEADS, GDN_HEAD_DIM, GDN_HEAD_DIM), F32)
            hp, buf_p, s_p = _odd_mixer(up, buf0, s0, False, *prm)
            hs, buf_s, s_s = _odd_mixer(us, state_conv_c[j], state_delta[j].astype(F32), True, *prm)
            ccp_l.append(buf_p)
            ccs_l.append(buf_s.astype(state_conv_c.dtype))
            dp_l.append(s_p.astype(state_delta.dtype))
            ds_l.append(s_s.astype(state_delta.dtype))
        xp = _layer_norm(DEEPNORM_ALPHA * xp + (1 + gt_p) * hp, ln_g[layer], ln_b[layer])
        xs = _layer_norm(DEEPNORM_ALPHA * xs + (1 + gt_s) * hs, ln_g[layer], ln_b[layer])
    return (xp, xs, jnp.stack(kp_l), jnp.stack(vp_l), jnp.stack(ks_l), jnp.stack(vs_l),
            jnp.stack(cap_l), jnp.stack(cas_l), jnp.stack(ccp_l), jnp.stack(ccs_l),
            jnp.stack(dp_l), jnp.stack(ds_l))
```

```python
import contextlib
import numpy as np
import concourse.bass as bass
import concourse.mybir as mybir
from concourse.bass_utils import run_bass_kernel_spmd

F32 = mybir.dt.float32
BF16 = mybir.dt.bfloat16
I32 = mybir.dt.int32
AF = mybir.ActivationFunctionType
ALU = mybir.AluOpType

D = 2048
KC = 16
W_A = 1024
W_B = 1024
IN_EVEN = 7168
IN_ODD = 12352
CONVW_A = 31
HD = 128
SBH = 8
GV = 32
GQ = 16
ALPHA = 8 ** 0.25
LN_EPS = 1e-5
RMS_EPS = 1e-6
NEG = -30000.0


class Tk:
    __slots__ = ("lastw", "readers")

    def __init__(self):
        self.lastw = []
        self.readers = []


class Prog:
    def __init__(self, nc, st, n_dma_ch=12):
        self.nc = nc
        self.eng = {"pe": nc.tensor, "act": nc.scalar, "dve": nc.vector, "pool": nc.gpsimd, "sp": nc.sync}
        self.cnt = {e: 0 for e in ("pe", "act", "dve", "pool")}
        self.known = {e: {} for e in self.eng}
        self.n_dma_ch = n_dma_ch
        self.chcnt = [0] * n_dma_ch
        self.chrr = {"sp": 0, "pool": 0}
        self.chown = {"sp": list(range(0, n_dma_ch - 4)), "pool": list(range(n_dma_ch - 4, n_dma_ch))}
        self.sems = {}
        for n in ["pe", "act", "dve", "pool"] + ["d%d" % c for c in range(n_dma_ch)]:
            self.sems[n] = st.enter_context(nc.semaphore("s_" + n))

    def _deps(self, eng, reads, writes):
        w = {}
        for t in reads:
            for s, v in t.lastw:
                if w.get(s, 0) < v:
                    w[s] = v
        for t in writes:
            for s, v in t.lastw:
                if w.get(s, 0) < v:
                    w[s] = v
            for s, v in t.readers:
                if w.get(s, 0) < v:
                    w[s] = v
        kn = self.known[eng]
        out = []
        for s, v in w.items():
            if eng == "pe" and s == "pe":
                continue
            if kn.get(s, 0) >= v:
                continue
            kn[s] = v
            out.append((s, v))
        return out

    def _mark(self, tok, reads, writes):
        for t in writes:
            t.lastw = [tok]
            t.readers = []
        for t in reads:
            if t not in writes:
                t.readers.append(tok)
                if len(t.readers) > 40:
                    m = {}
                    for s, v in t.readers:
                        if m.get(s, 0) < v:
                            m[s] = v
                    t.readers = list(m.items())

    def op(self, eng, fn, reads=(), writes=()):
        e = self.eng[eng]
        for s, v in self._deps(eng, reads, writes):
            e.wait_ge(self.sems[s], v)
        self.cnt[eng] += 1
        tok = (eng, self.cnt[eng])
        self._mark(tok, reads, writes)
        fn(e).then_inc(self.sems[eng], 1)
        return tok

    def dma(self, q, fn, reads=(), writes=()):
        e = self.eng[q]
        chs = self.chown[q]
        c = chs[self.chrr[q] % len(chs)]
        self.chrr[q] += 1
        waits = self._deps(q, reads, writes)
        sname = "d%d" % c
        prev = self.chcnt[c] * 16
        kn = self.known[q]
        if prev > 0 and kn.get(sname, 0) < prev:
            kn[sname] = prev
            waits.append((sname, prev))
        for s, v in waits:
            e.wait_ge(self.sems[s], v)
        self.chcnt[c] += 1
        tok = (sname, self.chcnt[c] * 16)
        self._mark(tok, reads, writes)
        fn(e).then_inc(self.sems[sname], 16)
        return tok

    def barrier(self):
        allw = []
        for c in range(self.n_dma_ch):
            if self.chcnt[c]:
                allw.append(("d%d" % c, self.chcnt[c] * 16))
        for en in ("pe", "act", "dve", "pool"):
            if self.cnt[en]:
                allw.append((en, self.cnt[en]))
        for en, e in self.eng.items():
            kn = self.known[en]
            for s, v in allw:
                if kn.get(s, 0) >= v:
                    continue
                kn[s] = v
                e.wait_ge(self.sems[s], v)


def build(T, NPG, NPOOL, DEPTH):
    NE = (DEPTH + 1) // 2
    NO = DEPTH // 2
    NTT = T // 512
    NB = T // 128
    assert T % 512 == 0
    nc = bass.Bass("TRN2", target_bir_lowering=False)

    def din(name, shape, dt=F32):
        return nc.dram_tensor(name, list(shape), dt, kind="ExternalInput").ap()

    def dout(name, shape, dt=F32):
        return nc.dram_tensor(name, list(shape), dt, kind="ExternalOutput").ap()

    def dscr(name, shape, dt=F32):
        return nc.dram_tensor(name, list(shape), dt, kind="Internal").ap()

    NO1 = max(NO, 1)
    xT_in = din("xT", [D, T])
    xsT_in = din("xsT", [128, KC, 1])
    cT_in = din("cT", [128, KC, 2])
    w_ada = din("w_ada", [DEPTH, D, 3 * D])
    b_adaT = din("b_adaT", [DEPTH, 128, 48])
    ln_gT = din("ln_gT", [DEPTH, 128, KC])
    ln_bT = din("ln_bT", [DEPTH, 128, KC])
    w_in_even = din("w_in_even", [NE, D, IN_EVEN])
    w_out_even = din("w_out_even", [NE, D, D])
    conv_w_aT = din("conv_w_aT", [NE, 128, 8, CONVW_A])
    gn_gT = din("gn_gT", [NE, 128, 8])
    gn_bT = din("gn_bT", [NE, 128, 8])
    sb_biasB = din("sb_biasB", [NE, 128, SBH])
    cache_k = din("cache_k", [NE, NPOOL * 128, 1024])
    cache_v = din("cache_v", [NE, NPOOL * 128, 1024])
    ptB = din("ptB", [128, NPG], I32)
    st_conv_aT = din("st_conv_aT", [NE, W_A, 30])
    w_in_odd = din("w_in_odd", [NO1, D, IN_ODD])
    w_out_odd = din("w_out_odd", [NO1, 2 * D, D])
    conv_w_cT = din("conv_w_cT", [NO1, 128, 64, 4])
    a_logB = din("a_logB", [NO1, 128, GV])
    dt_biasB = din("dt_biasB", [NO1, 128, GV])
    gnorm_wT = din("gnorm_wT", [NO1, 128, 1])
    st_conv_cT = din("st_conv_cT", [NO1, 8192, 3])
    st_delta = din("st_delta", [NO1, GV, 128, 128])
    cpack = din("cpack", [128, 2048])
    cpack2 = din("cpack2", [128, 7 * 128])
    selpack = din("selpack", [32, GV, 128])
    yT_out = dout("yT", [D, T])
    ysT_out = dout("ysT", [128, KC, 1])
    nk_p = dout("nk_p", [NE, T, 1024])
    nv_p = dout("nv_p", [NE, T, 1024])
    nk_s = dout("nk_s", [NE, 1, 1024])
    nv_s = dout("nv_s", [NE, 1, 1024])
    ca_pT = dout("ca_pT", [NE, W_A, 30])
    ca_sT = dout("ca_sT", [NE, W_A, 30])
    cc_pT = dout("cc_pT", [NO1, 8192, 3])
    cc_sT = dout("cc_sT", [NO1, 8192, 3])
    dl_p = dout("dl_p", [NO1, GV, 128, 128])
    dl_s = dout("dl_s", [NO1, GV, 128, 128])
    import os
    DBG = bool(os.environ.get("KDBG"))
    if DBG:
        dbg_mod = dout("dbg_mod", [DEPTH, 128, 48, 2])
        dbg32 = dout("dbg32", [2, 12, 128, 128])
        dbg16 = dout("dbg16", [2, 12, 128, 128], BF16)
    X = dscr("Xscr", [D, T])
    HT = dscr("HTscr", [IN_ODD if NO else IN_EVEN, T])
    YT = dscr("YTscr", [2 * D, T], BF16)
    Vb = dscr("Vbscr", [T, 1024], BF16)
    QKVn = dscr("QKVn", [8192, T], BF16)

    with contextlib.ExitStack() as st:
        P = Prog(nc, st)

        uid = [0]

        def sbt(stack, name, shape, dt):
            uid[0] += 1
            return stack.enter_context(nc.sbuf_tensor("%s_%d" % (name, uid[0]), list(shape), dt))

        pb = [st.enter_context(nc.psum_tensor("pb%d" % i, [128, 512], F32)) for i in range(8)]
        kb = [Tk() for _ in range(8)]

        def ACT(out, in_, func, reads, writes, bias=None, scale=None, accum=None):
            kw = {}
            if bias is not None:
                kw["bias"] = bias
            if scale is not None:
                kw["scale"] = scale
            if accum is not None:
                kw["accum_out"] = accum
            return P.op("act", lambda e: e.activation(out=out, in_=in_, func=func, **kw), reads, writes)

        def TT(eng, out, a, b, op, reads, writes):
            return P.op(eng, lambda e: e.tensor_tensor(out=out, in0=a, in1=b, op=op), reads, writes)

        def TS(eng, out, a, s1, s2, op0, op1, reads, writes):
            if s2 is None:
                return P.op(eng, lambda e: e.tensor_scalar(out=out, in0=a, scalar1=s1, scalar2=None, op0=op0),
                            reads, writes)
            return P.op(eng, lambda e: e.tensor_scalar(out=out, in0=a, scalar1=s1, scalar2=s2, op0=op0, op1=op1),
                        reads, writes)

        def STT(eng, out, in0, scalar, in1, op0, op1, reads, writes):
            return P.op(eng, lambda e: e.scalar_tensor_tensor(out=out, in0=in0, scalar=scalar, in1=in1,
                                                              op0=op0, op1=op1), reads, writes)

        def CP(eng, out, in_, reads, writes):
            if eng == "act":
                return P.op("act", lambda e: e.activation(out=out, in_=in_, func=AF.Copy), reads, writes)
            return P.op(eng, lambda e: e.tensor_copy(out=out, in_=in_), reads, writes)

        def MM(out, lhsT, rhs, start, stop, reads, writes):
            return P.op("pe", lambda e: e.matmul(out, lhsT=lhsT, rhs=rhs, start=start, stop=stop,
                                                 skip_group_check=True), reads, writes)

        def DMA(q, out, in_, reads=(), writes=()):
            return P.dma(q, lambda e: e.dma_start(out=out, in_=in_), reads, writes)

        def MEMSET(eng, ap, val, writes):
            return P.op(eng, lambda e: e.memset(ap, val), (), writes)

        evac_rr = [0]

        def EVAC(out, in_, reads, writes):
            evac_rr[0] += 1
            if evac_rr[0] % 2:
                return CP("act", out, in_, reads, writes)
            return CP("dve", out, in_, reads, writes)

        cpk = sbt(st, "cpk", [128, 2048], F32)
        k_c = Tk()
        identf = cpk[:, 0:128]
        trif = cpk[:, 1152:1280]
        sellast = cpk[:, 1408:1536]
        identb = sbt(st, "identb", [128, 128], BF16)
        nuincl = sbt(st, "nuincl", [128, 128], BF16)
        mkb = sbt(st, "mkb", [128, 896], BF16)
        nones = sbt(st, "nones", [128, 128], BF16)
        onesG = sbt(st, "onesG", [128, 128], BF16)
        onesD = sbt(st, "onesD", [128, 128], BF16)
        ones1 = sbt(st, "ones1", [128, 128], BF16)
        upbig = sbt(st, "upbig", [128, 128], BF16)
        lowstrict = sbt(st, "lowstrict", [128, 128], F32)
        DMA("sp", cpk[:], cpack[:, :], writes=[k_c])
        CP("dve", identb[:], cpk[:, 0:128], [k_c], [k_c])
        CP("dve", nuincl[:], cpk[:, 128:256], [k_c], [k_c])
        CP("dve", mkb[:], cpk[:, 256:1152], [k_c], [k_c])
        CP("dve", upbig[:], cpk[:, 1280:1408], [k_c], [k_c])
        CP("dve", lowstrict[:], cpk[:, 1536:1664], [k_c], [k_c])
        MEMSET("pool", nones[:], -1.0, [k_c])
        MEMSET("pool", onesG[:], 1.0 / 128, [k_c])
        MEMSET("pool", onesD[:], 1.0 / D, [k_c])
        MEMSET("pool", ones1[:], 1.0, [k_c])
        zerob = sbt(st, "zerob", [128, 128], BF16)
        MEMSET("pool", zerob[:], 0.0, [k_c])
        epsc = sbt(st, "epsc", [128, 2], F32)
        MEMSET("pool", epsc[:, 0:1], LN_EPS, [k_c])
        MEMSET("pool", epsc[:, 1:2], RMS_EPS, [k_c])
        modT = sbt(st, "modT", [128, 48, 2], F32)
        k_mod = Tk()
        XS = sbt(st, "XS", [128, KC, 1], F32)
        k_xs = Tk()
        HS = sbt(st, "HS", [128, 100], F32)
        k_hs = Tk()
        YS = sbt(st, "YS", [128, 32, 1], BF16)
        k_ys = Tk()
        SN = sbt(st, "SN", [128, 64], F32)
        k_sn = Tk()
        DMA("sp", XS[:], xsT_in[:, :, :], writes=[k_xs])
        P.barrier()

        def phase_M(l):
            with contextlib.ExitStack() as s2:
                wst = [sbt(s2, "mw%d" % i, [128, KC, 128], F32) for i in range(2)]
                kw = [Tk(), Tk()]
                ct = sbt(s2, "mct", [128, KC, 2], F32)
                sc = sbt(s2, "msc", [128, KC, 2], F32)
                bT = sbt(s2, "mbT", [128, 48], F32)
                k1 = Tk()
                DMA("sp", ct[:], cT_in[:, :, :], writes=[k1])
                DMA("sp", bT[:], b_adaT[l], writes=[k1])
                ACT(sc[:], ct[:], AF.Silu, [k1], [k1])
                for jj in range(48):
                    i = jj % 2
                    DMA("sp", wst[i][:], w_ada[l][:, jj * 128:(jj + 1) * 128].rearrange("(k p) c -> p k c", p=128),
                        writes=[kw[i]])
                    b = jj % 4
                    for k in range(KC):
                        MM(pb[b][:, 0:2], wst[i][:, k, :], sc[:, k, :], k == 0, k == KC - 1, [kw[i], k1], [kb[b]])
                    TT("dve", modT[:, jj, :], pb[b][:, 0:2], bT[:, jj:jj + 1].to_broadcast([128, 2]), ALU.add,
                       [kb[b], k1], [k_mod])
                TS("dve", modT[:, 16:48, :], modT[:, 16:48, :], 1.0, None, ALU.add, None, [k_mod], [k_mod])
                if DBG:
                    DMA("sp", dbg_mod[l], modT[:], reads=[k_mod])
                P.barrier()

        def phase_UP(l):
            even = (l % 2 == 0)
            j = l // 2
            w_in = w_in_even[j] if even else w_in_odd[j]
            Xsrc = xT_in if l == 0 else X
            TH = min(T, 2048)
            with contextlib.ExitStack() as s2:
                U = sbt(s2, "U", [128, KC, T], BF16)
                kU = Tk()
                Us = sbt(s2, "Us", [128, KC, 1], BF16)
                with contextlib.ExitStack() as s3:
                    xst = [sbt(s3, "xst%d" % i, [128, 512], F32) for i in range(3)]
                    kx = [Tk() for _ in range(3)]
                    n = 0
                    for k in range(KC):
                        for t in range(NTT):
                            i = n % 3
                            n += 1
                            DMA("sp", xst[i][:], Xsrc[k * 128:(k + 1) * 128, t * 512:(t + 1) * 512], writes=[kx[i]])
                            ACT(U[:, k, t * 512:(t + 1) * 512], xst[i][:], AF.Identity, [kx[i], k_mod], [kU],
                                bias=modT[:, k, 0:1], scale=modT[:, 16 + k, 0:1])
                        ACT(Us[:, k, :], XS[:, k, :], AF.Identity, [k_xs, k_mod], [kU],
                            bias=modT[:, k, 1:2], scale=modT[:, 16 + k, 1:2])
                    P.barrier()
                import os
                KUP = os.environ.get("KUP", "i,ii,s,s2,nk,vb").split(",")
                if even:
                    chunks = [(c, 128) for c in range(0, 5120, 128)] + [(c, 128) for c in range(6144, 7168, 128)]
                else:
                    chunks = [(c, 128) for c in range(0, 12288, 128)] + [(12288, 64)]
                if "i" not in KUP:
                    chunks = []
                with contextlib.ExitStack() as s3:
                    wst = [sbt(s3, "pw%d" % i, [128, KC, 128], F32) for i in range(2)]
                    wb = [sbt(s3, "pwb%d" % i, [128, KC, 128], BF16) for i in range(2)]
                    hst = [sbt(s3, "ph%d" % i, [128, TH], F32) for i in range(2)]
                    kws = [Tk(), Tk()]
                    kwb = [Tk(), Tk()]
                    kh = [Tk(), Tk()]
                    hi = 0
                    br = 0
                    for ci, (c0, M) in enumerate(chunks):
                        i = ci % 2
                        DMA("sp", wst[i][:, :, 0:M], w_in[:, c0:c0 + M].rearrange("(k p) c -> p k c", p=128),
                            writes=[kws[i]])
                        CP("pool", wb[i][:, :, 0:M], wst[i][:, :, 0:M], [kws[i]], [kwb[i]])
                        for half in range(T // TH):
                            hb = hi % 2
                            hi += 1
                            for tt in range(TH // 512):
                                t = half * (TH // 512) + tt
                                b = br % 4
                                br += 1
                                for k in range(KC):
                                    MM(pb[b][0:M, :], wb[i][:, k, 0:M], U[:, k, t * 512:(t + 1) * 512],
                                       k == 0, k == KC - 1, [kwb[i], kU], [kb[b]])
                                EVAC(hst[hb][0:M, tt * 512:(tt + 1) * 512], pb[b][0:M, :], [kb[b]], [kh[hb]])
                            DMA("pool", HT[c0:c0 + M, half * TH:(half + 1) * TH], hst[hb][0:M, :], reads=[kh[hb]])
                        if "s" in KUP:
                            for k in range(KC):
                                MM(pb[4][0:M, 0:1], wb[i][:, k, 0:M], Us[:, k, :], k == 0, k == KC - 1,
                                   [kwb[i], kU], [kb[4]])
                            CP("act", HS[0:M, c0 // 128:c0 // 128 + 1], pb[4][0:M, 0:1], [kb[4]], [k_hs])
                    P.barrier()
                if even and "ii" in KUP:
                    with contextlib.ExitStack() as s3:
                        W2 = sbt(s3, "W2", [128, KC, 512], BF16)
                        kW2 = Tk()
                        st2 = [sbt(s3, "st2%d" % i, [128, 512], F32) for i in range(2)]
                        ks2 = [Tk(), Tk()]
                        ev = [sbt(s3, "ev%d" % i, [128, 512], F32) for i in range(2)]
                        kev = [Tk(), Tk()]
                        evb = [sbt(s3, "evb%d" % i, [128, 512], BF16) for i in range(2)]
                        kevb = [Tk(), Tk()]
                        evs = sbt(s3, "evs", [1, 512], F32)
                        kevs = Tk()
                        n = 0
                        for ct in range(4):
                            c0 = 4096 + ct * 512
                            isv = ct >= 2
                            dst_p = nv_p if isv else nk_p
                            dst_s = nv_s if isv else nk_s
                            oc = (ct % 2) * 512
                            for k in range(KC):
                                DMA("sp", st2[k % 2][:], w_in[k * 128:(k + 1) * 128, c0:c0 + 512],
                                    writes=[ks2[k % 2]])
                                CP("pool" if k % 2 else "dve", W2[:, k, :], st2[k % 2][:], [ks2[k % 2]], [kW2])
                            for tb in range(NB):
                                b = n % 4
                                e_i = n % 2
                                n += 1
                                for k in range(KC):
                                    MM(pb[b][:, :], U[:, k, tb * 128:(tb + 1) * 128], W2[:, k, :],
                                       k == 0, k == KC - 1, [kU, kW2], [kb[b]])
                                CP("act", ev[e_i][:], pb[b][:, :], [kb[b]], [kev[e_i]])
                                if "nk" in KUP:
                                    DMA("pool", dst_p[j, tb * 128:(tb + 1) * 128, oc:oc + 512], ev[e_i][:],
                                        reads=[kev[e_i]])
                                if isv and "vb" in KUP:
                                    CP("dve", evb[e_i][:], ev[e_i][:], [kev[e_i]], [kevb[e_i]])
                                    DMA("sp", Vb[tb * 128:(tb + 1) * 128, oc:oc + 512], evb[e_i][:],
                                        reads=[kevb[e_i]])
                            if "s2" in KUP:
                                for k in range(KC):
                                    MM(pb[4][0:1, :], Us[:, k, 0:1], W2[:, k, :], k == 0, k == KC - 1,
                                       [kU, kW2], [kb[4]])
                                CP("act", evs[:], pb[4][0:1, :], [kb[4]], [kevs])
                                DMA("pool", dst_s[j, 0:1, oc:oc + 512], evs[:], reads=[kevs])
                        P.barrier()

        def phase_A(l):
            j = l // 2
            with contextlib.ExitStack() as s2:
                G = sbt(s2, "G", [128, 32 + T], BF16)
                kG = Tk()
                Gs = sbt(s2, "Gs", [128, 32], BF16)
                hs32 = sbt(s2, "hs32", [128, 32], F32)
                khs = Tk()
                cw = sbt(s2, "cw", [128, 8, CONVW_A], F32)
                gg = sbt(s2, "gg", [128, 8], F32)
                gb = sbt(s2, "gb", [128, 8], F32)
                kcw = Tk()
                Dg = sbt(s2, "Dg", [128, CONVW_A, 128], BF16)
                kDg = Tk()
                va = [sbt(s2, "va%d" % i, [128, 512], F32) for i in range(2)]
                gl = [sbt(s2, "gl%d" % i, [128, 512], F32) for i in range(2)]
                gt = [sbt(s2, "gt%d" % i, [128, 512], F32) for i in range(2)]
                kin = [Tk(), Tk()]
                kgt = [Tk(), Tk()]
                g32 = sbt(s2, "g32", [128, 512], F32)
                kg32 = Tk()
                yf = sbt(s2, "yf", [128, 512], F32)
                ybf = sbt(s2, "ybf", [128, 512], BF16)
                ysq = sbt(s2, "ysq", [128, 512], BF16)
                t1 = sbt(s2, "t1", [128, 512], F32)
                t2 = sbt(s2, "t2", [128, 512], F32)
                yo = [sbt(s2, "yo%d" % i, [128, 512], BF16) for i in range(2)]
                kyo = [Tk(), Tk()]
                kt = Tk()
                DMA("sp", cw[:], conv_w_aT[j], writes=[kcw])
                DMA("sp", gg[:], gn_gT[j], writes=[kcw])
                DMA("sp", gb[:], gn_bT[j], writes=[kcw])
                n = 0

                def post(N, ypsum, kyp, gate_ap, kgate, cc, out_bf, kout):
                    CP("act", yf[:, 0:N], ypsum, [kyp], [kt])
                    CP("dve", ybf[:, 0:N], yf[:, 0:N], [kt], [kt])
                    ACT(ysq[:, 0:N], yf[:, 0:N], AF.Square, [kt], [kt])
                    MM(pb[5][:, 0:N], onesG[:], ybf[:, 0:N], True, True, [kt, k_c], [kb[5]])
                    MM(pb[6][:, 0:N], onesG[:], ysq[:, 0:N], True, True, [kt, k_c], [kb[6]])
                    CP("act", t2[:, 0:N], pb[5][:, 0:N], [kb[5]], [kt])
                    TT("dve", t1[:, 0:N], yf[:, 0:N], t2[:, 0:N], ALU.subtract, [kt], [kt])
                    ACT(t2[:, 0:N], t2[:, 0:N], AF.Square, [kt], [kt])
                    TT("dve", t2[:, 0:N], pb[6][:, 0:N], t2[:, 0:N], ALU.subtract, [kt, kb[6]], [kt])
                    ACT(t2[:, 0:N], t2[:, 0:N], AF.Sqrt, [kt], [kt], bias=epsc[:, 0:1])
                    P.op("dve", lambda e: e.reciprocal(out=t2[:, 0:N], in_=t2[:, 0:N]), [kt], [kt])
                    TT("dve", t1[:, 0:N], t1[:, 0:N], t2[:, 0:N], ALU.mult, [kt], [kt])
                    ACT(t1[:, 0:N], t1[:, 0:N], AF.Silu, [kt, kcw], [kt], bias=gb[:, cc:cc + 1],
                        scale=gg[:, cc:cc + 1])
                    ACT(t2[:, 0:N], gate_ap, AF.Silu, [kgate], [kt])
                    TT("dve", out_bf, t1[:, 0:N], t2[:, 0:N], ALU.mult, [kt], [kout])

                for cc in range(8):
                    r0 = cc * 128
                    for kk in range(CONVW_A):
                        TS("dve" if kk % 2 else "pool", Dg[:, kk, :], identf, cw[:, cc, kk:kk + 1], None, ALU.mult,
                           None, [k_c, kcw], [kDg])
                    MEMSET("pool", G[:, 0:32], 0.0, [kG])
                    for t in range(NTT):
                        i = n % 2
                        n += 1
                        DMA("sp", va[i][:], HT[r0:r0 + 128, t * 512:(t + 1) * 512], writes=[kin[i]])
                        DMA("sp", gl[i][:], HT[1024 + r0:1024 + r0 + 128, t * 512:(t + 1) * 512], writes=[kin[i]])
                        ACT(gl[i][:], gl[i][:], AF.Sigmoid, [kin[i]], [kin[i]])
                        TT("dve", g32[:], va[i][:], gl[i][:], ALU.mult, [kin[i]], [kg32])
                        CP("pool", G[:, 32 + t * 512:32 + (t + 1) * 512], g32[:], [kg32], [kG])
                        if t == NTT - 1:
                            DMA("pool", ca_pT[j, r0:r0 + 128, :], g32[:, 482:512], reads=[kg32])
                    for t in range(NTT):
                        i = n % 2
                        n += 1
                        DMA("sp", gt[i][:], HT[2048 + r0:2048 + r0 + 128, t * 512:(t + 1) * 512], writes=[kgt[i]])
                        b = t % 2
                        for kk in range(CONVW_A):
                            MM(pb[b][:, :], Dg[:, kk, :], G[:, 2 + t * 512 + kk:2 + t * 512 + kk + 512],
                               kk == 0, kk == CONVW_A - 1, [kDg, kG], [kb[b]])
                        post(512, pb[b][:, :], kb[b], gt[i][:], kgt[i], cc, yo[i][:], kyo[i])
                        DMA("pool", YT[r0:r0 + 128, t * 512:(t + 1) * 512], yo[i][:], reads=[kyo[i]])
                    DMA("sp", hs32[:, 0:30], st_conv_aT[j, r0:r0 + 128, :], writes=[khs])
                    ACT(t2[:, 0:1], HS[:, 8 + cc:9 + cc], AF.Sigmoid, [k_hs], [kt])
                    TT("dve", hs32[:, 30:31], HS[:, cc:cc + 1], t2[:, 0:1], ALU.mult, [k_hs, kt], [khs])
                    CP("dve", Gs[:, 0:31], hs32[:, 0:31], [khs], [khs])
                    DMA("pool", ca_sT[j, r0:r0 + 128, :], hs32[:, 1:31], reads=[khs])
                    for kk in range(CONVW_A):
                        MM(pb[2][:, 0:1], Dg[:, kk, :], Gs[:, kk:kk + 1], kk == 0, kk == CONVW_A - 1,
                           [kDg, khs], [kb[2]])
                    post(1, pb[2][:, 0:1], kb[2], HS[:, 16 + cc:17 + cc], k_hs, cc, YS[:, cc, :], k_ys)
                P.barrier()

        def phase_B(l):
            j = l // 2
            with contextlib.ExitStack() as s2:
                sbb = sbt(s2, "sbb", [128, SBH], F32)
                ksbb = Tk()
                DMA("sp", sbb[:], sb_biasB[j], writes=[ksbb])
                QT = sbt(s2, "QT", [128, T], BF16)
                KT = sbt(s2, "KT", [128, T], BF16)
                Vh = sbt(s2, "Vh", [128, NB, 128], BF16)
                kq = Tk()
                stg = [sbt(s2, "stg%d" % i, [128, 512], F32) for i in range(2)]
                kstg = [Tk(), Tk()]
                bg = [sbt(s2, "bg%d" % i, [128, 512], F32) for i in range(2)]
                kbg = [Tk(), Tk()]
                E = [sbt(s2, "E%d" % i, [128, 512], F32) for i in range(2)]
                SP32 = [sbt(s2, "SP%d" % i, [128, 512], F32) for i in range(2)]
                SPb = [sbt(s2, "SPb%d" % i, [128, 512], BF16) for i in range(2)]
                Wt = [sbt(s2, "Wt%d" % i, [128, 512], BF16) for i in range(2)]
                kE = [Tk(), Tk()]
                kSP = [Tk(), Tk()]
                kSPb = [Tk(), Tk()]
                kWt = [Tk(), Tk()]
                CAR = sbt(s2, "CAR", [128, 512], F32)
                CARb = [sbt(s2, "CARb%d" % i, [128, 512], BF16) for i in range(2)]
                kCAR = Tk()
                kCARb = [Tk(), Tk()]
                yo = [sbt(s2, "byo%d" % i, [128, 512], BF16) for i in range(2)]
                kyo = [Tk(), Tk()]
                n = 0
                nblk = 0
                for h in range(SBH):
                    for t in range(NTT):
                        for which in range(2):
                            i = n % 2
                            n += 1
                            row = (3072 if which == 0 else 4096) + h * 128
                            DMA("sp", stg[i][:], HT[row:row + 128, t * 512:(t + 1) * 512], writes=[kstg[i]])
                            if which == 0:
                                ACT(QT[:, t * 512:(t + 1) * 512], stg[i][:], AF.Copy, [kstg[i]], [kq],
                                    scale=float(HD ** -0.5))
                            else:
                                CP("dve", KT[:, t * 512:(t + 1) * 512], stg[i][:], [kstg[i]], [kq])
                    DMA("sp", Vh[:], Vb[:, h * 128:(h + 1) * 128].rearrange("(nb p) d -> p nb d", p=128),
                        writes=[kq])
                    for qt in range(NTT):
                        q0 = qt * 512
                        gi = qt % 2
                        DMA("sp", bg[gi][:], HT[6144 + h * 128:6144 + (h + 1) * 128, q0:q0 + 512], writes=[kbg[gi]])
                        ACT(bg[gi][:], bg[gi][:], AF.Silu, [kbg[gi]], [kbg[gi]])
                        ob = 6 + (qt % 2)
                        blocks = list(range(q0 // 128 + 3, -1, -1))
                        for bi, kbk in enumerate(blocks):
                            r = kbk - q0 // 128
                            zi = nblk % 2
                            nblk += 1
                            zb = zi
                            diag = r >= 0
                            MM(pb[zb][:, :], KT[:, kbk * 128:(kbk + 1) * 128], QT[:, q0:q0 + 512], True, False,
                               [kq], [kb[zb]])
                            if diag:
                                MM(pb[zb][:, :], identb[:], mkb[:, 384 - r * 128:384 - r * 128 + 512], False, False,
                                   [k_c], [kb[zb]])
                            ACT(E[zi][:], pb[zb][:, :], AF.Exp, [kb[zb], ksbb], [kE[zi]], bias=sbb[:, h:h + 1])
                            ACT(SP32[zi][:], E[zi][:], AF.Ln, [kE[zi]], [kSP[zi]], bias=1.0)
                            CP("dve", SPb[zi][:], SP32[zi][:], [kSP[zi]], [kSPb[zi]])
                            MM(pb[zb][:, :], nuincl[:], SPb[zi][:], False, bi == 0, [kSPb[zi], k_c], [kb[zb]])
                            if bi > 0:
                                MM(pb[zb][:, :], nones[:], CARb[(bi - 1) % 2][:], False, True,
                                   [kCARb[(bi - 1) % 2], k_c], [kb[zb]])
                            ACT(Wt[zi][:], pb[zb][:, :], AF.Exp, [kb[zb], ksbb], [kWt[zi]], bias=sbb[:, h:h + 1])
                            MM(pb[ob][:, :], Vh[:, kbk, :], Wt[zi][:], bi == 0, bi == len(blocks) - 1,
                               [kq, kWt[zi]], [kb[ob]])
                            if bi < len(blocks) - 1:
                                if bi == 0:
                                    CP("pool", CAR[:], SP32[zi][:], [kSP[zi]], [kCAR])
                                else:
                                    TT("pool", CAR[:], CAR[:], SP32[zi][:], ALU.add, [kSP[zi], kCAR], [kCAR])
                                CP("pool", CARb[bi % 2][:], CAR[:], [kCAR], [kCARb[bi % 2]])
                        TT("dve", yo[gi][:], pb[ob][:, :], bg[gi][:], ALU.mult, [kb[ob], kbg[gi]], [kyo[gi]])
                        DMA("pool", YT[1024 + h * 128:1024 + (h + 1) * 128, q0:q0 + 512], yo[gi][:], reads=[kyo[gi]])
                P.barrier()

        def phase_Bs(l):
            j = l // 2
            NC8 = NPG * SBH
            with contextlib.ExitStack() as s2:
                sbb = sbt(s2, "ssbb", [128, SBH], F32)
                ptf = sbt(s2, "ptf", [128, NPG], F32)
                pti = sbt(s2, "pti", [128, NPG], I32)
                idx = sbt(s2, "idx", [128, NPG], I32)
                k0 = Tk()
                DMA("sp", sbb[:], sb_biasB[j], writes=[k0])
                DMA("sp", pti[:], ptB[:, :], writes=[k0])
                CP("dve", ptf[:], pti[:], [k0], [k0])
                TS("dve", ptf[:], ptf[:], 128.0, cpk[:, 1664:1665], ALU.mult, ALU.add, [k0, k_c], [k0])
                if j > 0:
                    TS("dve", ptf[:], ptf[:], float(j * NPOOL * 128), None, ALU.add, None, [k0], [k0])
                CP("dve", idx[:], ptf[:], [k0], [k0])
                qcol = sbt(s2, "qcol", [128, SBH], F32)
                TS("dve", qcol[:], HS[:, 24:32], float(HD ** -0.5), None, ALU.mult, None, [k_hs], [k0])
                qb = sbt(s2, "qb", [128, SBH, 128], F32)
                dq = sbt(s2, "dq", [128, 128], F32)
                for h in range(SBH):
                    TS("dve", dq[:], identf, qcol[:, h:h + 1], None, ALU.mult, None, [k0, k_c], [k0])
                    MM(pb[0][:, 0:128], cpk[:, 1792:1920], dq[:], True, True, [k0, k_c], [kb[0]])
                    CP("act", qb[:, h, :], pb[0][:, 0:128], [kb[0]], [k0])
                Z = sbt(s2, "Z", [128, NPG, SBH], F32)
                kZ = Tk()
                kpg = [sbt(s2, "kpg%d" % i, [128, 1024], F32) for i in range(2)]
                kkpg = [Tk(), Tk()]
                junk = sbt(s2, "junk", [128, 128], F32)
                kj = Tk()
                cflat_k = cache_k.rearrange("e r c -> (e r) c")
                cflat_v = cache_v.rearrange("e r c -> (e r) c")
                for pg in range(NPG):
                    i = pg % 2
                    P.dma("pool", lambda e, i=i, pg=pg: e.indirect_dma_start(
                        out=kpg[i][:], out_offset=None, in_=cflat_k,
                        in_offset=bass.IndirectOffsetOnAxis(ap=idx[:, pg:pg + 1], axis=0)),
                        reads=[k0], writes=[kkpg[i]])
                    for h in range(SBH):
                        STT("dve", junk[:], kpg[i][:, h * 128:(h + 1) * 128], 1.0, qb[:, h, :], ALU.mult, ALU.mult,
                            [kkpg[i], k0], [kj])
                        P.op("dve", lambda e, pg=pg, h=h: e.tensor_reduce(
                            out=Z[:, pg, h:h + 1], in_=junk[:], axis=mybir.AxisListType.X, op=ALU.add),
                            [kj], [kZ])
                Zf = Z[:].rearrange("p g h -> p (g h)")
                for pg in range(NPG):
                    TT("dve", Z[:, pg, :], Z[:, pg, :], sbb[:], ALU.add, [kZ, k0], [kZ])
                Ee = sbt(s2, "Ee", [128, NC8], F32)
                SPs = sbt(s2, "SPs", [128, NC8], F32)
                TOT = sbt(s2, "TOT", [128, NC8], F32)
                TO2 = sbt(s2, "TO2", [128, NC8], F32)
                ACT(Ee[:], Zf, AF.Exp, [kZ], [kZ])
                ACT(SPs[:], Ee[:], AF.Ln, [kZ], [kZ], bias=1.0)
                ARG = sbt(s2, "ARG", [128, NC8], F32)
                for c0 in range(0, NC8, 512):
                    c1 = min(NC8, c0 + 512)
                    MM(pb[1][:, 0:c1 - c0], cpk[:, 1920:2048], SPs[:, c0:c1], True, True, [kZ, k_c], [kb[1]])
                    TT("dve", ARG[:, c0:c1], Zf[:, c0:c1], pb[1][:, 0:c1 - c0], ALU.subtract, [kZ, kb[1]], [kZ])
                    MM(pb[2][:, 0:c1 - c0], cpk[:, 1792:1920], SPs[:, c0:c1], True, True, [kZ, k_c], [kb[2]])
                    CP("act", TOT[:, c0:c1], pb[2][:, 0:c1 - c0], [kb[2]], [kZ])
                T3 = TOT[:].rearrange("p (g h) -> p g h", h=SBH)
                T4 = TO2[:].rearrange("p (g h) -> p g h", h=SBH)
                src, dst = T3, T4
                sh = 1
                while sh < NPG:
                    CP("dve", dst[:, NPG - sh:NPG, :], src[:, NPG - sh:NPG, :], [kZ], [kZ])
                    TT("dve", dst[:, 0:NPG - sh, :], src[:, 0:NPG - sh, :], src[:, sh:NPG, :], ALU.add, [kZ], [kZ])
                    src, dst = dst, src
                    sh *= 2
                A3 = ARG[:].rearrange("p (g h) -> p g h", h=SBH)
                if NPG > 1:
                    TT("dve", A3[:, 0:NPG - 1, :], A3[:, 0:NPG - 1, :], src[:, 1:NPG, :], ALU.subtract, [kZ], [kZ])
                Wg = sbt(s2, "Wg", [128, NPG, SBH], F32)
                ACT(Wg[:].rearrange("p g h -> p (g h)"), ARG[:], AF.Exp, [kZ], [kZ])
                vpg = [sbt(s2, "vpg%d" % i, [128, 1024], F32) for i in range(2)]
                kvpg = [Tk(), Tk()]
                for pg in range(NPG):
                    i = pg % 2
                    P.dma("pool", lambda e, i=i, pg=pg: e.indirect_dma_start(
                        out=vpg[i][:], out_offset=None, in_=cflat_v,
                        in_offset=bass.IndirectOffsetOnAxis(ap=idx[:, pg:pg + 1], axis=0)),
                        reads=[k0], writes=[kvpg[i]])
                    if pg == 0:
                        MM(pb[3][:, 0:SBH], zerob[:], zerob[:, 0:SBH], True, False, [k_c], [kb[3]])
                    for h in range(SBH):
                        MM(pb[3][:, h:h + 1], vpg[i][:, h * 128:(h + 1) * 128], Wg[:, pg, h:h + 1],
                           False, pg == NPG - 1 and h == SBH - 1, [kvpg[i], kZ], [kb[3]])
                gsl = sbt(s2, "gsl", [128, SBH], F32)
                ACT(gsl[:], HS[:, 48:56], AF.Silu, [k_hs], [k0])
                TT("dve", YS[:, 8:16, :].rearrange("p h o -> p (h o)"), pb[3][:, 0:SBH], gsl[:], ALU.mult,
                   [kb[3], k0], [k_ys])
                P.barrier()

        def phase_C(l):
            j = l // 2
            with contextlib.ExitStack() as s2:
                cw = sbt(s2, "ccw", [128, 64, 4], F32)
                kcw = Tk()
                DMA("sp", cw[:], conv_w_cT[j], writes=[kcw])
                PRE = [sbt(s2, "PRE%d" % i, [128, 4 + T], F32) for i in range(2)]
                kpre = [Tk(), Tk()]
                acc = sbt(s2, "cacc", [128, T], F32)
                kacc = Tk()
                sq = sbt(s2, "csq", [128, T], BF16)
                ksq = Tk()
                rs = [sbt(s2, "crs%d" % i, [128, 512], F32) for i in range(2)]
                krs = [Tk(), Tk()]
                ob = [sbt(s2, "cob%d" % i, [128, T], BF16) for i in range(2)]
                kob = [Tk(), Tk()]
                for i in range(2):
                    MEMSET("pool", PRE[i][:, 0:4], 0.0, [kpre[i]])
                nr = 0
                for cc in range(64):
                    i = cc % 2
                    r0 = cc * 128
                    DMA("sp", PRE[i][:, 4:4 + T], HT[r0:r0 + 128, :], writes=[kpre[i]])
                    DMA("pool", cc_pT[j, r0:r0 + 128, :], PRE[i][:, 1 + T:4 + T], reads=[kpre[i]])
                    TS("dve", acc[:], PRE[i][:, 1:1 + T], cw[:, cc, 0:1], None, ALU.mult, None, [kpre[i], kcw], [kacc])
                    for kk in range(1, 4):
                        STT("dve", acc[:], PRE[i][:, 1 + kk:1 + kk + T], cw[:, cc, kk:kk + 1], acc[:], ALU.mult,
                            ALU.add, [kpre[i], kcw, kacc], [kacc])
                    ACT(acc[:], acc[:], AF.Silu, [kacc], [kacc])
                    if cc < 32:
                        ACT(sq[:], acc[:], AF.Square, [kacc], [ksq])
                        for t in range(NTT):
                            b = nr % 4
                            q = nr % 2
                            nr += 1
                            MM(pb[b][:, :], ones1[:], sq[:, t * 512:(t + 1) * 512], True, True, [ksq, k_c], [kb[b]])
                            ACT(rs[q][:], pb[b][:, :], AF.Sqrt, [kb[b]], [krs[q]], bias=epsc[:, 1:2])
                            P.op("dve", lambda e, q=q: e.reciprocal(out=rs[q][:], in_=rs[q][:]), [krs[q]], [krs[q]])
                            if cc < 16:
                                STT("dve", ob[i][:, t * 512:(t + 1) * 512], acc[:, t * 512:(t + 1) * 512],
                                    float(HD ** -0.5), rs[q][:], ALU.mult, ALU.mult, [kacc, krs[q]], [kob[i]])
                            else:
                                TT("dve", ob[i][:, t * 512:(t + 1) * 512], acc[:, t * 512:(t + 1) * 512], rs[q][:],
                                   ALU.mult, [kacc, krs[q]], [kob[i]])
                    else:
                        CP("pool", ob[i][:], acc[:], [kacc], [kob[i]])
                    DMA("pool", QKVn[r0:r0 + 128, :], ob[i][:], reads=[kob[i]])
                FS = sbt(s2, "FS", [128, 64, 4], F32)
                kfs = Tk()
                DMA("sp", FS[:, :, 0:3], st_conv_cT[j].rearrange("(c p) k -> p c k", p=128), writes=[kfs])
                CP("dve", FS[:, :, 3], HS[:, 0:64], [k_hs], [kfs])
                DMA("pool", cc_sT[j].rearrange("(c p) k -> p c k", p=128), FS[:, :, 1:4], reads=[kfs])
                pr = sbt(s2, "cpr", [128, 64, 4], F32)
                TT("dve", pr[:], FS[:], cw[:], ALU.mult, [kfs, kcw], [kfs])
                P.op("dve", lambda e: e.tensor_reduce(out=SN[:, 0:64], in_=pr[:], axis=mybir.AxisListType.X,
                                                      op=ALU.add), [kfs], [k_sn])
                ACT(SN[:, 0:64], SN[:, 0:64], AF.Silu, [k_sn], [k_sn])
                sqs = sbt(s2, "csqs", [128, 32], F32)
                ACT(sqs[:], SN[:, 0:32], AF.Square, [k_sn], [kfs])
                MM(pb[5][:, 0:32], cpk[:, 1792:1920], sqs[:], True, True, [kfs, k_c], [kb[5]])
                ACT(sqs[:], pb[5][:, 0:32], AF.Sqrt, [kb[5]], [kfs], bias=epsc[:, 1:2])
                P.op("dve", lambda e: e.reciprocal(out=sqs[:], in_=sqs[:]), [kfs], [kfs])
                TT("dve", SN[:, 0:32], SN[:, 0:32], sqs[:], ALU.mult, [kfs, k_sn], [k_sn])
                TS("dve", SN[:, 0:16], SN[:, 0:16], float(HD ** -0.5), None, ALU.mult, None, [k_sn], [k_sn])
                P.barrier()

        def phase_G(l):
            j = l // 2
            NCH = T // 128
            NCS = NCH + 1
            with contextlib.ExitStack() as s2:
                BETA = sbt(s2, "BETA", [128, NCS, GV], F32)
                GC = sbt(s2, "GC", [128, NCS, GV], F32)
                EGC = sbt(s2, "EGC", [128, NCS, GV], F32)
                BE = sbt(s2, "BE", [128, NCS, GV], F32)
                EKD = sbt(s2, "EKD", [128, NCS, GV], F32)
                EGL = sbt(s2, "EGL", [128, NCS, GV], F32)
                GCT = sbt(s2, "GCT", [32, NCS, 128], F32)
                nGCT = sbt(s2, "nGCT", [32, NCS, 128], F32)
                SEL = sbt(s2, "SEL", [32, GV, 128], F32)
                gw = sbt(s2, "gw", [128, 1], F32)
                ktab = Tk()
                DMA("sp", SEL[:], selpack[:, :, :], writes=[ktab])
                DMA("sp", gw[:], gnorm_wT[j], writes=[ktab])
                with contextlib.ExitStack() as s3:
                    BA = sbt(s3, "BA", [64, T + 128], F32)
                    kba = Tk()
                    nea = sbt(s3, "nea", [128, GV], F32)
                    dtb = sbt(s3, "dtb", [128, GV], F32)
                    kq = Tk()
                    DMA("sp", BA[:, 0:T], HT[12288:12352, :], writes=[kba])
                    MEMSET("pool", BA[:, T:T + 128], 0.0, [kba])
                    CP("dve", BA[:, T:T + 1], HS[0:64, 96:97], [k_hs], [kba])
                    DMA("sp", nea[:], a_logB[j], writes=[kq])
                    DMA("sp", dtb[:], dt_biasB[j], writes=[kq])
                    ACT(nea[:], nea[:], AF.Exp, [kq], [kq])
                    TS("dve", nea[:], nea[:], -1.0, None, ALU.mult, None, [kq], [kq])
                    tmp = [sbt(s3, "g0t%d" % i, [128, GV], F32) for i in range(2)]
                    g32 = [sbt(s3, "g0g%d" % i, [128, GV], F32) for i in range(2)]
                    gl = [sbt(s3, "g0l%d" % i, [128, GV], F32) for i in range(2)]
                    kt = [Tk(), Tk()]
                    for n in range(NCS):
                        q = n % 2
                        MM(pb[0 + q][:, 0:64], BA[:, n * 128:(n + 1) * 128], identf[0:64, 0:64], True, True,
                           [kba, k_c], [kb[0 + q]])
                        ACT(BETA[:, n, :], pb[0 + q][:, 0:32], AF.Sigmoid, [kb[0 + q]], [ktab])
                        CP("act", tmp[q][:], pb[0 + q][:, 32:64], [kb[0 + q]], [kt[q]])
                        TT("dve", tmp[q][:], tmp[q][:], dtb[:], ALU.add, [kt[q], kq], [kt[q]])
                        ACT(tmp[q][:], tmp[q][:], AF.Exp, [kt[q]], [kt[q]])
                        ACT(tmp[q][:], tmp[q][:], AF.Ln, [kt[q]], [kt[q]], bias=1.0)
                        TT("dve", g32[q][:], tmp[q][:], nea[:], ALU.mult, [kt[q], kq], [kt[q]])
                        if n == NCH:
                            TS("dve", g32[q][:], g32[q][:], identf[:, 0:1], None, ALU.mult, None, [kt[q], k_c],
                               [kt[q]])
                        MM(pb[2 + q][:, 0:32], trif, g32[q][:], True, True, [kt[q], k_c], [kb[2 + q]])
                        CP("act", GC[:, n, :], pb[2 + q][:, 0:32], [kb[2 + q]], [ktab])
                        MM(pb[4 + q][0:32, 0:128], g32[q][:], trif, True, True, [kt[q], k_c], [kb[4 + q]])
                        CP("act", GCT[:, n, :], pb[4 + q][0:32, 0:128], [kb[4 + q]], [ktab])
                        TS("dve", nGCT[:, n, :], GCT[:, n, :], -1.0, None, ALU.mult, None, [ktab], [ktab])
                        MM(pb[6 + q][:, 0:32], sellast, GC[:, n, :], True, True, [ktab, k_c], [kb[6 + q]])
                        ACT(EGL[:, n, :], pb[6 + q][:, 0:32], AF.Exp, [kb[6 + q]], [ktab])
                        CP("act", gl[q][:], pb[6 + q][:, 0:32], [kb[6 + q]], [kt[q]])
                        TT("dve", gl[q][:], gl[q][:], GC[:, n, :], ALU.subtract, [kt[q], ktab], [kt[q]])
                        ACT(EKD[:, n, :], gl[q][:], AF.Exp, [kt[q]], [ktab])
                        ACT(EGC[:, n, :], GC[:, n, :], AF.Exp, [ktab], [ktab])
                        TT("dve", BE[:, n, :], BETA[:, n, :], EGC[:, n, :], ALU.mult, [ktab], [ktab])
                    P.barrier()
                qT = sbt(s2, "gqT", [128, T], BF16)
                kT = sbt(s2, "gkT", [128, T], BF16)
                vT = sbt(s2, "gvT", [128, 2, T], BF16)
                yTh = sbt(s2, "gyT", [128, 2, T], BF16)
                zB = [sbt(s2, "gzB%d" % i, [128, 2, 256], F32) for i in range(2)]
                kzB = [Tk(), Tk()]
                kin = Tk()
                kyT = Tk()
                sqT = sbt(s2, "sqT", [128, 128], BF16)
                skT = sbt(s2, "skT", [128, 128], BF16)
                svT = sbt(s2, "svT", [128, 2, 128], BF16)
                szT = sbt(s2, "szT", [128, 2, 128], F32)
                syT = sbt(s2, "syT", [128, 2, 128], BF16)
                ksin = Tk()
                ksy = Tk()
                S32 = sbt(s2, "S32", [128, 2, 128], F32)
                Sb = sbt(s2, "Sb", [128, 2, 128], BF16)
                kS = Tk()
                kSb = Tk()

                class TL:
                    def __init__(self, name, shape, dt, n=1):
                        self.t = [sbt(s2, "%s%d" % (name, i), list(shape), dt) for i in range(n)]
                        self.k = [Tk() for _ in range(n)]

                def fl(t, w):
                    return t[:].rearrange("p m d -> p (m d)")[:, 0:w]

                msk4 = sbt(s2, "msk4", [128, 7, 512], BF16)
                id4 = sbt(s2, "id4", [128, 512], BF16)
                up4 = sbt(s2, "up4", [128, 512], BF16)
                low2 = sbt(s2, "low2", [128, 256], F32)
                with contextlib.ExitStack() as s3:
                    mskf = sbt(s3, "mskf", [128, 7 * 128], F32)
                    DMA("sp", mskf[:], cpack2[:, :], writes=[ktab])
                    for r in range(4):
                        CP("dve", msk4[:, :, r * 128:(r + 1) * 128], mskf[:].rearrange("p (a b) -> p a b", b=128),
                           [ktab], [ktab])
                        CP("dve", id4[:, r * 128:(r + 1) * 128], identb[:], [k_c], [ktab])
                        CP("dve", up4[:, r * 128:(r + 1) * 128], upbig[:], [k_c], [ktab])
                    for r in range(2):
                        CP("dve", low2[:, r * 128:(r + 1) * 128], lowstrict[:], [k_c], [ktab])
                    P.barrier()
                ktok2 = TL("ktok2", [128, 2, 128], F32)
                KKs2 = TL("KKs2", [128, 2, 128], F32)
                QKs2 = TL("QKs2", [128, 2, 128], F32)
                bv4 = TL("bv4", [128, 4, 128], BF16, 2)
                kbg4 = TL("kbg4", [128, 4, 128], BF16)
                kdec4 = TL("kdec4", [128, 4, 128], BF16, 2)
                dec4 = TL("dec4", [128, 4, 128], F32)
                L4 = TL("L4", [128, 4, 128], BF16)
                A4 = TL("A4", [128, 4, 128], BF16)
                Mf4 = TL("Mf4", [128, 4, 128], BF16)
                AT4 = TL("AT4", [128, 4, 128], BF16, 2)
                nwT4 = TL("nwT4", [128, 4, 128], BF16, 2)
                Lp = TL("Lp", [128, 4, 128], BF16, 2)
                Mp = TL("Mp", [128, 4, 128], BF16, 2)
                Tn = TL("Tn", [128, 4, 128], BF16, 2)
                Tt = TL("Tt", [128, 4, 128], BF16, 4)
                Cs = [TL("Cs%d" % i, [128, 4, 128], BF16) for i in range(3)]
                Cts = [TL("Cts%d" % i, [128, 4, 128], BF16) for i in range(3)]
                IL = TL("IL", [128, 4, 128], BF16)
                IM = TL("IM", [128, 4, 128], BF16)
                Xs = TL("Xs", [128, 4, 128], BF16)
                X2s = TL("X2s", [128, 4, 128], BF16)
                vnb2 = TL("vnb2", [128, 2, 128], BF16)
                qS2 = TL("qS2", [128, 2, 128], F32)
                o2 = TL("o2", [128, 2, 128], F32)
                junk2 = TL("junk2", [128, 2, 128], F32)
                onb2 = TL("onb2", [128, 2, 128], BF16)
                ss2 = TL("ss2", [128, 2], F32)
                zg2 = TL("zg2", [128, 2, 128], F32)
                bank = [0]

                def nb():
                    bank[0] = (bank[0] + 1) % 8
                    return bank[0]

                def bc_d(tab, n, hv0):
                    return tab[:, n, hv0:hv0 + 2].unsqueeze(2).to_broadcast([128, 2, 128])

                def bc_v(t3, ci):
                    return t3[:, ci:ci + 1, :].to_broadcast([128, 2, 128])

                def mm4(dst_bank, nm, lhs_fn, rhs_fn, reads):
                    for m in range(nm):
                        MM(pb[dst_bank][:, m * 128:(m + 1) * 128], lhs_fn(m), rhs_fn(m), True, True, reads,
                           [kb[dst_bank]])

                ttc = [0]

                def stage1(bp, hq, cbs, kI):
                    hv0 = 2 * hq
                    nc_ = len(cbs)
                    nm = 2 * nc_
                    W = nm * 128
                    Wk = nc_ * 128
                    b = nb()
                    for ci, (n, qc, kc, vc) in enumerate(cbs):
                        MM(pb[b][:, ci * 128:(ci + 1) * 128], kc, identb[:], True, True, [kI, k_c], [kb[b]])
                    CP("act", fl(ktok2.t[0], Wk), pb[b][:, 0:Wk], [kb[b]], [ktok2.k[0]])
                    b = nb()
                    for ci, (n, qc, kc, vc) in enumerate(cbs):
                        MM(pb[b][:, ci * 128:(ci + 1) * 128], kc, kc, True, True, [kI], [kb[b]])
                    TT("dve", fl(KKs2.t[0], Wk), pb[b][:, 0:Wk], low2[:, 0:Wk], ALU.mult, [kb[b], ktab], [KKs2.k[0]])
                    b = nb()
                    for ci, (n, qc, kc, vc) in enumerate(cbs):
                        MM(pb[b][:, ci * 128:(ci + 1) * 128], qc, kc, True, True, [kI], [kb[b]])
                    CP("act", fl(QKs2.t[0], Wk), pb[b][:, 0:Wk], [kb[b]], [QKs2.k[0]])
                    b = nb()
                    for ci, (n, qc, kc, vc) in enumerate(cbs):
                        for vh in range(2):
                            m = 2 * ci + vh
                            MM(pb[b][:, m * 128:(m + 1) * 128], vc(vh), identb[:], True, True, [kI, k_c], [kb[b]])
                    for ci, (n, qc, kc, vc) in enumerate(cbs):
                        pv = pb[b][:, 2 * ci * 128:(2 * ci + 2) * 128].rearrange("p (v d) -> p v d", d=128)
                        TT("dve", bv4.t[bp][:, 2 * ci:2 * ci + 2, :], pv, bc_d(BETA, n, hv0), ALU.mult,
                           [kb[b], ktab], [bv4.k[bp]])
                        TT("pool", kbg4.t[0][:, 2 * ci:2 * ci + 2, :], bc_v(ktok2.t[0], ci), bc_d(BE, n, hv0),
                           ALU.mult, [ktok2.k[0], ktab], [kbg4.k[0]])
                        TT("pool", kdec4.t[bp][:, 2 * ci:2 * ci + 2, :], bc_v(ktok2.t[0], ci), bc_d(EKD, n, hv0),
                           ALU.mult, [ktok2.k[0], ktab], [kdec4.k[bp]])
                    b = nb()
                    MM(pb[b][:, 0:W], identb[:], up4[:, 0:W], True, False, [k_c, ktab], [kb[b]])
                    for ci, (n, qc, kc, vc) in enumerate(cbs):
                        for vh in range(2):
                            m = 2 * ci + vh
                            hv = hv0 + vh
                            MM(pb[b][:, m * 128:(m + 1) * 128], SEL[:, hv, :], GCT[:, n, :], False, False, [ktab],
                               [kb[b]])
                            MM(pb[b][:, m * 128:(m + 1) * 128], nGCT[:, n, :], SEL[:, hv, :], False,
                               m == nm - 1, [ktab], [kb[b]])
                    ACT(fl(dec4.t[0], W), pb[b][:, 0:W], AF.Exp, [kb[b]], [dec4.k[0]], scale=-1.0)
                    for ci, (n, qc, kc, vc) in enumerate(cbs):
                        sl = slice(2 * ci, 2 * ci + 2)
                        TT("dve", L4.t[0][:, sl, :], dec4.t[0][:, sl, :], bc_d(BETA, n, hv0), ALU.mult,
                           [dec4.k[0], ktab], [L4.k[0]])
                        TT("dve", L4.t[0][:, sl, :], L4.t[0][:, sl, :], bc_v(KKs2.t[0], ci), ALU.mult,
                           [KKs2.k[0], L4.k[0]], [L4.k[0]])
                        TT("pool", A4.t[0][:, sl, :], dec4.t[0][:, sl, :], bc_v(QKs2.t[0], ci), ALU.mult,
                           [dec4.k[0], QKs2.k[0]], [A4.k[0]])
                    b = nb()
                    mm4(b, nm, lambda m: L4.t[0][:, m, :], lambda m: identb[:], [L4.k[0], k_c])
                    CP("act", fl(Mf4.t[0], W), pb[b][:, 0:W], [kb[b]], [Mf4.k[0]])
                    b = nb()
                    mm4(b, nm, lambda m: A4.t[0][:, m, :], lambda m: identb[:], [A4.k[0], k_c])
                    CP("act", fl(AT4.t[bp], W), pb[b][:, 0:W], [kb[b]], [AT4.k[bp]])
                    mk_ = lambda i: msk4[:, i, 0:W]
                    TT("pool", fl(Lp.t[0], W), fl(L4.t[0], W), mk_(0), ALU.mult, [L4.k[0], ktab], [Lp.k[0]])
                    TT("pool", fl(Mp.t[0], W), fl(Mf4.t[0], W), mk_(0), ALU.mult, [Mf4.k[0], ktab], [Mp.k[0]])
                    for si in range(3):
                        TT("pool", fl(Cs[si].t[0], W), fl(L4.t[0], W), mk_(1 + si), ALU.mult, [L4.k[0], ktab],
                           [Cs[si].k[0]])
                        TT("pool", fl(Cts[si].t[0], W), fl(Mf4.t[0], W), mk_(4 + si), ALU.mult, [Mf4.k[0], ktab],
                           [Cts[si].k[0]])
                    tb0 = 2 * bp
                    TT("pool", fl(Tn.t[0], W), id4[:, 0:W], fl(Lp.t[0], W), ALU.subtract, [Lp.k[0], ktab], [Tn.k[0]])
                    TT("pool", fl(Tt.t[tb0], W), id4[:, 0:W], fl(Mp.t[0], W), ALU.subtract, [Mp.k[0], ktab],
                       [Tt.k[tb0]])
                    cl, cm, ct_, cn = 0, 0, 0, 0
                    for lev in range(3):
                        nl, nm_ = 1 - cl, 1 - cm
                        bl = nb()
                        mm4(bl, nm, lambda m: Mp.t[cm][:, m, :], lambda m: Lp.t[cl][:, m, :], [Lp.k[cl], Mp.k[cm]])
                        bm = nb()
                        mm4(bm, nm, lambda m: Lp.t[cl][:, m, :], lambda m: Mp.t[cm][:, m, :], [Lp.k[cl], Mp.k[cm]])
                        CP("act", fl(Lp.t[nl], W), pb[bl][:, 0:W], [kb[bl]], [Lp.k[nl]])
                        CP("dve", fl(Mp.t[nm_], W), pb[bm][:, 0:W], [kb[bm]], [Mp.k[nm_]])
                        TT("pool", fl(IL.t[0], W), fl(Lp.t[nl], W), id4[:, 0:W], ALU.add, [Lp.k[nl], ktab], [IL.k[0]])
                        TT("pool", fl(IM.t[0], W), fl(Mp.t[nm_], W), id4[:, 0:W], ALU.add, [Mp.k[nm_], ktab],
                           [IM.k[0]])
                        nt, nn = 1 - ct_, 1 - cn
                        bt_ = nb()
                        mm4(bt_, nm, lambda m: IL.t[0][:, m, :], lambda m: Tt.t[tb0 + ct_][:, m, :],
                            [IL.k[0], Tt.k[tb0 + ct_]])
                        CP("act", fl(Tt.t[tb0 + nt], W), pb[bt_][:, 0:W], [kb[bt_]], [Tt.k[tb0 + nt]])
                        bn_ = nb()
                        mm4(bn_, nm, lambda m: IM.t[0][:, m, :], lambda m: Tn.t[cn][:, m, :], [IM.k[0], Tn.k[cn]])
                        CP("dve", fl(Tn.t[nn], W), pb[bn_][:, 0:W], [kb[bn_]], [Tn.k[nn]])
                        cl, cm, ct_, cn = nl, nm_, nt, nn
                    for si in range(3):
                        nt, nn = 1 - ct_, 1 - cn
                        bx2 = nb()
                        mm4(bx2, nm, lambda m: Cs[si].t[0][:, m, :], lambda m: Tt.t[tb0 + ct_][:, m, :],
                            [Cs[si].k[0], Tt.k[tb0 + ct_]])
                        CP("dve", fl(X2s.t[0], W), pb[bx2][:, 0:W], [kb[bx2]], [X2s.k[0]])
                        if si < 2:
                            bx = nb()
                            mm4(bx, nm, lambda m: Cts[si].t[0][:, m, :], lambda m: Tn.t[cn][:, m, :],
                                [Cts[si].k[0], Tn.k[cn]])
                            CP("act", fl(Xs.t[0], W), pb[bx][:, 0:W], [kb[bx]], [Xs.k[0]])
                        by2 = nb()
                        mm4(by2, nm, lambda m: Tn.t[cn][:, m, :], lambda m: X2s.t[0][:, m, :], [Tn.k[cn], X2s.k[0]])
                        TT("dve", fl(Tt.t[tb0 + nt], W), fl(Tt.t[tb0 + ct_], W), pb[by2][:, 0:W], ALU.subtract,
                           [Tt.k[tb0 + ct_], kb[by2]], [Tt.k[tb0 + nt]])
                        if si < 2:
                            by = nb()
                            mm4(by, nm, lambda m: Tt.t[tb0 + ct_][:, m, :], lambda m: Xs.t[0][:, m, :],
                                [Tt.k[tb0 + ct_], Xs.k[0]])
                            TT("dve", fl(Tn.t[nn], W), fl(Tn.t[cn], W), pb[by][:, 0:W], ALU.subtract,
                               [Tn.k[cn], kb[by]], [Tn.k[nn]])
                            cn = nn
                        ct_ = nt
                    ti = tb0 + ct_
                    b = nb()
                    mm4(b, nm, lambda m: kbg4.t[0][:, m, :], lambda m: Tt.t[ti][:, m, :], [kbg4.k[0], Tt.k[ti]])
                    ACT(fl(nwT4.t[bp], W), pb[b][:, 0:W], AF.Copy, [kb[b]], [nwT4.k[bp]], scale=-1.0)
                    return ti

                def stage2(bp, ti, hq, ci, n, qc, zc, yout, kZ, kY, kI):
                    hv0 = 2 * hq
                    m0 = 2 * ci
                    TiT = Tt.t[ti]
                    kTi = Tt.k[ti]
                    b = nb()
                    for vh in range(2):
                        MM(pb[b][:, vh * 128:(vh + 1) * 128], TiT[:, m0 + vh, :], bv4.t[bp][:, m0 + vh, :], True, False,
                           [kTi, bv4.k[bp]], [kb[b]])
                        MM(pb[b][:, vh * 128:(vh + 1) * 128], nwT4.t[bp][:, m0 + vh, :], Sb[:, vh, :], False, True,
                           [nwT4.k[bp], kSb], [kb[b]])
                    CP("dve", fl(vnb2.t[0], 256), pb[b][:, 0:256], [kb[b]], [vnb2.k[0]])
                    b = nb()
                    MM(pb[b][:, 0:256], qc, Sb[:].rearrange("p v d -> p (v d)"), True, True, [kI, kSb], [kb[b]])
                    TT("dve", qS2.t[0][:], pb[b][:, 0:256].rearrange("p (v d) -> p v d", d=128), bc_d(EGC, n, hv0),
                       ALU.mult, [kb[b], ktab], [qS2.k[0]])
                    b = nb()
                    for vh in range(2):
                        MM(pb[b][:, vh * 128:(vh + 1) * 128], AT4.t[bp][:, m0 + vh, :], vnb2.t[0][:, vh, :], True, True,
                           [AT4.k[bp], vnb2.k[0]], [kb[b]])
                    TT("dve", fl(o2.t[0], 256), pb[b][:, 0:256], fl(qS2.t[0], 256), ALU.add, [kb[b], qS2.k[0]],
                       [o2.k[0]])
                    b = nb()
                    for vh in range(2):
                        MM(pb[b][:, vh * 128:(vh + 1) * 128], kdec4.t[bp][:, m0 + vh, :], vnb2.t[0][:, vh, :], True, True,
                           [kdec4.k[bp], vnb2.k[0]], [kb[b]])
                    TT("dve", S32[:], S32[:], bc_d(EGL, n, hv0), ALU.mult, [kS, kSb, ktab], [kS])
                    TT("dve", S32[:].rearrange("p v d -> p (v d)"), S32[:].rearrange("p v d -> p (v d)"),
                       pb[b][:, 0:256], ALU.add, [kS, kb[b]], [kS])
                    CP("pool", Sb[:], S32[:], [kS], [kSb])
                    ACT(junk2.t[0][:], o2.t[0][:], AF.Square, [o2.k[0]], [junk2.k[0]])
                    P.op("dve", lambda e: e.tensor_reduce(out=ss2.t[0][:], in_=junk2.t[0][:],
                                                          axis=mybir.AxisListType.X, op=ALU.add),
                         [junk2.k[0]], [ss2.k[0]])
                    ACT(ss2.t[0][:], ss2.t[0][:], AF.Sqrt, [ss2.k[0]], [ss2.k[0]], bias=epsc[:, 1:2], scale=1.0 / HD)
                    P.op("dve", lambda e: e.reciprocal(out=ss2.t[0][:], in_=ss2.t[0][:]), [ss2.k[0]], [ss2.k[0]])
                    TT("pool", onb2.t[0][:], o2.t[0][:], ss2.t[0][:].unsqueeze(2).to_broadcast([128, 2, 128]), ALU.mult,
                       [o2.k[0], ss2.k[0]], [onb2.k[0]])
                    b = nb()
                    for vh in range(2):
                        MM(pb[b][:, vh * 128:(vh + 1) * 128], onb2.t[0][:, vh, :], identb[:], True, True,
                           [onb2.k[0], k_c], [kb[b]])
                    ACT(zg2.t[0][:], zc, AF.Silu, [kZ], [zg2.k[0]])
                    STT("dve", yout, pb[b][:, 0:256].rearrange("p (v d) -> p v d", d=128), gw[:, 0:1], zg2.t[0][:],
                        ALU.mult, ALU.mult, [kb[b], ktab, zg2.k[0]], [kY])

                nbatch = [0]
                for hq in range(GQ):
                    DMA("sp", qT[:], QKVn[hq * 128:(hq + 1) * 128, :], writes=[kin])
                    DMA("sp", kT[:], QKVn[2048 + hq * 128:2048 + (hq + 1) * 128, :], writes=[kin])
                    DMA("sp", vT[:], QKVn[4096 + 2 * hq * 128:4096 + (2 * hq + 2) * 128, :].rearrange(
                        "(v p) t -> p v t", p=128), writes=[kin])
                    MEMSET("pool", S32[:], 0.0, [kS])
                    MEMSET("pool", Sb[:], 0.0, [kSb])
                    for n0 in range(0, NCH, 2):
                        bp = nbatch[0] % 2
                        nbatch[0] += 1
                        cbs = []
                        for n in range(n0, min(n0 + 2, NCH)):
                            c0 = n * 128
                            cbs.append((n, qT[:, c0:c0 + 128], kT[:, c0:c0 + 128],
                                        (lambda vh, c0=c0: vT[:, vh, c0:c0 + 128])))
                        wz = len(cbs) * 128
                        DMA("sp", zB[bp][:, :, 0:wz],
                            HT[8192 + 2 * hq * 128:8192 + (2 * hq + 2) * 128, n0 * 128:n0 * 128 + wz].rearrange(
                                "(v p) t -> p v t", p=128), writes=[kzB[bp]])
                        ti = stage1(bp, hq, cbs, kin)
                        for ci, (n, qc, kc, vc) in enumerate(cbs):
                            c0 = n * 128
                            stage2(bp, ti, hq, ci, n, qc, zB[bp][:, :, ci * 128:(ci + 1) * 128],
                                   yTh[:, :, c0:c0 + 128], kzB[bp], kyT, kin)
                    DMA("pool", YT[2 * hq * 128:(2 * hq + 2) * 128, :].rearrange("(v p) t -> p v t", p=128), yTh[:],
                        reads=[kyT])
                    DMA("pool", dl_p[j, 2 * hq:2 * hq + 2].rearrange("v k d -> k v d"), S32[:], reads=[kS])
                    MEMSET("pool", sqT[:], 0.0, [ksin])
                    MEMSET("pool", skT[:], 0.0, [ksin])
                    MEMSET("pool", svT[:], 0.0, [ksin])
                    MEMSET("pool", szT[:], 0.0, [ksin])
                    CP("dve", sqT[:, 0:1], SN[:, hq:hq + 1], [k_sn], [ksin])
                    CP("dve", skT[:, 0:1], SN[:, 16 + hq:17 + hq], [k_sn], [ksin])
                    for vh in range(2):
                        CP("dve", svT[:, vh, 0:1], SN[:, 32 + 2 * hq + vh:33 + 2 * hq + vh], [k_sn], [ksin])
                        CP("dve", szT[:, vh, 0:1], HS[:, 64 + 2 * hq + vh:65 + 2 * hq + vh], [k_hs], [ksin])
                    DMA("sp", S32[:], st_delta[j, 2 * hq:2 * hq + 2].rearrange("v k d -> k v d"), writes=[kS])
                    CP("pool", Sb[:], S32[:], [kS], [kSb])
                    bp = nbatch[0] % 2
                    nbatch[0] += 1
                    cbs = [(NCH, sqT[:], skT[:], (lambda vh: svT[:, vh, :]))]
                    ti = stage1(bp, hq, cbs, ksin)
                    stage2(bp, ti, hq, 0, NCH, sqT[:], szT[:], syT[:], ksin, ksy, ksin)
                    CP("dve", YS[:, 2 * hq:2 * hq + 2, :].rearrange("p v o -> p (v o)"), syT[:, :, 0], [ksy], [k_ys])
                    DMA("pool", dl_s[j, 2 * hq:2 * hq + 2].rearrange("v k d -> k v d"), S32[:], reads=[kS])
                P.barrier()

        def phase_GDN(l):
            import os
            KG = os.environ.get("KG", "c,g").split(",")
            if "c" in KG:
                phase_C(l)
            if "g" in KG:
                phase_G(l)

        def phase_O(l):
            even = (l % 2 == 0)
            j = l // 2
            KY = 16 if even else 32
            w_out = w_out_even[j] if even else w_out_odd[j]
            TO = 256 if even else 128
            last = (l == DEPTH - 1)
            Xsrc = xT_in if l == 0 else X
            Xdst = yT_out if last else X
            with contextlib.ExitStack() as s2:
                Wo = sbt(s2, "Wo", [128, KY, D], BF16)
                kWo = Tk()
                wst = [sbt(s2, "ow%d" % i, [128, D], F32) for i in range(2)]
                kws = [Tk(), Tk()]
                lg = sbt(s2, "lg", [128, KC], F32)
                lb = sbt(s2, "lb", [128, KC], F32)
                klg = Tk()
                DMA("sp", lg[:], ln_gT[l], writes=[klg])
                DMA("sp", lb[:], ln_bT[l], writes=[klg])
                for k in range(KY):
                    DMA("sp", wst[k % 2][:], w_out[k * 128:(k + 1) * 128, :], writes=[kws[k % 2]])
                    CP("pool" if k % 2 else "dve", Wo[:, k, :], wst[k % 2][:], [kws[k % 2]], [kWo])
                Yt = [sbt(s2, "Yt%d" % i, [128, KY, TO], BF16) for i in range(2)]
                kYt = [Tk(), Tk()]
                Xt = [sbt(s2, "Xt%d" % i, [128, KC, TO], F32) for i in range(2)]
                kXt = [Tk(), Tk()]
                rb = sbt(s2, "rb", [128, KC, TO], BF16)
                rsq = sbt(s2, "rsq", [128, KC, TO], BF16)
                krb = Tk()
                mean = sbt(s2, "mean", [128, TO], F32)
                rstd = sbt(s2, "rstd", [128, TO], F32)
                kst = Tk()
                tmp = [sbt(s2, "otmp%d" % i, [128, TO], F32) for i in range(2)]
                ktmp = [Tk(), Tk()]
                ntile = T // TO
                br = 0
                for ti in range(ntile + 1):
                    samp = (ti == ntile)
                    N = 1 if samp else TO
                    i = ti % 2
                    r = 1 if samp else 0
                    if samp:
                        yt_ap = YS[:, 0:KY, :]
                        kyt = k_ys
                        xt_t = XS
                        kxt = k_xs
                    else:
                        t0 = ti * TO
                        DMA("sp", Yt[i][:], YT[0:KY * 128, t0:t0 + TO].rearrange("(k p) t -> p k t", p=128),
                            writes=[kYt[i]])
                        DMA("sp", Xt[i][:], Xsrc[:, t0:t0 + TO].rearrange("(k p) t -> p k t", p=128),
                            writes=[kXt[i]])
                        yt_ap = Yt[i][:]
                        kyt = kYt[i]
                        xt_t = Xt[i]
                        kxt = kXt[i]
                    for d in range(KC):
                        b = br % 4
                        br += 1
                        for k in range(KY):
                            MM(pb[b][:, 0:N], Wo[:, k, d * 128:(d + 1) * 128], yt_ap[:, k, :], k == 0, k == KY - 1,
                               [kWo, kyt], [kb[b]])
                        ACT(xt_t[:, d, :], xt_t[:, d, :], AF.Copy, [kxt], [kxt], scale=float(ALPHA))
                        STT("dve", xt_t[:, d, :], pb[b][:, 0:N], modT[:, 32 + d, r:r + 1], xt_t[:, d, :],
                            ALU.mult, ALU.add, [kb[b], k_mod, kxt], [kxt])
                        CP("pool", rb[:, d, 0:N], xt_t[:, d, :], [kxt], [krb])
                        ACT(rsq[:, d, 0:N], xt_t[:, d, :], AF.Square, [kxt], [krb])
                    for d in range(KC):
                        MM(pb[4][:, 0:N], onesD[:], rb[:, d, 0:N], d == 0, d == KC - 1, [krb, k_c], [kb[4]])
                    for d in range(KC):
                        MM(pb[5][:, 0:N], onesD[:], rsq[:, d, 0:N], d == 0, d == KC - 1, [krb, k_c], [kb[5]])
                    CP("act", mean[:, 0:N], pb[4][:, 0:N], [kb[4]], [kst])
                    ACT(rstd[:, 0:N], pb[4][:, 0:N], AF.Square, [kb[4]], [kst])
                    TT("dve", rstd[:, 0:N], pb[5][:, 0:N], rstd[:, 0:N], ALU.subtract, [kb[5], kst], [kst])
                    ACT(rstd[:, 0:N], rstd[:, 0:N], AF.Sqrt, [kst], [kst], bias=epsc[:, 0:1])
                    P.op("dve", lambda e: e.reciprocal(out=rstd[:, 0:N], in_=rstd[:, 0:N]), [kst], [kst])
                    for d in range(KC):
                        q = d % 2
                        TT("pool", tmp[q][:, 0:N], xt_t[:, d, :], mean[:, 0:N], ALU.subtract, [kxt, kst], [ktmp[q]])
                        TT("dve", tmp[q][:, 0:N], tmp[q][:, 0:N], rstd[:, 0:N], ALU.mult, [ktmp[q], kst], [ktmp[q]])
                        ACT(xt_t[:, d, :], tmp[q][:, 0:N], AF.Identity, [ktmp[q], klg], [kxt],
                            bias=lb[:, d:d + 1], scale=lg[:, d:d + 1])
                    if samp:
                        if last:
                            DMA("pool", ysT_out[:, :, :], XS[:], reads=[k_xs])
                    else:
                        DMA("pool", Xdst[:, t0:t0 + TO].rearrange("(k p) t -> p k t", p=128), Xt[i][:],
                            reads=[kXt[i]])
                P.barrier()

        import os
        PH = os.environ.get("KPH", "M,UP,A,B,Bs,G,O").split(",")
        for l in range(DEPTH):
            if "M" in PH:
                phase_M(l)
            if "UP" in PH:
                phase_UP(l)
            if l % 2 == 0:
                if "A" in PH:
                    phase_A(l)
                if "B" in PH:
                    phase_B(l)
                if "Bs" in PH:
                    phase_Bs(l)
            else:
                if "G" in PH:
                    phase_GDN(l)
            if "O" in PH:
                phase_O(l)
        P.barrier()
    return nc


def make_consts():
    cp = np.zeros((128, 2048), np.float32)
    m = np.arange(128)[:, None]
    jj = np.arange(128)[None, :]
    cp[:, 0:128] = np.eye(128, dtype=np.float32)
    cp[:, 128:256] = np.where(m >= jj, -1.0, 0.0)
    i9 = np.arange(896)[None, :]
    cp[:, 256:1152] = np.where(m >= (i9 - 384), NEG, 0.0)
    cp[:, 1152:1280] = np.where(m <= jj, 1.0, 0.0)
    cp[:, 1280:1408] = np.where(jj > m, -NEG, 0.0)
    cp[:, 1408:1536] = np.where(m == 127, 1.0, 0.0)
    cp[:, 1536:1664] = np.where(m > jj, 1.0, 0.0)
    cp[:, 1664] = np.arange(128)
    cp[:, 1792:1920] = 1.0
    cp[:, 1920:2048] = np.where(m >= jj, 1.0, 0.0)
    cp2 = np.zeros((128, 7, 128), np.float32)
    cp2[:, 0] = (m // 16 == jj // 16)
    for si, sz in enumerate((16, 32, 64)):
        off = ((m // (2 * sz) == jj // (2 * sz)) & (m % (2 * sz) >= sz) & (jj % (2 * sz) < sz)).astype(np.float32)
        cp2[:, 1 + si] = off
        cp2[:, 4 + si] = off.T
    global CP2
    CP2 = cp2.reshape(128, 7 * 128)
    sel = np.zeros((32, GV, 128), np.float32)
    for h in range(GV):
        sel[h, h, :] = 1.0
    return cp, sel


def fm(v):
    n = v.shape[-1] // 128
    return np.ascontiguousarray(np.moveaxis(v.reshape(v.shape[:-1] + (n, 128)), -1, -2))


_CACHE = {}


def kernel(x_prompt, x_sample, c_prompt, c_sample, cache_k, cache_v, page_table,
           state_conv_a, state_conv_c, state_delta, w_ada, b_ada, ln_g, ln_b,
           w_in_even, conv_w_a, gn_g_a, gn_b_a, sb_bias, w_out_even,
           w_in_odd, conv_w_c, a_log_c, dt_bias_c, gnorm_w_c, w_out_odd):
    A = lambda v: np.ascontiguousarray(np.asarray(v))
    x_prompt = A(x_prompt); x_sample = A(x_sample); c_prompt = A(c_prompt); c_sample = A(c_sample)
    B, T, _ = x_prompt.shape
    NS = x_sample.shape[0]
    DEPTH = w_ada.shape[0]
    NE = (DEPTH + 1) // 2
    NO = DEPTH // 2
    NPOOL = cache_k.shape[1]
    NPG = page_table.shape[1]
    ncores = 8
    key = (T, NPG, NPOOL, DEPTH)
    if key not in _CACHE:
        _CACHE[key] = build(*key)
    nc = _CACHE[key]
    cp, sel = make_consts()
    NO1 = max(NO, 1)

    def pad_odd(a, shape):
        a = A(a)
        if a.shape[0] == 0:
            return np.zeros((1,) + tuple(shape), np.float32)
        return a

    shared = {
        "w_ada": A(w_ada),
        "b_adaT": fm(A(b_ada)),
        "ln_gT": fm(A(ln_g)), "ln_bT": fm(A(ln_b)),
        "w_in_even": A(w_in_even), "w_out_even": A(w_out_even),
        "conv_w_aT": np.ascontiguousarray(A(conv_w_a).reshape(NE, CONVW_A, 8, 128).transpose(0, 3, 2, 1)),
        "gn_gT": fm(A(gn_g_a)), "gn_bT": fm(A(gn_b_a)),
        "sb_biasB": np.ascontiguousarray(np.broadcast_to(A(sb_bias)[:, None, :], (NE, 128, SBH))),
        "cache_k": A(cache_k).reshape(NE, NPOOL * 128, 1024),
        "cache_v": A(cache_v).reshape(NE, NPOOL * 128, 1024),
        "cpack": cp, "selpack": sel, "cpack2": CP2,
    }
    if NO:
        shared.update({
            "w_in_odd": A(w_in_odd), "w_out_odd": A(w_out_odd),
            "conv_w_cT": np.ascontiguousarray(A(conv_w_c).reshape(NO, 4, 64, 128).transpose(0, 3, 2, 1)),
            "a_logB": np.ascontiguousarray(np.broadcast_to(A(a_log_c)[:, None, :], (NO, 128, GV))),
            "dt_biasB": np.ascontiguousarray(np.broadcast_to(A(dt_bias_c)[:, None, :], (NO, 128, GV))),
            "gnorm_wT": np.ascontiguousarray(A(gnorm_w_c)[:, :, None]),
        })
    else:
        shared.update({
            "w_in_odd": np.zeros((1, D, IN_ODD), np.float32), "w_out_odd": np.zeros((1, 2 * D, D), np.float32),
            "conv_w_cT": np.zeros((1, 128, 64, 4), np.float32), "a_logB": np.zeros((1, 128, GV), np.float32),
            "dt_biasB": np.zeros((1, 128, GV), np.float32), "gnorm_wT": np.zeros((1, 128, 1), np.float32),
        })
    in_maps = []
    for c in range(ncores):
        b = (c * B) // ncores
        s = c % NS
        m = dict(shared)
        m["xT"] = np.ascontiguousarray(x_prompt[b].T)
        m["xsT"] = fm(x_sample[s, 0])[:, :, None].copy()
        cc = np.stack([c_prompt[b], c_sample[s]], -1)
        m["cT"] = np.ascontiguousarray(cc.reshape(KC, 128, 2).transpose(1, 0, 2))
        m["ptB"] = np.ascontiguousarray(np.broadcast_to(A(page_table)[s][None, :], (128, NPG))).astype(np.int32)
        m["st_conv_aT"] = np.ascontiguousarray(A(state_conv_a)[:, s].transpose(0, 2, 1))
        if NO:
            m["st_conv_cT"] = np.ascontiguousarray(A(state_conv_c)[:, s].transpose(0, 2, 1))
            m["st_delta"] = np.ascontiguousarray(A(state_delta)[:, s])
        else:
            m["st_conv_cT"] = np.zeros((1, 8192, 3), np.float32)
            m["st_delta"] = np.zeros((1, GV, 128, 128), np.float32)
        in_maps.append(m)
    import os
    if os.environ.get("KTRACE"):
        res = run_bass_kernel_spmd(nc, in_maps, core_ids=list(range(ncores)), trace=True)
        print("EXEC_TIME_NS", res.exec_time_ns, flush=True)
    else:
        res = run_bass_kernel_spmd(nc, in_maps, core_ids=list(range(ncores)))
    R = res.results
    global LAST_R
    LAST_R = R
    cores_b = [(b * ncores) // B for b in range(B)]
    y_p = np.stack([R[c]["yT"].T for c in cores_b]).astype(np.float32)
    y_s = np.stack([R[s]["ysT"][:, :, 0].T.reshape(1, D) for s in range(NS)]).astype(np.float32)
    nk_p = np.stack([R[c]["nk_p"] for c in cores_b], 1).reshape(NE, B, T, SBH, HD)
    nv_p = np.stack([R[c]["nv_p"] for c in cores_b], 1).reshape(NE, B, T, SBH, HD)
    nk_s = np.stack([R[s]["nk_s"] for s in range(NS)], 1).reshape(NE, NS, 1, SBH, HD)
    nv_s = np.stack([R[s]["nv_s"] for s in range(NS)], 1).reshape(NE, NS, 1, SBH, HD)
    ca_p = np.stack([R[c]["ca_pT"].transpose(0, 2, 1) for c in cores_b], 1)
    ca_s = np.stack([R[s]["ca_sT"].transpose(0, 2, 1) for s in range(NS)], 1)
    cc_p = np.stack([R[c]["cc_pT"].transpose(0, 2, 1) for c in cores_b], 1)[:NO]
    cc_s = np.stack([R[s]["cc_sT"].transpose(0, 2, 1) for s in range(NS)], 1)[:NO]
    dl_p = np.stack([R[c]["dl_p"] for c in cores_b], 1)[:NO]
    dl_s = np.stack([R[s]["dl_s"] for s in range(NS)], 1)[:NO]
    f = lambda a: np.ascontiguousarray(a, dtype=np.float32)
    return tuple(f(a) for a in (y_p, y_s, nk_p, nv_p, nk_s, nv_s, ca_p, ca_s, cc_p, cc_s, dl_p, dl_s))
```

```python
import contextlib
import numpy as np
import concourse.bass as bass
import concourse.mybir as mybir
from concourse.bass_utils import run_bass_kernel_spmd

F32 = mybir.dt.float32
BF16 = mybir.dt.bfloat16
I32 = mybir.dt.int32
AF = mybir.ActivationFunctionType
ALU = mybir.AluOpType

D = 2048
KC = 16
W_A = 1024
W_B = 1024
IN_EVEN = 7168
IN_ODD = 12352
CONVW_A = 31
HD = 128
SBH = 8
GV = 32
GQ = 16
ALPHA = 8 ** 0.25
LN_EPS = 1e-5
RMS_EPS = 1e-6
NEG = -30000.0


class Tk:
    __slots__ = ("lastw", "readers")

    def __init__(self):
        self.lastw = []
        self.readers = []


class Prog:
    def __init__(self, nc, st, n_dma_ch=12):
        self.nc = nc
        self.eng = {"pe": nc.tensor, "act": nc.scalar, "dve": nc.vector, "pool": nc.gpsimd, "sp": nc.sync}
        self.cnt = {e: 0 for e in ("pe", "act", "dve", "pool")}
        self.known = {e: {} for e in self.eng}
        self.n_dma_ch = n_dma_ch
        self.chcnt = [0] * n_dma_ch
        self.chrr = {"sp": 0, "pool": 0}
        self.chown = {"sp": list(range(0, n_dma_ch - 4)), "pool": list(range(n_dma_ch - 4, n_dma_ch))}
        self.sems = {}
        for n in ["pe", "act", "dve", "pool"] + ["d%d" % c for c in range(n_dma_ch)]:
            self.sems[n] = st.enter_context(nc.semaphore("s_" + n))

    def _deps(self, eng, reads, writes):
        w = {}
        for t in reads:
            for s, v in t.lastw:
                if w.get(s, 0) < v:
                    w[s] = v
        for t in writes:
            for s, v in t.lastw:
                if w.get(s, 0) < v:
                    w[s] = v
            for s, v in t.readers:
                if w.get(s, 0) < v:
                    w[s] = v
        kn = self.known[eng]
        out = []
        for s, v in w.items():
            if eng == "pe" and s == "pe":
                continue
            if kn.get(s, 0) >= v:
                continue
            kn[s] = v
            out.append((s, v))
        return out

    def _mark(self, tok, reads, writes):
        for t in writes:
            t.lastw = [tok]
            t.readers = []
        for t in reads:
            if t not in writes:
                t.readers.append(tok)
                if len(t.readers) > 40:
                    m = {}
                    for s, v in t.readers:
                        if m.get(s, 0) < v:
                            m[s] = v
                    t.readers = list(m.items())

    def op(self, eng, fn, reads=(), writes=()):
        e = self.eng[eng]
        for s, v in self._deps(eng, reads, writes):
            e.wait_ge(self.sems[s], v)
        self.cnt[eng] += 1
        tok = (eng, self.cnt[eng])
        self._mark(tok, reads, writes)
        fn(e).then_inc(self.sems[eng], 1)
        return tok

    def dma(self, q, fn, reads=(), writes=()):
        e = self.eng[q]
        chs = self.chown[q]
        c = chs[self.chrr[q] % len(chs)]
        self.chrr[q] += 1
        waits = self._deps(q, reads, writes)
        sname = "d%d" % c
        prev = self.chcnt[c] * 16
        kn = self.known[q]
        if prev > 0 and kn.get(sname, 0) < prev:
            kn[sname] = prev
            waits.append((sname, prev))
        for s, v in waits:
            e.wait_ge(self.sems[s], v)
        self.chcnt[c] += 1
        tok = (sname, self.chcnt[c] * 16)
        self._mark(tok, reads, writes)
        fn(e).then_inc(self.sems[sname], 16)
        return tok

    def barrier(self):
        allw = []
        for c in range(self.n_dma_ch):
            if self.chcnt[c]:
                allw.append(("d%d" % c, self.chcnt[c] * 16))
        for en in ("pe", "act", "dve", "pool"):
            if self.cnt[en]:
                allw.append((en, self.cnt[en]))
        for en, e in self.eng.items():
            kn = self.known[en]
            for s, v in allw:
                if kn.get(s, 0) >= v:
                    continue
                kn[s] = v
                e.wait_ge(self.sems[s], v)


def build(T, NPG, NPOOL, DEPTH):
    NE = (DEPTH + 1) // 2
    NO = DEPTH // 2
    NTT = T // 512
    NB = T // 128
    assert T % 512 == 0
    nc = bass.Bass("TRN2", target_bir_lowering=False)

    def din(name, shape, dt=F32):
        return nc.dram_tensor(name, list(shape), dt, kind="ExternalInput").ap()

    def dout(name, shape, dt=F32):
        return nc.dram_tensor(name, list(shape), dt, kind="ExternalOutput").ap()

    def dscr(name, shape, dt=F32):
        return nc.dram_tensor(name, list(shape), dt, kind="Internal").ap()

    NO1 = max(NO, 1)
    xT_in = din("xT", [D, T])
    xsT_in = din("xsT", [128, KC, 1])
    cT_in = din("cT", [128, KC, 2])
    w_ada = din("w_ada", [DEPTH, D, 3 * D])
    b_adaT = din("b_adaT", [DEPTH, 128, 48])
    ln_gT = din("ln_gT", [DEPTH, 128, KC])
    ln_bT = din("ln_bT", [DEPTH, 128, KC])
    w_in_even = din("w_in_even", [NE, D, IN_EVEN])
    w_out_even = din("w_out_even", [NE, D, D])
    conv_w_aT = din("conv_w_aT", [NE, 128, 8, CONVW_A])
    gn_gT = din("gn_gT", [NE, 128, 8])
    gn_bT = din("gn_bT", [NE, 128, 8])
    sb_biasB = din("sb_biasB", [NE, 128, SBH])
    cache_k = din("cache_k", [NE, NPOOL * 128, 1024])
    cache_v = din("cache_v", [NE, NPOOL * 128, 1024])
    ptB = din("ptB", [128, NPG], I32)
    st_conv_aT = din("st_conv_aT", [NE, W_A, 30])
    w_in_odd = din("w_in_odd", [NO1, D, IN_ODD])
    w_out_odd = din("w_out_odd", [NO1, 2 * D, D])
    conv_w_cT = din("conv_w_cT", [NO1, 128, 64, 4])
    a_logB = din("a_logB", [NO1, 128, GV])
    dt_biasB = din("dt_biasB", [NO1, 128, GV])
    gnorm_wT = din("gnorm_wT", [NO1, 128, 1])
    st_conv_cT = din("st_conv_cT", [NO1, 8192, 3])
    st_delta = din("st_delta", [NO1, GV, 128, 128])
    cpack = din("cpack", [128, 2048])
    cpack2 = din("cpack2", [128, 7 * 128])
    selpack = din("selpack", [32, GV, 128])
    yT_out = dout("yT", [D, T])
    ysT_out = dout("ysT", [128, KC, 1])
    nk_p = dout("nk_p", [NE, T, 1024])
    nv_p = dout("nv_p", [NE, T, 1024])
    nk_s = dout("nk_s", [NE, 1, 1024])
    nv_s = dout("nv_s", [NE, 1, 1024])
    ca_pT = dout("ca_pT", [NE, W_A, 30])
    ca_sT = dout("ca_sT", [NE, W_A, 30])
    cc_pT = dout("cc_pT", [NO1, 8192, 3])
    cc_sT = dout("cc_sT", [NO1, 8192, 3])
    dl_p = dout("dl_p", [NO1, GV, 128, 128])
    dl_s = dout("dl_s", [NO1, GV, 128, 128])
    import os
    DBG = bool(os.environ.get("KDBG"))
    if DBG:
        dbg_mod = dout("dbg_mod", [DEPTH, 128, 48, 2])
        dbg32 = dout("dbg32", [2, 12, 128, 128])
        dbg16 = dout("dbg16", [2, 12, 128, 128], BF16)
    X = dscr("Xscr", [D, T])
    HT = dscr("HTscr", [IN_ODD if NO else IN_EVEN, T])
    YT = dscr("YTscr", [2 * D, T], BF16)
    Vb = dscr("Vbscr", [T, 1024], BF16)
    QKVn = dscr("QKVn", [8192, T], BF16)

    with contextlib.ExitStack() as st:
        P = Prog(nc, st)

        uid = [0]

        def sbt(stack, name, shape, dt):
            uid[0] += 1
            return stack.enter_context(nc.sbuf_tensor("%s_%d" % (name, uid[0]), list(shape), dt))

        pb = [st.enter_context(nc.psum_tensor("pb%d" % i, [128, 512], F32)) for i in range(8)]
        kb = [Tk() for _ in range(8)]

        def ACT(out, in_, func, reads, writes, bias=None, scale=None, accum=None):
            kw = {}
            if bias is not None:
                kw["bias"] = bias
            if scale is not None:
                kw["scale"] = scale
            if accum is not None:
                kw["accum_out"] = accum
            return P.op("act", lambda e: e.activation(out=out, in_=in_, func=func, **kw), reads, writes)

        def TT(eng, out, a, b, op, reads, writes):
            return P.op(eng, lambda e: e.tensor_tensor(out=out, in0=a, in1=b, op=op), reads, writes)

        def TS(eng, out, a, s1, s2, op0, op1, reads, writes):
            if s2 is None:
                return P.op(eng, lambda e: e.tensor_scalar(out=out, in0=a, scalar1=s1, scalar2=None, op0=op0),
                            reads, writes)
            return P.op(eng, lambda e: e.tensor_scalar(out=out, in0=a, scalar1=s1, scalar2=s2, op0=op0, op1=op1),
                        reads, writes)

        def STT(eng, out, in0, scalar, in1, op0, op1, reads, writes):
            return P.op(eng, lambda e: e.scalar_tensor_tensor(out=out, in0=in0, scalar=scalar, in1=in1,
                                                              op0=op0, op1=op1), reads, writes)

        def CP(eng, out, in_, reads, writes):
            if eng == "act":
                return P.op("act", lambda e: e.activation(out=out, in_=in_, func=AF.Copy), reads, writes)
            return P.op(eng, lambda e: e.tensor_copy(out=out, in_=in_), reads, writes)

        def MM(out, lhsT, rhs, start, stop, reads, writes):
            return P.op("pe", lambda e: e.matmul(out, lhsT=lhsT, rhs=rhs, start=start, stop=stop,
                                                 skip_group_check=True), reads, writes)

        def DMA(q, out, in_, reads=(), writes=()):
            return P.dma(q, lambda e: e.dma_start(out=out, in_=in_), reads, writes)

        def MEMSET(eng, ap, val, writes):
            return P.op(eng, lambda e: e.memset(ap, val), (), writes)

        evac_rr = [0]

        def EVAC(out, in_, reads, writes):
            evac_rr[0] += 1
            if evac_rr[0] % 2:
                return CP("act", out, in_, reads, writes)
            return CP("dve", out, in_, reads, writes)

        cpk = sbt(st, "cpk", [128, 2048], F32)
        k_c = Tk()
        identf = cpk[:, 0:128]
        trif = cpk[:, 1152:1280]
        sellast = cpk[:, 1408:1536]
        identb = sbt(st, "identb", [128, 128], BF16)
        nuincl = sbt(st, "nuincl", [128, 128], BF16)
        mkb = sbt(st, "mkb", [128, 896], BF16)
        nones = sbt(st, "nones", [128, 128], BF16)
        onesG = sbt(st, "onesG", [128, 128], BF16)
        onesD = sbt(st, "onesD", [128, 128], BF16)
        ones1 = sbt(st, "ones1", [128, 128], BF16)
        upbig = sbt(st, "upbig", [128, 128], BF16)
        lowstrict = sbt(st, "lowstrict", [128, 128], F32)
        DMA("sp", cpk[:], cpack[:, :], writes=[k_c])
        CP("dve", identb[:], cpk[:, 0:128], [k_c], [k_c])
        CP("dve", nuincl[:], cpk[:, 128:256], [k_c], [k_c])
        CP("dve", mkb[:], cpk[:, 256:1152], [k_c], [k_c])
        CP("dve", upbig[:], cpk[:, 1280:1408], [k_c], [k_c])
        CP("dve", lowstrict[:], cpk[:, 1536:1664], [k_c], [k_c])
        MEMSET("pool", nones[:], -1.0, [k_c])
        MEMSET("pool", onesG[:], 1.0 / 128, [k_c])
        MEMSET("pool", onesD[:], 1.0 / D, [k_c])
        MEMSET("pool", ones1[:], 1.0, [k_c])
        zerob = sbt(st, "zerob", [128, 128], BF16)
        MEMSET("pool", zerob[:], 0.0, [k_c])
        epsc = sbt(st, "epsc", [128, 2], F32)
        MEMSET("pool", epsc[:, 0:1], LN_EPS, [k_c])
        MEMSET("pool", epsc[:, 1:2], RMS_EPS, [k_c])
        modT = sbt(st, "modT", [128, 48, 2], F32)
        k_mod = Tk()
        XS = sbt(st, "XS", [128, KC, 1], F32)
        k_xs = Tk()
        HS = sbt(st, "HS", [128, 100], F32)
        k_hs = Tk()
        YS = sbt(st, "YS", [128, 32, 1], BF16)
        k_ys = Tk()
        SN = sbt(st, "SN", [128, 64], F32)
        k_sn = Tk()
        DMA("sp", XS[:], xsT_in[:, :, :], writes=[k_xs])
        P.barrier()

        def phase_M(l):
            with contextlib.ExitStack() as s2:
                wst = [sbt(s2, "mw%d" % i, [128, KC, 128], F32) for i in range(2)]
                kw = [Tk(), Tk()]
                ct = sbt(s2, "mct", [128, KC, 2], F32)
                sc = sbt(s2, "msc", [128, KC, 2], F32)
                bT = sbt(s2, "mbT", [128, 48], F32)
                k1 = Tk()
                DMA("sp", ct[:], cT_in[:, :, :], writes=[k1])
                DMA("sp", bT[:], b_adaT[l], writes=[k1])
                ACT(sc[:], ct[:], AF.Silu, [k1], [k1])
                for jj in range(48):
                    i = jj % 2
                    DMA("sp", wst[i][:], w_ada[l][:, jj * 128:(jj + 1) * 128].rearrange("(k p) c -> p k c", p=128),
                        writes=[kw[i]])
                    b = jj % 4
                    for k in range(KC):
                        MM(pb[b][:, 0:2], wst[i][:, k, :], sc[:, k, :], k == 0, k == KC - 1, [kw[i], k1], [kb[b]])
                    TT("dve", modT[:, jj, :], pb[b][:, 0:2], bT[:, jj:jj + 1].to_broadcast([128, 2]), ALU.add,
                       [kb[b], k1], [k_mod])
                TS("dve", modT[:, 16:48, :], modT[:, 16:48, :], 1.0, None, ALU.add, None, [k_mod], [k_mod])
                if DBG:
                    DMA("sp", dbg_mod[l], modT[:], reads=[k_mod])
                P.barrier()

        def phase_UP(l):
            even = (l % 2 == 0)
            j = l // 2
            w_in = w_in_even[j] if even else w_in_odd[j]
            Xsrc = xT_in if l == 0 else X
            TH = min(T, 2048)
            with contextlib.ExitStack() as s2:
                U = sbt(s2, "U", [128, KC, T], BF16)
                kU = Tk()
                Us = sbt(s2, "Us", [128, KC, 1], BF16)
                with contextlib.ExitStack() as s3:
                    xst = [sbt(s3, "xst%d" % i, [128, 512], F32) for i in range(3)]
                    kx = [Tk() for _ in range(3)]
                    n = 0
                    for k in range(KC):
                        for t in range(NTT):
                            i = n % 3
                            n += 1
                            DMA("sp", xst[i][:], Xsrc[k * 128:(k + 1) * 128, t * 512:(t + 1) * 512], writes=[kx[i]])
                            ACT(U[:, k, t * 512:(t + 1) * 512], xst[i][:], AF.Identity, [kx[i], k_mod], [kU],
                                bias=modT[:, k, 0:1], scale=modT[:, 16 + k, 0:1])
                        ACT(Us[:, k, :], XS[:, k, :], AF.Identity, [k_xs, k_mod], [kU],
                            bias=modT[:, k, 1:2], scale=modT[:, 16 + k, 1:2])
                    P.barrier()
                import os
                KUP = os.environ.get("KUP", "i,ii,s,s2,nk,vb").split(",")
                if even:
                    chunks = [(c, 128) for c in range(0, 5120, 128)] + [(c, 128) for c in range(6144, 7168, 128)]
                else:
                    chunks = [(c, 128) for c in range(0, 12288, 128)] + [(12288, 64)]
                if "i" not in KUP:
                    chunks = []
                with contextlib.ExitStack() as s3:
                    wst = [sbt(s3, "pw%d" % i, [128, KC, 128], F32) for i in range(2)]
                    wb = [sbt(s3, "pwb%d" % i, [128, KC, 128], BF16) for i in range(2)]
                    hst = [sbt(s3, "ph%d" % i, [128, TH], F32) for i in range(2)]
                    kws = [Tk(), Tk()]
                    kwb = [Tk(), Tk()]
                    kh = [Tk(), Tk()]
                    hi = 0
                    br = 0
                    for ci, (c0, M) in enumerate(chunks):
                        i = ci % 2
                        DMA("sp", wst[i][:, :, 0:M], w_in[:, c0:c0 + M].rearrange("(k p) c -> p k c", p=128),
                            writes=[kws[i]])
                        CP("pool", wb[i][:, :, 0:M], wst[i][:, :, 0:M], [kws[i]], [kwb[i]])
                        for half in range(T // TH):
                            hb = hi % 2
                            hi += 1
                            for tt in range(TH // 512):
                                t = half * (TH // 512) + tt
                                b = br % 4
                                br += 1
                                for k in range(KC):
                                    MM(pb[b][0:M, :], wb[i][:, k, 0:M], U[:, k, t * 512:(t + 1) * 512],
                                       k == 0, k == KC - 1, [kwb[i], kU], [kb[b]])
                                EVAC(hst[hb][0:M, tt * 512:(tt + 1) * 512], pb[b][0:M, :], [kb[b]], [kh[hb]])
                            DMA("pool", HT[c0:c0 + M, half * TH:(half + 1) * TH], hst[hb][0:M, :], reads=[kh[hb]])
                        if "s" in KUP:
                            for k in range(KC):
                                MM(pb[4][0:M, 0:1], wb[i][:, k, 0:M], Us[:, k, :], k == 0, k == KC - 1,
                                   [kwb[i], kU], [kb[4]])
                            CP("act", HS[0:M, c0 // 128:c0 // 128 + 1], pb[4][0:M, 0:1], [kb[4]], [k_hs])
                    P.barrier()
                if even and "ii" in KUP:
                    with contextlib.ExitStack() as s3:
                        W2 = sbt(s3, "W2", [128, KC, 512], BF16)
                        kW2 = Tk()
                        st2 = [sbt(s3, "st2%d" % i, [128, 512], F32) for i in range(2)]
                        ks2 = [Tk(), Tk()]
                        ev = [sbt(s3, "ev%d" % i, [128, 512], F32) for i in range(2)]
                        kev = [Tk(), Tk()]
                        evb = [sbt(s3, "evb%d" % i, [128, 512], BF16) for i in range(2)]
                        kevb = [Tk(), Tk()]
                        evs = sbt(s3, "evs", [1, 512], F32)
                        kevs = Tk()
                        n = 0
                        for ct in range(4):
                            c0 = 4096 + ct * 512
                            isv = ct >= 2
                            dst_p = nv_p if isv else nk_p
                            dst_s = nv_s if isv else nk_s
                            oc = (ct % 2) * 512
                            for k in range(KC):
                                DMA("sp", st2[k % 2][:], w_in[k * 128:(k + 1) * 128, c0:c0 + 512],
                                    writes=[ks2[k % 2]])
                                CP("pool" if k % 2 else "dve", W2[:, k, :], st2[k % 2][:], [ks2[k % 2]], [kW2])
                            for tb in range(NB):
                                b = n % 4
                                e_i = n % 2
                                n += 1
                                for k in range(KC):
                                    MM(pb[b][:, :], U[:, k, tb * 128:(tb + 1) * 128], W2[:, k, :],
                                       k == 0, k == KC - 1, [kU, kW2], [kb[b]])
                                CP("act", ev[e_i][:], pb[b][:, :], [kb[b]], [kev[e_i]])
                                if "nk" in KUP:
                                    DMA("pool", dst_p[j, tb * 128:(tb + 1) * 128, oc:oc + 512], ev[e_i][:],
                                        reads=[kev[e_i]])
                                if isv and "vb" in KUP:
                                    CP("dve", evb[e_i][:], ev[e_i][:], [kev[e_i]], [kevb[e_i]])
                                    DMA("sp", Vb[tb * 128:(tb + 1) * 128, oc:oc + 512], evb[e_i][:],
                                        reads=[kevb[e_i]])
                            if "s2" in KUP:
                                for k in range(KC):
                                    MM(pb[4][0:1, :], Us[:, k, 0:1], W2[:, k, :], k == 0, k == KC - 1,
                                       [kU, kW2], [kb[4]])
                                CP("act", evs[:], pb[4][0:1, :], [kb[4]], [kevs])
                                DMA("pool", dst_s[j, 0:1, oc:oc + 512], evs[:], reads=[kevs])
                        P.barrier()

        def phase_A(l):
            j = l // 2
            with contextlib.ExitStack() as s2:
                G = sbt(s2, "G", [128, 32 + T], BF16)
                kG = Tk()
                Gs = sbt(s2, "Gs", [128, 32], BF16)
                hs32 = sbt(s2, "hs32", [128, 32], F32)
                khs = Tk()
                cw = sbt(s2, "cw", [128, 8, CONVW_A], F32)
                gg = sbt(s2, "gg", [128, 8], F32)
                gb = sbt(s2, "gb", [128, 8], F32)
                kcw = Tk()
                Dg = sbt(s2, "Dg", [128, CONVW_A, 128], BF16)
                kDg = Tk()
                va = [sbt(s2, "va%d" % i, [128, 512], F32) for i in range(2)]
                gl = [sbt(s2, "gl%d" % i, [128, 512], F32) for i in range(2)]
                gt = [sbt(s2, "gt%d" % i, [128, 512], F32) for i in range(2)]
                kin = [Tk(), Tk()]
                kgt = [Tk(), Tk()]
                g32 = sbt(s2, "g32", [128, 512], F32)
                kg32 = Tk()
                yf = sbt(s2, "yf", [128, 512], F32)
                ybf = sbt(s2, "ybf", [128, 512], BF16)
                ysq = sbt(s2, "ysq", [128, 512], BF16)
                t1 = sbt(s2, "t1", [128, 512], F32)
                t2 = sbt(s2, "t2", [128, 512], F32)
                yo = [sbt(s2, "yo%d" % i, [128, 512], BF16) for i in range(2)]
                kyo = [Tk(), Tk()]
                kt = Tk()
                DMA("sp", cw[:], conv_w_aT[j], writes=[kcw])
                DMA("sp", gg[:], gn_gT[j], writes=[kcw])
                DMA("sp", gb[:], gn_bT[j], writes=[kcw])
                n = 0

                def post(N, ypsum, kyp, gate_ap, kgate, cc, out_bf, kout):
                    CP("act", yf[:, 0:N], ypsum, [kyp], [kt])
                    CP("dve", ybf[:, 0:N], yf[:, 0:N], [kt], [kt])
                    ACT(ysq[:, 0:N], yf[:, 0:N], AF.Square, [kt], [kt])
                    MM(pb[5][:, 0:N], onesG[:], ybf[:, 0:N], True, True, [kt, k_c], [kb[5]])
                    MM(pb[6][:, 0:N], onesG[:], ysq[:, 0:N], True, True, [kt, k_c], [kb[6]])
                    CP("act", t2[:, 0:N], pb[5][:, 0:N], [kb[5]], [kt])
                    TT("dve", t1[:, 0:N], yf[:, 0:N], t2[:, 0:N], ALU.subtract, [kt], [kt])
                    ACT(t2[:, 0:N], t2[:, 0:N], AF.Square, [kt], [kt])
                    TT("dve", t2[:, 0:N], pb[6][:, 0:N], t2[:, 0:N], ALU.subtract, [kt, kb[6]], [kt])
                    ACT(t2[:, 0:N], t2[:, 0:N], AF.Sqrt, [kt], [kt], bias=epsc[:, 0:1])
                    P.op("dve", lambda e: e.reciprocal(out=t2[:, 0:N], in_=t2[:, 0:N]), [kt], [kt])
                    TT("dve", t1[:, 0:N], t1[:, 0:N], t2[:, 0:N], ALU.mult, [kt], [kt])
                    ACT(t1[:, 0:N], t1[:, 0:N], AF.Silu, [kt, kcw], [kt], bias=gb[:, cc:cc + 1],
                        scale=gg[:, cc:cc + 1])
                    ACT(t2[:, 0:N], gate_ap, AF.Silu, [kgate], [kt])
                    TT("dve", out_bf, t1[:, 0:N], t2[:, 0:N], ALU.mult, [kt], [kout])

                for cc in range(8):
                    r0 = cc * 128
                    for kk in range(CONVW_A):
                        TS("dve" if kk % 2 else "pool", Dg[:, kk, :], identf, cw[:, cc, kk:kk + 1], None, ALU.mult,
                           None, [k_c, kcw], [kDg])
                    MEMSET("pool", G[:, 0:32], 0.0, [kG])
                    for t in range(NTT):
                        i = n % 2
                        n += 1
                        DMA("sp", va[i][:], HT[r0:r0 + 128, t * 512:(t + 1) * 512], writes=[kin[i]])
                        DMA("sp", gl[i][:], HT[1024 + r0:1024 + r0 + 128, t * 512:(t + 1) * 512], writes=[kin[i]])
                        ACT(gl[i][:], gl[i][:], AF.Sigmoid, [kin[i]], [kin[i]])
                        TT("dve", g32[:], va[i][:], gl[i][:], ALU.mult, [kin[i]], [kg32])
                        CP("pool", G[:, 32 + t * 512:32 + (t + 1) * 512], g32[:], [kg32], [kG])
                        if t == NTT - 1:
                            DMA("pool", ca_pT[j, r0:r0 + 128, :], g32[:, 482:512], reads=[kg32])
                    for t in range(NTT):
                        i = n % 2
                        n += 1
                        DMA("sp", gt[i][:], HT[2048 + r0:2048 + r0 + 128, t * 512:(t + 1) * 512], writes=[kgt[i]])
                        b = t % 2
                        for kk in range(CONVW_A):
                            MM(pb[b][:, :], Dg[:, kk, :], G[:, 2 + t * 512 + kk:2 + t * 512 + kk + 512],
                               kk == 0, kk == CONVW_A - 1, [kDg, kG], [kb[b]])
                        post(512, pb[b][:, :], kb[b], gt[i][:], kgt[i], cc, yo[i][:], kyo[i])
                        DMA("pool", YT[r0:r0 + 128, t * 512:(t + 1) * 512], yo[i][:], reads=[kyo[i]])
                    DMA("sp", hs32[:, 0:30], st_conv_aT[j, r0:r0 + 128, :], writes=[khs])
                    ACT(t2[:, 0:1], HS[:, 8 + cc:9 + cc], AF.Sigmoid, [k_hs], [kt])
                    TT("dve", hs32[:, 30:31], HS[:, cc:cc + 1], t2[:, 0:1], ALU.mult, [k_hs, kt], [khs])
                    CP("dve", Gs[:, 0:31], hs32[:, 0:31], [khs], [khs])
                    DMA("pool", ca_sT[j, r0:r0 + 128, :], hs32[:, 1:31], reads=[khs])
                    for kk in range(CONVW_A):
                        MM(pb[2][:, 0:1], Dg[:, kk, :], Gs[:, kk:kk + 1], kk == 0, kk == CONVW_A - 1,
                           [kDg, khs], [kb[2]])
                    post(1, pb[2][:, 0:1], kb[2], HS[:, 16 + cc:17 + cc], k_hs, cc, YS[:, cc, :], k_ys)
                P.barrier()

        def phase_B(l):
            j = l // 2
            with contextlib.ExitStack() as s2:
                sbb = sbt(s2, "sbb", [128, SBH], F32)
                ksbb = Tk()
                DMA("sp", sbb[:], sb_biasB[j], writes=[ksbb])

                class Str:
                    pass

                def mkstream(sid):
                    S_ = Str()
                    S_.QT = sbt(s2, "QT", [128, T], BF16)
                    S_.KT = sbt(s2, "KT", [128, T], BF16)
                    S_.Vh = sbt(s2, "Vh", [128, NB, 128], BF16)
                    S_.kq = Tk()
                    S_.stg = [sbt(s2, "stg%d" % i, [128, 512], F32) for i in range(2)]
                    S_.kstg = [Tk(), Tk()]
                    S_.bg = [sbt(s2, "bg%d" % i, [128, 512], F32) for i in range(2)]
                    S_.kbg = [Tk(), Tk()]
                    S_.E = [sbt(s2, "E%d" % i, [128, 512], F32) for i in range(2)]
                    S_.SP32 = [sbt(s2, "SP%d" % i, [128, 512], F32) for i in range(2)]
                    S_.SPb = [sbt(s2, "SPb%d" % i, [128, 512], BF16) for i in range(2)]
                    S_.Wt = [sbt(s2, "Wt%d" % i, [128, 512], BF16) for i in range(2)]
                    S_.kE = [Tk(), Tk()]
                    S_.kSP = [Tk(), Tk()]
                    S_.kSPb = [Tk(), Tk()]
                    S_.kWt = [Tk(), Tk()]
                    S_.CAR = sbt(s2, "CAR", [128, 512], F32)
                    S_.CARb = [sbt(s2, "CARb%d" % i, [128, 512], BF16) for i in range(2)]
                    S_.kCAR = Tk()
                    S_.kCARb = [Tk(), Tk()]
                    S_.yo = [sbt(s2, "byo%d" % i, [128, 512], BF16) for i in range(2)]
                    S_.kyo = [Tk(), Tk()]
                    S_.zb = [4 * sid, 4 * sid + 1]
                    S_.ob = [4 * sid + 2, 4 * sid + 3]
                    S_.n = 0
                    S_.nblk = 0
                    return S_

                def head_stream(S_, h):
                    for t in range(NTT):
                        for which in range(2):
                            i = S_.n % 2
                            S_.n += 1
                            row = (3072 if which == 0 else 4096) + h * 128
                            DMA("sp", S_.stg[i][:], HT[row:row + 128, t * 512:(t + 1) * 512], writes=[S_.kstg[i]])
                            if which == 0:
                                ACT(S_.QT[:, t * 512:(t + 1) * 512], S_.stg[i][:], AF.Copy, [S_.kstg[i]], [S_.kq],
                                    scale=float(HD ** -0.5))
                            else:
                                CP("dve", S_.KT[:, t * 512:(t + 1) * 512], S_.stg[i][:], [S_.kstg[i]], [S_.kq])
                        yield
                    DMA("sp", S_.Vh[:], Vb[:, h * 128:(h + 1) * 128].rearrange("(nb p) d -> p nb d", p=128),
                        writes=[S_.kq])
                    for qt in range(NTT):
                        q0 = qt * 512
                        gi = qt % 2
                        DMA("sp", S_.bg[gi][:], HT[6144 + h * 128:6144 + (h + 1) * 128, q0:q0 + 512],
                            writes=[S_.kbg[gi]])
                        ACT(S_.bg[gi][:], S_.bg[gi][:], AF.Silu, [S_.kbg[gi]], [S_.kbg[gi]])
                        ob = S_.ob[qt % 2]
                        blocks = list(range(q0 // 128 + 3, -1, -1))
                        for bi, kbk in enumerate(blocks):
                            r = kbk - q0 // 128
                            zi = S_.nblk % 2
                            S_.nblk += 1
                            zb = S_.zb[zi]
                            diag = r >= 0
                            MM(pb[zb][:, :], S_.KT[:, kbk * 128:(kbk + 1) * 128], S_.QT[:, q0:q0 + 512], True, False,
                               [S_.kq], [kb[zb]])
                            if diag:
                                MM(pb[zb][:, :], identb[:], mkb[:, 384 - r * 128:384 - r * 128 + 512], False, False,
                                   [k_c], [kb[zb]])
                            ACT(S_.E[zi][:], pb[zb][:, :], AF.Exp, [kb[zb], ksbb], [S_.kE[zi]], bias=sbb[:, h:h + 1])
                            yield
                            ACT(S_.SP32[zi][:], S_.E[zi][:], AF.Ln, [S_.kE[zi]], [S_.kSP[zi]], bias=1.0)
                            CP("dve", S_.SPb[zi][:], S_.SP32[zi][:], [S_.kSP[zi]], [S_.kSPb[zi]])
                            MM(pb[zb][:, :], nuincl[:], S_.SPb[zi][:], False, bi == 0, [S_.kSPb[zi], k_c], [kb[zb]])
                            if bi > 0:
                                MM(pb[zb][:, :], nones[:], S_.CARb[(bi - 1) % 2][:], False, True,
                                   [S_.kCARb[(bi - 1) % 2], k_c], [kb[zb]])
                            yield
                            ACT(S_.Wt[zi][:], pb[zb][:, :], AF.Exp, [kb[zb], ksbb], [S_.kWt[zi]], bias=sbb[:, h:h + 1])
                            MM(pb[ob][:, :], S_.Vh[:, kbk, :], S_.Wt[zi][:], bi == 0, bi == len(blocks) - 1,
                               [S_.kq, S_.kWt[zi]], [kb[ob]])
                            if bi < len(blocks) - 1:
                                if bi == 0:
                                    CP("pool", S_.CAR[:], S_.SP32[zi][:], [S_.kSP[zi]], [S_.kCAR])
                                else:
                                    TT("pool", S_.CAR[:], S_.CAR[:], S_.SP32[zi][:], ALU.add, [S_.kSP[zi], S_.kCAR],
                                       [S_.kCAR])
                                CP("pool", S_.CARb[bi % 2][:], S_.CAR[:], [S_.kCAR], [S_.kCARb[bi % 2]])
                            yield
                        TT("dve", S_.yo[gi][:], pb[ob][:, :], S_.bg[gi][:], ALU.mult, [kb[ob], S_.kbg[gi]],
                           [S_.kyo[gi]])
                        DMA("pool", YT[1024 + h * 128:1024 + (h + 1) * 128, q0:q0 + 512], S_.yo[gi][:],
                            reads=[S_.kyo[gi]])
                        yield

                streams = [mkstream(0), mkstream(1)]
                for h0 in range(0, SBH, 2):
                    gens = [head_stream(streams[0], h0), head_stream(streams[1], h0 + 1)]
                    alive = [True, True]
                    while any(alive):
                        for gi_ in range(2):
                            if alive[gi_]:
                                try:
                                    next(gens[gi_])
                                except StopIteration:
                                    alive[gi_] = False
                P.barrier()

        def phase_Bs(l):
            j = l // 2
            NC8 = NPG * SBH
            with contextlib.ExitStack() as s2:
                sbb = sbt(s2, "ssbb", [128, SBH], F32)
                ptf = sbt(s2, "ptf", [128, NPG], F32)
                pti = sbt(s2, "pti", [128, NPG], I32)
                idx = sbt(s2, "idx", [128, NPG], I32)
                k0 = Tk()
                DMA("sp", sbb[:], sb_biasB[j], writes=[k0])
                DMA("sp", pti[:], ptB[:, :], writes=[k0])
                CP("dve", ptf[:], pti[:], [k0], [k0])
                TS("dve", ptf[:], ptf[:], 128.0, cpk[:, 1664:1665], ALU.mult, ALU.add, [k0, k_c], [k0])
                if j > 0:
                    TS("dve", ptf[:], ptf[:], float(j * NPOOL * 128), None, ALU.add, None, [k0], [k0])
                CP("dve", idx[:], ptf[:], [k0], [k0])
                qcol = sbt(s2, "qcol", [128, SBH], F32)
                TS("dve", qcol[:], HS[:, 24:32], float(HD ** -0.5), None, ALU.mult, None, [k_hs], [k0])
                qb = sbt(s2, "qb", [128, SBH, 128], F32)
                dq = sbt(s2, "dq", [128, 128], F32)
                for h in range(SBH):
                    TS("dve", dq[:], identf, qcol[:, h:h + 1], None, ALU.mult, None, [k0, k_c], [k0])
                    MM(pb[0][:, 0:128], cpk[:, 1792:1920], dq[:], True, True, [k0, k_c], [kb[0]])
                    CP("act", qb[:, h, :], pb[0][:, 0:128], [kb[0]], [k0])
                Z = sbt(s2, "Z", [128, NPG, SBH], F32)
                kZ = Tk()
                kpg = [sbt(s2, "kpg%d" % i, [128, 1024], F32) for i in range(2)]
                kkpg = [Tk(), Tk()]
                junk = sbt(s2, "junk", [128, 128], F32)
                kj = Tk()
                cflat_k = cache_k.rearrange("e r c -> (e r) c")
                cflat_v = cache_v.rearrange("e r c -> (e r) c")
                for pg in range(NPG):
                    i = pg % 2
                    P.dma("pool", lambda e, i=i, pg=pg: e.indirect_dma_start(
                        out=kpg[i][:], out_offset=None, in_=cflat_k,
                        in_offset=bass.IndirectOffsetOnAxis(ap=idx[:, pg:pg + 1], axis=0)),
                        reads=[k0], writes=[kkpg[i]])
                    for h in range(SBH):
                        STT("dve", junk[:], kpg[i][:, h * 128:(h + 1) * 128], 1.0, qb[:, h, :], ALU.mult, ALU.mult,
                            [kkpg[i], k0], [kj])
                        P.op("dve", lambda e, pg=pg, h=h: e.tensor_reduce(
                            out=Z[:, pg, h:h + 1], in_=junk[:], axis=mybir.AxisListType.X, op=ALU.add),
                            [kj], [kZ])
                Zf = Z[:].rearrange("p g h -> p (g h)")
                for pg in range(NPG):
                    TT("dve", Z[:, pg, :], Z[:, pg, :], sbb[:], ALU.add, [kZ, k0], [kZ])
                Ee = sbt(s2, "Ee", [128, NC8], F32)
                SPs = sbt(s2, "SPs", [128, NC8], F32)
                TOT = sbt(s2, "TOT", [128, NC8], F32)
                TO2 = sbt(s2, "TO2", [128, NC8], F32)
                ACT(Ee[:], Zf, AF.Exp, [kZ], [kZ])
                ACT(SPs[:], Ee[:], AF.Ln, [kZ], [kZ], bias=1.0)
                ARG = sbt(s2, "ARG", [128, NC8], F32)
                for c0 in range(0, NC8, 512):
                    c1 = min(NC8, c0 + 512)
                    MM(pb[1][:, 0:c1 - c0], cpk[:, 1920:2048], SPs[:, c0:c1], True, True, [kZ, k_c], [kb[1]])
                    TT("dve", ARG[:, c0:c1], Zf[:, c0:c1], pb[1][:, 0:c1 - c0], ALU.subtract, [kZ, kb[1]], [kZ])
                    MM(pb[2][:, 0:c1 - c0], cpk[:, 1792:1920], SPs[:, c0:c1], True, True, [kZ, k_c], [kb[2]])
                    CP("act", TOT[:, c0:c1], pb[2][:, 0:c1 - c0], [kb[2]], [kZ])
                T3 = TOT[:].rearrange("p (g h) -> p g h", h=SBH)
                T4 = TO2[:].rearrange("p (g h) -> p g h", h=SBH)
                src, dst = T3, T4
                sh = 1
                while sh < NPG:
                    CP("dve", dst[:, NPG - sh:NPG, :], src[:, NPG - sh:NPG, :], [kZ], [kZ])
                    TT("dve", dst[:, 0:NPG - sh, :], src[:, 0:NPG - sh, :], src[:, sh:NPG, :], ALU.add, [kZ], [kZ])
                    src, dst = dst, src
                    sh *= 2
                A3 = ARG[:].rearrange("p (g h) -> p g h", h=SBH)
                if NPG > 1:
                    TT("dve", A3[:, 0:NPG - 1, :], A3[:, 0:NPG - 1, :], src[:, 1:NPG, :], ALU.subtract, [kZ], [kZ])
                Wg = sbt(s2, "Wg", [128, NPG, SBH], F32)
                ACT(Wg[:].rearrange("p g h -> p (g h)"), ARG[:], AF.Exp, [kZ], [kZ])
                vpg = [sbt(s2, "vpg%d" % i, [128, 1024], F32) for i in range(2)]
                kvpg = [Tk(), Tk()]
                for pg in range(NPG):
                    i = pg % 2
                    P.dma("pool", lambda e, i=i, pg=pg: e.indirect_dma_start(
                        out=vpg[i][:], out_offset=None, in_=cflat_v,
                        in_offset=bass.IndirectOffsetOnAxis(ap=idx[:, pg:pg + 1], axis=0)),
                        reads=[k0], writes=[kvpg[i]])
                    if pg == 0:
                        MM(pb[3][:, 0:SBH], zerob[:], zerob[:, 0:SBH], True, False, [k_c], [kb[3]])
                    for h in range(SBH):
                        MM(pb[3][:, h:h + 1], vpg[i][:, h * 128:(h + 1) * 128], Wg[:, pg, h:h + 1],
                           False, pg == NPG - 1 and h == SBH - 1, [kvpg[i], kZ], [kb[3]])
                gsl = sbt(s2, "gsl", [128, SBH], F32)
                ACT(gsl[:], HS[:, 48:56], AF.Silu, [k_hs], [k0])
                TT("dve", YS[:, 8:16, :].rearrange("p h o -> p (h o)"), pb[3][:, 0:SBH], gsl[:], ALU.mult,
                   [kb[3], k0], [k_ys])
                P.barrier()

        def phase_C(l):
            j = l // 2
            with contextlib.ExitStack() as s2:
                cw = sbt(s2, "ccw", [128, 64, 4], F32)
                kcw = Tk()
                DMA("sp", cw[:], conv_w_cT[j], writes=[kcw])
                PRE = [sbt(s2, "PRE%d" % i, [128, 4 + T], F32) for i in range(2)]
                kpre = [Tk(), Tk()]
                acc = sbt(s2, "cacc", [128, T], F32)
                kacc = Tk()
                sq = sbt(s2, "csq", [128, T], BF16)
                ksq = Tk()
                rs = [sbt(s2, "crs%d" % i, [128, 512], F32) for i in range(2)]
                krs = [Tk(), Tk()]
                ob = [sbt(s2, "cob%d" % i, [128, T], BF16) for i in range(2)]
                kob = [Tk(), Tk()]
                for i in range(2):
                    MEMSET("pool", PRE[i][:, 0:4], 0.0, [kpre[i]])
                nr = 0
                for cc in range(64):
                    i = cc % 2
                    r0 = cc * 128
                    DMA("sp", PRE[i][:, 4:4 + T], HT[r0:r0 + 128, :], writes=[kpre[i]])
                    DMA("pool", cc_pT[j, r0:r0 + 128, :], PRE[i][:, 1 + T:4 + T], reads=[kpre[i]])
                    TS("dve", acc[:], PRE[i][:, 1:1 + T], cw[:, cc, 0:1], None, ALU.mult, None, [kpre[i], kcw], [kacc])
                    for kk in range(1, 4):
                        STT("dve", acc[:], PRE[i][:, 1 + kk:1 + kk + T], cw[:, cc, kk:kk + 1], acc[:], ALU.mult,
                            ALU.add, [kpre[i], kcw, kacc], [kacc])
                    ACT(acc[:], acc[:], AF.Silu, [kacc], [kacc])
                    if cc < 32:
                        ACT(sq[:], acc[:], AF.Square, [kacc], [ksq])
                        for t in range(NTT):
                            b = nr % 4
                            q = nr % 2
                            nr += 1
                            MM(pb[b][:, :], ones1[:], sq[:, t * 512:(t + 1) * 512], True, True, [ksq, k_c], [kb[b]])
                            ACT(rs[q][:], pb[b][:, :], AF.Sqrt, [kb[b]], [krs[q]], bias=epsc[:, 1:2])
                            P.op("dve", lambda e, q=q: e.reciprocal(out=rs[q][:], in_=rs[q][:]), [krs[q]], [krs[q]])
                            if cc < 16:
                                STT("dve", ob[i][:, t * 512:(t + 1) * 512], acc[:, t * 512:(t + 1) * 512],
                                    float(HD ** -0.5), rs[q][:], ALU.mult, ALU.mult, [kacc, krs[q]], [kob[i]])
                            else:
                                TT("dve", ob[i][:, t * 512:(t + 1) * 512], acc[:, t * 512:(t + 1) * 512], rs[q][:],
                                   ALU.mult, [kacc, krs[q]], [kob[i]])
                    else:
                        CP("pool", ob[i][:], acc[:], [kacc], [kob[i]])
                    DMA("pool", QKVn[r0:r0 + 128, :], ob[i][:], reads=[kob[i]])
                FS = sbt(s2, "FS", [128, 64, 4], F32)
                kfs = Tk()
                DMA("sp", FS[:, :, 0:3], st_conv_cT[j].rearrange("(c p) k -> p c k", p=128), writes=[kfs])
                CP("dve", FS[:, :, 3], HS[:, 0:64], [k_hs], [kfs])
                DMA("pool", cc_sT[j].rearrange("(c p) k -> p c k", p=128), FS[:, :, 1:4], reads=[kfs])
                pr = sbt(s2, "cpr", [128, 64, 4], F32)
                TT("dve", pr[:], FS[:], cw[:], ALU.mult, [kfs, kcw], [kfs])
                P.op("dve", lambda e: e.tensor_reduce(out=SN[:, 0:64], in_=pr[:], axis=mybir.AxisListType.X,
                                                      op=ALU.add), [kfs], [k_sn])
                ACT(SN[:, 0:64], SN[:, 0:64], AF.Silu, [k_sn], [k_sn])
                sqs = sbt(s2, "csqs", [128, 32], F32)
                ACT(sqs[:], SN[:, 0:32], AF.Square, [k_sn], [kfs])
                MM(pb[5][:, 0:32], cpk[:, 1792:1920], sqs[:], True, True, [kfs, k_c], [kb[5]])
                ACT(sqs[:], pb[5][:, 0:32], AF.Sqrt, [kb[5]], [kfs], bias=epsc[:, 1:2])
                P.op("dve", lambda e: e.reciprocal(out=sqs[:], in_=sqs[:]), [kfs], [kfs])
                TT("dve", SN[:, 0:32], SN[:, 0:32], sqs[:], ALU.mult, [kfs, k_sn], [k_sn])
                TS("dve", SN[:, 0:16], SN[:, 0:16], float(HD ** -0.5), None, ALU.mult, None, [k_sn], [k_sn])
                P.barrier()

        def phase_G(l):
            j = l // 2
            NCH = T // 128
            NCS = NCH + 1
            with contextlib.ExitStack() as s2:
                BETA = sbt(s2, "BETA", [128, NCS, GV], F32)
                GC = sbt(s2, "GC", [128, NCS, GV], F32)
                EGC = sbt(s2, "EGC", [128, NCS, GV], F32)
                BE = sbt(s2, "BE", [128, NCS, GV], F32)
                EKD = sbt(s2, "EKD", [128, NCS, GV], F32)
                EGL = sbt(s2, "EGL", [128, NCS, GV], F32)
                GCT = sbt(s2, "GCT", [32, NCS, 128], F32)
                nGCT = sbt(s2, "nGCT", [32, NCS, 128], F32)
                SEL = sbt(s2, "SEL", [32, GV, 128], F32)
                gw = sbt(s2, "gw", [128, 1], F32)
                ktab = Tk()
                DMA("sp", SEL[:], selpack[:, :, :], writes=[ktab])
                DMA("sp", gw[:], gnorm_wT[j], writes=[ktab])
                with contextlib.ExitStack() as s3:
                    BA = sbt(s3, "BA", [64, T + 128], F32)
                    kba = Tk()
                    nea = sbt(s3, "nea", [128, GV], F32)
                    dtb = sbt(s3, "dtb", [128, GV], F32)
                    kq = Tk()
                    DMA("sp", BA[:, 0:T], HT[12288:12352, :], writes=[kba])
                    MEMSET("pool", BA[:, T:T + 128], 0.0, [kba])
                    CP("dve", BA[:, T:T + 1], HS[0:64, 96:97], [k_hs], [kba])
                    DMA("sp", nea[:], a_logB[j], writes=[kq])
                    DMA("sp", dtb[:], dt_biasB[j], writes=[kq])
                    ACT(nea[:], nea[:], AF.Exp, [kq], [kq])
                    TS("dve", nea[:], nea[:], -1.0, None, ALU.mult, None, [kq], [kq])
                    tmp = [sbt(s3, "g0t%d" % i, [128, GV], F32) for i in range(2)]
                    g32 = [sbt(s3, "g0g%d" % i, [128, GV], F32) for i in range(2)]
                    gl = [sbt(s3, "g0l%d" % i, [128, GV], F32) for i in range(2)]
                    kt = [Tk(), Tk()]
                    for n in range(NCS):
                        q = n % 2
                        MM(pb[0 + q][:, 0:64], BA[:, n * 128:(n + 1) * 128], identf[0:64, 0:64], True, True,
                           [kba, k_c], [kb[0 + q]])
                        ACT(BETA[:, n, :], pb[0 + q][:, 0:32], AF.Sigmoid, [kb[0 + q]], [ktab])
                        CP("act", tmp[q][:], pb[0 + q][:, 32:64], [kb[0 + q]], [kt[q]])
                        TT("dve", tmp[q][:], tmp[q][:], dtb[:], ALU.add, [kt[q], kq], [kt[q]])
                        ACT(tmp[q][:], tmp[q][:], AF.Exp, [kt[q]], [kt[q]])
                        ACT(tmp[q][:], tmp[q][:], AF.Ln, [kt[q]], [kt[q]], bias=1.0)
                        TT("dve", g32[q][:], tmp[q][:], nea[:], ALU.mult, [kt[q], kq], [kt[q]])
                        if n == NCH:
                            TS("dve", g32[q][:], g32[q][:], identf[:, 0:1], None, ALU.mult, None, [kt[q], k_c],
                               [kt[q]])
                        MM(pb[2 + q][:, 0:32], trif, g32[q][:], True, True, [kt[q], k_c], [kb[2 + q]])
                        CP("act", GC[:, n, :], pb[2 + q][:, 0:32], [kb[2 + q]], [ktab])
                        MM(pb[4 + q][0:32, 0:128], g32[q][:], trif, True, True, [kt[q], k_c], [kb[4 + q]])
                        CP("act", GCT[:, n, :], pb[4 + q][0:32, 0:128], [kb[4 + q]], [ktab])
                        TS("dve", nGCT[:, n, :], GCT[:, n, :], -1.0, None, ALU.mult, None, [ktab], [ktab])
                        MM(pb[6 + q][:, 0:32], sellast, GC[:, n, :], True, True, [ktab, k_c], [kb[6 + q]])
                        ACT(EGL[:, n, :], pb[6 + q][:, 0:32], AF.Exp, [kb[6 + q]], [ktab])
                        CP("act", gl[q][:], pb[6 + q][:, 0:32], [kb[6 + q]], [kt[q]])
                        TT("dve", gl[q][:], gl[q][:], GC[:, n, :], ALU.subtract, [kt[q], ktab], [kt[q]])
                        ACT(EKD[:, n, :], gl[q][:], AF.Exp, [kt[q]], [ktab])
                        ACT(EGC[:, n, :], GC[:, n, :], AF.Exp, [ktab], [ktab])
                        TT("dve", BE[:, n, :], BETA[:, n, :], EGC[:, n, :], ALU.mult, [ktab], [ktab])
                    P.barrier()
                qT = sbt(s2, "gqT", [128, T], BF16)
                kT = sbt(s2, "gkT", [128, T], BF16)
                vT = sbt(s2, "gvT", [128, 2, T], BF16)
                yTh = sbt(s2, "gyT", [128, 2, T], BF16)
                zB = [sbt(s2, "gzB%d" % i, [128, 2, 256], F32) for i in range(2)]
                kzB = [Tk(), Tk()]
                kin = Tk()
                kyT = Tk()
                sqT = sbt(s2, "sqT", [128, 128], BF16)
                skT = sbt(s2, "skT", [128, 128], BF16)
                svT = sbt(s2, "svT", [128, 2, 128], BF16)
                szT = sbt(s2, "szT", [128, 2, 128], F32)
                syT = sbt(s2, "syT", [128, 2, 128], BF16)
                ksin = Tk()
                ksy = Tk()
                S32 = sbt(s2, "S32", [128, 2, 128], F32)
                Sb = sbt(s2, "Sb", [128, 2, 128], BF16)
                kS = Tk()
                kSb = Tk()

                class TL:
                    def __init__(self, name, shape, dt, n=1):
                        self.t = [sbt(s2, "%s%d" % (name, i), list(shape), dt) for i in range(n)]
                        self.k = [Tk() for _ in range(n)]

                def fl(t, w):
                    return t[:].rearrange("p m d -> p (m d)")[:, 0:w]

                msk4 = sbt(s2, "msk4", [128, 7, 512], BF16)
                id4 = sbt(s2, "id4", [128, 512], BF16)
                up4 = sbt(s2, "up4", [128, 512], BF16)
                low2 = sbt(s2, "low2", [128, 256], F32)
                with contextlib.ExitStack() as s3:
                    mskf = sbt(s3, "mskf", [128, 7 * 128], F32)
                    DMA("sp", mskf[:], cpack2[:, :], writes=[ktab])
                    for r in range(4):
                        CP("dve", msk4[:, :, r * 128:(r + 1) * 128], mskf[:].rearrange("p (a b) -> p a b", b=128),
                           [ktab], [ktab])
                        CP("dve", id4[:, r * 128:(r + 1) * 128], identb[:], [k_c], [ktab])
                        CP("dve", up4[:, r * 128:(r + 1) * 128], upbig[:], [k_c], [ktab])
                    for r in range(2):
                        CP("dve", low2[:, r * 128:(r + 1) * 128], lowstrict[:], [k_c], [ktab])
                    P.barrier()
                ktok2 = TL("ktok2", [128, 2, 128], F32)
                KKs2 = TL("KKs2", [128, 2, 128], F32)
                QKs2 = TL("QKs2", [128, 2, 128], F32)
                bv4 = TL("bv4", [128, 4, 128], BF16, 2)
                kbg4 = TL("kbg4", [128, 4, 128], BF16)
                kdec4 = TL("kdec4", [128, 4, 128], BF16, 2)
                dec4 = TL("dec4", [128, 4, 128], F32)
                L4 = TL("L4", [128, 4, 128], BF16)
                A4 = TL("A4", [128, 4, 128], BF16)
                Mf4 = TL("Mf4", [128, 4, 128], BF16)
                AT4 = TL("AT4", [128, 4, 128], BF16, 2)
                nwT4 = TL("nwT4", [128, 4, 128], BF16, 2)
                Lp = TL("Lp", [128, 4, 128], BF16, 2)
                Mp = TL("Mp", [128, 4, 128], BF16, 2)
                Tn = TL("Tn", [128, 4, 128], BF16, 2)
                Tt = TL("Tt", [128, 4, 128], BF16, 4)
                Cs = [TL("Cs%d" % i, [128, 4, 128], BF16) for i in range(3)]
                Cts = [TL("Cts%d" % i, [128, 4, 128], BF16) for i in range(3)]
                IL = TL("IL", [128, 4, 128], BF16)
                IM = TL("IM", [128, 4, 128], BF16)
                Xs = TL("Xs", [128, 4, 128], BF16)
                X2s = TL("X2s", [128, 4, 128], BF16)
                vnb2 = TL("vnb2", [128, 2, 128], BF16)
                qS2 = TL("qS2", [128, 2, 128], F32)
                o2 = TL("o2", [128, 2, 128], F32)
                junk2 = TL("junk2", [128, 2, 128], F32)
                onb2 = TL("onb2", [128, 2, 128], BF16)
                ss2 = TL("ss2", [128, 2], F32)
                zg2 = TL("zg2", [128, 2, 128], F32)
                bank = [0]

                def nb():
                    bank[0] = (bank[0] + 1) % 8
                    return bank[0]

                def bc_d(tab, n, hv0):
                    return tab[:, n, hv0:hv0 + 2].unsqueeze(2).to_broadcast([128, 2, 128])

                def bc_v(t3, ci):
                    return t3[:, ci:ci + 1, :].to_broadcast([128, 2, 128])

                def mm4(dst_bank, nm, lhs_fn, rhs_fn, reads):
                    for m in range(nm):
                        MM(pb[dst_bank][:, m * 128:(m + 1) * 128], lhs_fn(m), rhs_fn(m), True, True, reads,
                           [kb[dst_bank]])

                ttc = [0]

                def stage1(bp, hq, cbs, kI):
                    hv0 = 2 * hq
                    nc_ = len(cbs)
                    nm = 2 * nc_
                    W = nm * 128
                    Wk = nc_ * 128
                    b = nb()
                    for ci, (n, qc, kc, vc) in enumerate(cbs):
                        MM(pb[b][:, ci * 128:(ci + 1) * 128], kc, identb[:], True, True, [kI, k_c], [kb[b]])
                    CP("act", fl(ktok2.t[0], Wk), pb[b][:, 0:Wk], [kb[b]], [ktok2.k[0]])
                    yield
                    b = nb()
                    for ci, (n, qc, kc, vc) in enumerate(cbs):
                        MM(pb[b][:, ci * 128:(ci + 1) * 128], kc, kc, True, True, [kI], [kb[b]])
                    TT("dve", fl(KKs2.t[0], Wk), pb[b][:, 0:Wk], low2[:, 0:Wk], ALU.mult, [kb[b], ktab], [KKs2.k[0]])
                    yield
                    b = nb()
                    for ci, (n, qc, kc, vc) in enumerate(cbs):
                        MM(pb[b][:, ci * 128:(ci + 1) * 128], qc, kc, True, True, [kI], [kb[b]])
                    CP("act", fl(QKs2.t[0], Wk), pb[b][:, 0:Wk], [kb[b]], [QKs2.k[0]])
                    yield
                    b = nb()
                    for ci, (n, qc, kc, vc) in enumerate(cbs):
                        for vh in range(2):
                            m = 2 * ci + vh
                            MM(pb[b][:, m * 128:(m + 1) * 128], vc(vh), identb[:], True, True, [kI, k_c], [kb[b]])
                    for ci, (n, qc, kc, vc) in enumerate(cbs):
                        pv = pb[b][:, 2 * ci * 128:(2 * ci + 2) * 128].rearrange("p (v d) -> p v d", d=128)
                        TT("dve", bv4.t[bp][:, 2 * ci:2 * ci + 2, :], pv, bc_d(BETA, n, hv0), ALU.mult,
                           [kb[b], ktab], [bv4.k[bp]])
                        TT("pool", kbg4.t[0][:, 2 * ci:2 * ci + 2, :], bc_v(ktok2.t[0], ci), bc_d(BE, n, hv0),
                           ALU.mult, [ktok2.k[0], ktab], [kbg4.k[0]])
                        TT("pool", kdec4.t[bp][:, 2 * ci:2 * ci + 2, :], bc_v(ktok2.t[0], ci), bc_d(EKD, n, hv0),
                           ALU.mult, [ktok2.k[0], ktab], [kdec4.k[bp]])
                    b = nb()
                    MM(pb[b][:, 0:W], identb[:], up4[:, 0:W], True, False, [k_c, ktab], [kb[b]])
                    for ci, (n, qc, kc, vc) in enumerate(cbs):
                        for vh in range(2):
                            m = 2 * ci + vh
                            hv = hv0 + vh
                            MM(pb[b][:, m * 128:(m + 1) * 128], SEL[:, hv, :], GCT[:, n, :], False, False, [ktab],
                               [kb[b]])
                            MM(pb[b][:, m * 128:(m + 1) * 128], nGCT[:, n, :], SEL[:, hv, :], False,
                               m == nm - 1, [ktab], [kb[b]])
                    ACT(fl(dec4.t[0], W), pb[b][:, 0:W], AF.Exp, [kb[b]], [dec4.k[0]], scale=-1.0)
                    yield
                    for ci, (n, qc, kc, vc) in enumerate(cbs):
                        sl = slice(2 * ci, 2 * ci + 2)
                        TT("dve", L4.t[0][:, sl, :], dec4.t[0][:, sl, :], bc_d(BETA, n, hv0), ALU.mult,
                           [dec4.k[0], ktab], [L4.k[0]])
                        TT("dve", L4.t[0][:, sl, :], L4.t[0][:, sl, :], bc_v(KKs2.t[0], ci), ALU.mult,
                           [KKs2.k[0], L4.k[0]], [L4.k[0]])
                        TT("pool", A4.t[0][:, sl, :], dec4.t[0][:, sl, :], bc_v(QKs2.t[0], ci), ALU.mult,
                           [dec4.k[0], QKs2.k[0]], [A4.k[0]])
                    b = nb()
                    mm4(b, nm, lambda m: L4.t[0][:, m, :], lambda m: identb[:], [L4.k[0], k_c])
                    CP("act", fl(Mf4.t[0], W), pb[b][:, 0:W], [kb[b]], [Mf4.k[0]])
                    yield
                    b = nb()
                    mm4(b, nm, lambda m: A4.t[0][:, m, :], lambda m: identb[:], [A4.k[0], k_c])
                    CP("act", fl(AT4.t[bp], W), pb[b][:, 0:W], [kb[b]], [AT4.k[bp]])
                    yield
                    mk_ = lambda i: msk4[:, i, 0:W]
                    TT("pool", fl(Lp.t[0], W), fl(L4.t[0], W), mk_(0), ALU.mult, [L4.k[0], ktab], [Lp.k[0]])
                    TT("pool", fl(Mp.t[0], W), fl(Mf4.t[0], W), mk_(0), ALU.mult, [Mf4.k[0], ktab], [Mp.k[0]])
                    for si in range(3):
                        TT("pool", fl(Cs[si].t[0], W), fl(L4.t[0], W), mk_(1 + si), ALU.mult, [L4.k[0], ktab],
                           [Cs[si].k[0]])
                        TT("pool", fl(Cts[si].t[0], W), fl(Mf4.t[0], W), mk_(4 + si), ALU.mult, [Mf4.k[0], ktab],
                           [Cts[si].k[0]])
                    tb0 = 2 * bp
                    TT("pool", fl(Tn.t[0], W), id4[:, 0:W], fl(Lp.t[0], W), ALU.subtract, [Lp.k[0], ktab], [Tn.k[0]])
                    TT("pool", fl(Tt.t[tb0], W), id4[:, 0:W], fl(Mp.t[0], W), ALU.subtract, [Mp.k[0], ktab],
                       [Tt.k[tb0]])
                    cl, cm, ct_, cn = 0, 0, 0, 0
                    for lev in range(3):
                        nl, nm_ = 1 - cl, 1 - cm
                        bl = nb()
                        mm4(bl, nm, lambda m: Mp.t[cm][:, m, :], lambda m: Lp.t[cl][:, m, :], [Lp.k[cl], Mp.k[cm]])
                        bm = nb()
                        mm4(bm, nm, lambda m: Lp.t[cl][:, m, :], lambda m: Mp.t[cm][:, m, :], [Lp.k[cl], Mp.k[cm]])
                        CP("act", fl(Lp.t[nl], W), pb[bl][:, 0:W], [kb[bl]], [Lp.k[nl]])
                        CP("dve", fl(Mp.t[nm_], W), pb[bm][:, 0:W], [kb[bm]], [Mp.k[nm_]])
                        yield
                        TT("pool", fl(IL.t[0], W), fl(Lp.t[nl], W), id4[:, 0:W], ALU.add, [Lp.k[nl], ktab], [IL.k[0]])
                        TT("pool", fl(IM.t[0], W), fl(Mp.t[nm_], W), id4[:, 0:W], ALU.add, [Mp.k[nm_], ktab],
                           [IM.k[0]])
                        nt, nn = 1 - ct_, 1 - cn
                        bt_ = nb()
                        mm4(bt_, nm, lambda m: IL.t[0][:, m, :], lambda m: Tt.t[tb0 + ct_][:, m, :],
                            [IL.k[0], Tt.k[tb0 + ct_]])
                        CP("act", fl(Tt.t[tb0 + nt], W), pb[bt_][:, 0:W], [kb[bt_]], [Tt.k[tb0 + nt]])
                        bn_ = nb()
                        mm4(bn_, nm, lambda m: IM.t[0][:, m, :], lambda m: Tn.t[cn][:, m, :], [IM.k[0], Tn.k[cn]])
                        CP("dve", fl(Tn.t[nn], W), pb[bn_][:, 0:W], [kb[bn_]], [Tn.k[nn]])
                        yield
                        cl, cm, ct_, cn = nl, nm_, nt, nn
                    for si in range(3):
                        nt, nn = 1 - ct_, 1 - cn
                        bx2 = nb()
                        mm4(bx2, nm, lambda m: Cs[si].t[0][:, m, :], lambda m: Tt.t[tb0 + ct_][:, m, :],
                            [Cs[si].k[0], Tt.k[tb0 + ct_]])
                        CP("dve", fl(X2s.t[0], W), pb[bx2][:, 0:W], [kb[bx2]], [X2s.k[0]])
                        yield
                        if si < 2:
                            bx = nb()
                            mm4(bx, nm, lambda m: Cts[si].t[0][:, m, :], lambda m: Tn.t[cn][:, m, :],
                                [Cts[si].k[0], Tn.k[cn]])
                            CP("act", fl(Xs.t[0], W), pb[bx][:, 0:W], [kb[bx]], [Xs.k[0]])
                        by2 = nb()
                        mm4(by2, nm, lambda m: Tn.t[cn][:, m, :], lambda m: X2s.t[0][:, m, :], [Tn.k[cn], X2s.k[0]])
                        TT("dve", fl(Tt.t[tb0 + nt], W), fl(Tt.t[tb0 + ct_], W), pb[by2][:, 0:W], ALU.subtract,
                           [Tt.k[tb0 + ct_], kb[by2]], [Tt.k[tb0 + nt]])
                        yield
                        if si < 2:
                            by = nb()
                            mm4(by, nm, lambda m: Tt.t[tb0 + ct_][:, m, :], lambda m: Xs.t[0][:, m, :],
                                [Tt.k[tb0 + ct_], Xs.k[0]])
                            TT("dve", fl(Tn.t[nn], W), fl(Tn.t[cn], W), pb[by][:, 0:W], ALU.subtract,
                               [Tn.k[cn], kb[by]], [Tn.k[nn]])
                            cn = nn
                        ct_ = nt
                    ti = tb0 + ct_
                    b = nb()
                    mm4(b, nm, lambda m: kbg4.t[0][:, m, :], lambda m: Tt.t[ti][:, m, :], [kbg4.k[0], Tt.k[ti]])
                    ACT(fl(nwT4.t[bp], W), pb[b][:, 0:W], AF.Copy, [kb[b]], [nwT4.k[bp]], scale=-1.0)
                    yield

                def stage2(bp, ti, hq, ci, n, qc, zc, yout, kZ, kY, kI):
                    hv0 = 2 * hq
                    ti = 2 * bp
                    m0 = 2 * ci
                    TiT = Tt.t[ti]
                    kTi = Tt.k[ti]
                    b = nb()
                    for vh in range(2):
                        MM(pb[b][:, vh * 128:(vh + 1) * 128], TiT[:, m0 + vh, :], bv4.t[bp][:, m0 + vh, :], True, False,
                           [kTi, bv4.k[bp]], [kb[b]])
                        MM(pb[b][:, vh * 128:(vh + 1) * 128], nwT4.t[bp][:, m0 + vh, :], Sb[:, vh, :], False, True,
                           [nwT4.k[bp], kSb], [kb[b]])
                    CP("dve", fl(vnb2.t[0], 256), pb[b][:, 0:256], [kb[b]], [vnb2.k[0]])
                    yield
                    b = nb()
                    MM(pb[b][:, 0:256], qc, Sb[:].rearrange("p v d -> p (v d)"), True, True, [kI, kSb], [kb[b]])
                    TT("dve", qS2.t[0][:], pb[b][:, 0:256].rearrange("p (v d) -> p v d", d=128), bc_d(EGC, n, hv0),
                       ALU.mult, [kb[b], ktab], [qS2.k[0]])
                    b = nb()
                    for vh in range(2):
                        MM(pb[b][:, vh * 128:(vh + 1) * 128], AT4.t[bp][:, m0 + vh, :], vnb2.t[0][:, vh, :], True, True,
                           [AT4.k[bp], vnb2.k[0]], [kb[b]])
                    TT("dve", fl(o2.t[0], 256), pb[b][:, 0:256], fl(qS2.t[0], 256), ALU.add, [kb[b], qS2.k[0]],
                       [o2.k[0]])
                    yield
                    b = nb()
                    for vh in range(2):
                        MM(pb[b][:, vh * 128:(vh + 1) * 128], kdec4.t[bp][:, m0 + vh, :], vnb2.t[0][:, vh, :], True, True,
                           [kdec4.k[bp], vnb2.k[0]], [kb[b]])
                    TT("dve", S32[:], S32[:], bc_d(EGL, n, hv0), ALU.mult, [kS, kSb, ktab], [kS])
                    TT("dve", S32[:].rearrange("p v d -> p (v d)"), S32[:].rearrange("p v d -> p (v d)"),
                       pb[b][:, 0:256], ALU.add, [kS, kb[b]], [kS])
                    CP("pool", Sb[:], S32[:], [kS], [kSb])
                    yield
                    ACT(junk2.t[0][:], o2.t[0][:], AF.Square, [o2.k[0]], [junk2.k[0]])
                    P.op("dve", lambda e: e.tensor_reduce(out=ss2.t[0][:], in_=junk2.t[0][:],
                                                          axis=mybir.AxisListType.X, op=ALU.add),
                         [junk2.k[0]], [ss2.k[0]])
                    ACT(ss2.t[0][:], ss2.t[0][:], AF.Sqrt, [ss2.k[0]], [ss2.k[0]], bias=epsc[:, 1:2], scale=1.0 / HD)
                    P.op("dve", lambda e: e.reciprocal(out=ss2.t[0][:], in_=ss2.t[0][:]), [ss2.k[0]], [ss2.k[0]])
                    TT("pool", onb2.t[0][:], o2.t[0][:], ss2.t[0][:].unsqueeze(2).to_broadcast([128, 2, 128]), ALU.mult,
                       [o2.k[0], ss2.k[0]], [onb2.k[0]])
                    yield
                    b = nb()
                    for vh in range(2):
                        MM(pb[b][:, vh * 128:(vh + 1) * 128], onb2.t[0][:, vh, :], identb[:], True, True,
                           [onb2.k[0], k_c], [kb[b]])
                    ACT(zg2.t[0][:], zc, AF.Silu, [kZ], [zg2.k[0]])
                    STT("dve", yout, pb[b][:, 0:256].rearrange("p (v d) -> p v d", d=128), gw[:, 0:1], zg2.t[0][:],
                        ALU.mult, ALU.mult, [kb[b], ktab, zg2.k[0]], [kY])
                    yield

                nbatch = [0]

                def run_all(g):
                    for _ in g:
                        pass

                def interleave(gs):
                    gs = [g for g in gs if g is not None]
                    alive = [True] * len(gs)
                    while any(alive):
                        for i_, g in enumerate(gs):
                            if alive[i_]:
                                try:
                                    next(g)
                                except StopIteration:
                                    alive[i_] = False

                def chain(*gens):
                    for g in gens:
                        yield from g

                for hq in range(GQ):
                    DMA("sp", qT[:], QKVn[hq * 128:(hq + 1) * 128, :], writes=[kin])
                    DMA("sp", kT[:], QKVn[2048 + hq * 128:2048 + (hq + 1) * 128, :], writes=[kin])
                    DMA("sp", vT[:], QKVn[4096 + 2 * hq * 128:4096 + (2 * hq + 2) * 128, :].rearrange(
                        "(v p) t -> p v t", p=128), writes=[kin])
                    MEMSET("pool", S32[:], 0.0, [kS])
                    MEMSET("pool", Sb[:], 0.0, [kSb])
                    MEMSET("pool", sqT[:], 0.0, [ksin])
                    MEMSET("pool", skT[:], 0.0, [ksin])
                    MEMSET("pool", svT[:], 0.0, [ksin])
                    MEMSET("pool", szT[:], 0.0, [ksin])
                    CP("dve", sqT[:, 0:1], SN[:, hq:hq + 1], [k_sn], [ksin])
                    CP("dve", skT[:, 0:1], SN[:, 16 + hq:17 + hq], [k_sn], [ksin])
                    for vh in range(2):
                        CP("dve", svT[:, vh, 0:1], SN[:, 32 + 2 * hq + vh:33 + 2 * hq + vh], [k_sn], [ksin])
                        CP("dve", szT[:, vh, 0:1], HS[:, 64 + 2 * hq + vh:65 + 2 * hq + vh], [k_hs], [ksin])
                    batches = []
                    for n0 in range(0, NCH, 2):
                        cbs = []
                        for n in range(n0, min(n0 + 2, NCH)):
                            c0 = n * 128
                            cbs.append((n, qT[:, c0:c0 + 128], kT[:, c0:c0 + 128],
                                        (lambda vh, c0=c0: vT[:, vh, c0:c0 + 128])))
                        batches.append((n0, cbs, kin))
                    batches.append((NCH, [(NCH, sqT[:], skT[:], (lambda vh: svT[:, vh, :]))], ksin))

                    def start1(bi):
                        n0, cbs, kI = batches[bi]
                        bp = (nbatch[0] + bi) % 2
                        if bi < len(batches) - 1:
                            wz = len(cbs) * 128
                            DMA("sp", zB[bp][:, :, 0:wz],
                                HT[8192 + 2 * hq * 128:8192 + (2 * hq + 2) * 128, n0 * 128:n0 * 128 + wz].rearrange(
                                    "(v p) t -> p v t", p=128), writes=[kzB[bp]])
                        return stage1(bp, hq, cbs, kI)

                    def make2(bi):
                        n0, cbs, kI = batches[bi]
                        bp = (nbatch[0] + bi) % 2
                        gs = []
                        if bi < len(batches) - 1:
                            for ci, (n, qc, kc, vc) in enumerate(cbs):
                                c0 = n * 128
                                gs.append(stage2(bp, 0, hq, ci, n, qc, zB[bp][:, :, ci * 128:(ci + 1) * 128],
                                                 yTh[:, :, c0:c0 + 128], kzB[bp], kyT, kI))
                        else:
                            gs.append(stage2(bp, 0, hq, 0, NCH, sqT[:], szT[:], syT[:], ksin, ksy, ksin))
                        return chain(*gs)

                    run_all(start1(0))
                    for bi in range(len(batches)):
                        g1 = start1(bi + 1) if bi + 1 < len(batches) else None
                        if bi == len(batches) - 1:
                            DMA("pool", YT[2 * hq * 128:(2 * hq + 2) * 128, :].rearrange("(v p) t -> p v t", p=128),
                                yTh[:], reads=[kyT])
                            DMA("pool", dl_p[j, 2 * hq:2 * hq + 2].rearrange("v k d -> k v d"), S32[:], reads=[kS])
                            DMA("sp", S32[:], st_delta[j, 2 * hq:2 * hq + 2].rearrange("v k d -> k v d"), writes=[kS])
                            CP("pool", Sb[:], S32[:], [kS], [kSb])
                        interleave([g1, make2(bi)])
                    nbatch[0] += len(batches)
                    CP("dve", YS[:, 2 * hq:2 * hq + 2, :].rearrange("p v o -> p (v o)"), syT[:, :, 0], [ksy], [k_ys])
                    DMA("pool", dl_s[j, 2 * hq:2 * hq + 2].rearrange("v k d -> k v d"), S32[:], reads=[kS])
                P.barrier()

        def phase_GDN(l):
            import os
            KG = os.environ.get("KG", "c,g").split(",")
            if "c" in KG:
                phase_C(l)
            if "g" in KG:
                phase_G(l)

        def phase_O(l):
            even = (l % 2 == 0)
            j = l // 2
            KY = 16 if even else 32
            w_out = w_out_even[j] if even else w_out_odd[j]
            TO = 256 if even else 128
            last = (l == DEPTH - 1)
            Xsrc = xT_in if l == 0 else X
            Xdst = yT_out if last else X
            with contextlib.ExitStack() as s2:
                Wo = sbt(s2, "Wo", [128, KY, D], BF16)
                kWo = Tk()
                wst = [sbt(s2, "ow%d" % i, [128, D], F32) for i in range(2)]
                kws = [Tk(), Tk()]
                lg = sbt(s2, "lg", [128, KC], F32)
                lb = sbt(s2, "lb", [128, KC], F32)
                klg = Tk()
                DMA("sp", lg[:], ln_gT[l], writes=[klg])
                DMA("sp", lb[:], ln_bT[l], writes=[klg])
                for k in range(KY):
                    DMA("sp", wst[k % 2][:], w_out[k * 128:(k + 1) * 128, :], writes=[kws[k % 2]])
                    CP("pool" if k % 2 else "dve", Wo[:, k, :], wst[k % 2][:], [kws[k % 2]], [kWo])
                Yt = [sbt(s2, "Yt%d" % i, [128, KY, TO], BF16) for i in range(2)]
                kYt = [Tk(), Tk()]
                Xt = [sbt(s2, "Xt%d" % i, [128, KC, TO], F32) for i in range(2)]
                kXt = [Tk(), Tk()]
                rb = sbt(s2, "rb", [128, KC, TO], BF16)
                rsq = sbt(s2, "rsq", [128, KC, TO], BF16)
                krb = Tk()
                mean = sbt(s2, "mean", [128, TO], F32)
                rstd = sbt(s2, "rstd", [128, TO], F32)
                kst = Tk()
                tmp = [sbt(s2, "otmp%d" % i, [128, TO], F32) for i in range(2)]
                ktmp = [Tk(), Tk()]
                ntile = T // TO
                br = 0
                for ti in range(ntile + 1):
                    samp = (ti == ntile)
                    N = 1 if samp else TO
                    i = ti % 2
                    r = 1 if samp else 0
                    if samp:
                        yt_ap = YS[:, 0:KY, :]
                        kyt = k_ys
                        xt_t = XS
                        kxt = k_xs
                    else:
                        t0 = ti * TO
                        DMA("sp", Yt[i][:], YT[0:KY * 128, t0:t0 + TO].rearrange("(k p) t -> p k t", p=128),
                            writes=[kYt[i]])
                        DMA("sp", Xt[i][:], Xsrc[:, t0:t0 + TO].rearrange("(k p) t -> p k t", p=128),
                            writes=[kXt[i]])
                        yt_ap = Yt[i][:]
                        kyt = kYt[i]
                        xt_t = Xt[i]
                        kxt = kXt[i]
                    for d in range(KC):
                        b = br % 4
                        br += 1
                        for k in range(KY):
                            MM(pb[b][:, 0:N], Wo[:, k, d * 128:(d + 1) * 128], yt_ap[:, k, :], k == 0, k == KY - 1,
                               [kWo, kyt], [kb[b]])
                        ACT(xt_t[:, d, :], xt_t[:, d, :], AF.Copy, [kxt], [kxt], scale=float(ALPHA))
                        STT("dve", xt_t[:, d, :], pb[b][:, 0:N], modT[:, 32 + d, r:r + 1], xt_t[:, d, :],
                            ALU.mult, ALU.add, [kb[b], k_mod, kxt], [kxt])
                        CP("pool", rb[:, d, 0:N], xt_t[:, d, :], [kxt], [krb])
                        ACT(rsq[:, d, 0:N], xt_t[:, d, :], AF.Square, [kxt], [krb])
                    for d in range(KC):
                        MM(pb[4][:, 0:N], onesD[:], rb[:, d, 0:N], d == 0, d == KC - 1, [krb, k_c], [kb[4]])
                    for d in range(KC):
                        MM(pb[5][:, 0:N], onesD[:], rsq[:, d, 0:N], d == 0, d == KC - 1, [krb, k_c], [kb[5]])
                    CP("act", mean[:, 0:N], pb[4][:, 0:N], [kb[4]], [kst])
                    ACT(rstd[:, 0:N], pb[4][:, 0:N], AF.Square, [kb[4]], [kst])
                    TT("dve", rstd[:, 0:N], pb[5][:, 0:N], rstd[:, 0:N], ALU.subtract, [kb[5], kst], [kst])
                    ACT(rstd[:, 0:N], rstd[:, 0:N], AF.Sqrt, [kst], [kst], bias=epsc[:, 0:1])
                    P.op("dve", lambda e: e.reciprocal(out=rstd[:, 0:N], in_=rstd[:, 0:N]), [kst], [kst])
                    for d in range(KC):
                        q = d % 2
                        TT("pool", tmp[q][:, 0:N], xt_t[:, d, :], mean[:, 0:N], ALU.subtract, [kxt, kst], [ktmp[q]])
                        TT("dve", tmp[q][:, 0:N], tmp[q][:, 0:N], rstd[:, 0:N], ALU.mult, [ktmp[q], kst], [ktmp[q]])
                        ACT(xt_t[:, d, :], tmp[q][:, 0:N], AF.Identity, [ktmp[q], klg], [kxt],
                            bias=lb[:, d:d + 1], scale=lg[:, d:d + 1])
                    if samp:
                        if last:
                            DMA("pool", ysT_out[:, :, :], XS[:], reads=[k_xs])
                    else:
                        DMA("pool", Xdst[:, t0:t0 + TO].rearrange("(k p) t -> p k t", p=128), Xt[i][:],
                            reads=[kXt[i]])
                P.barrier()

        import os
        PH = os.environ.get("KPH", "M,UP,A,B,Bs,G,O").split(",")
        for l in range(DEPTH):
            if "M" in PH:
                phase_M(l)
            if "UP" in PH:
                phase_UP(l)
            if l % 2 == 0:
                if "A" in PH:
                    phase_A(l)
                if "B" in PH:
                    phase_B(l)
                if "Bs" in PH:
                    phase_Bs(l)
            else:
                if "G" in PH:
                    phase_GDN(l)
            if "O" in PH:
                phase_O(l)
        P.barrier()
    return nc


def make_consts():
    cp = np.zeros((128, 2048), np.float32)
    m = np.arange(128)[:, None]
    jj = np.arange(128)[None, :]
    cp[:, 0:128] = np.eye(128, dtype=np.float32)
    cp[:, 128:256] = np.where(m >= jj, -1.0, 0.0)
    i9 = np.arange(896)[None, :]
    cp[:, 256:1152] = np.where(m >= (i9 - 384), NEG, 0.0)
    cp[:, 1152:1280] = np.where(m <= jj, 1.0, 0.0)
    cp[:, 1280:1408] = np.where(jj > m, -NEG, 0.0)
    cp[:, 1408:1536] = np.where(m == 127, 1.0, 0.0)
    cp[:, 1536:1664] = np.where(m > jj, 1.0, 0.0)
    cp[:, 1664] = np.arange(128)
    cp[:, 1792:1920] = 1.0
    cp[:, 1920:2048] = np.where(m >= jj, 1.0, 0.0)
    cp2 = np.zeros((128, 7, 128), np.float32)
    cp2[:, 0] = (m // 16 == jj // 16)
    for si, sz in enumerate((16, 32, 64)):
        off = ((m // (2 * sz) == jj // (2 * sz)) & (m % (2 * sz) >= sz) & (jj % (2 * sz) < sz)).astype(np.float32)
        cp2[:, 1 + si] = off
        cp2[:, 4 + si] = off.T
    global CP2
    CP2 = cp2.reshape(128, 7 * 128)
    sel = np.zeros((32, GV, 128), np.float32)
    for h in range(GV):
        sel[h, h, :] = 1.0
    return cp, sel


def fm(v):
    n = v.shape[-1] // 128
    return np.ascontiguousarray(np.moveaxis(v.reshape(v.shape[:-1] + (n, 128)), -1, -2))


_CACHE = {}


def kernel(x_prompt, x_sample, c_prompt, c_sample, cache_k, cache_v, page_table,
           state_conv_a, state_conv_c, state_delta, w_ada, b_ada, ln_g, ln_b,
           w_in_even, conv_w_a, gn_g_a, gn_b_a, sb_bias, w_out_even,
           w_in_odd, conv_w_c, a_log_c, dt_bias_c, gnorm_w_c, w_out_odd):
    A = lambda v: np.ascontiguousarray(np.asarray(v))
    x_prompt = A(x_prompt); x_sample = A(x_sample); c_prompt = A(c_prompt); c_sample = A(c_sample)
    B, T, _ = x_prompt.shape
    NS = x_sample.shape[0]
    DEPTH = w_ada.shape[0]
    NE = (DEPTH + 1) // 2
    NO = DEPTH // 2
    NPOOL = cache_k.shape[1]
    NPG = page_table.shape[1]
    ncores = 8
    key = (T, NPG, NPOOL, DEPTH)
    if key not in _CACHE:
        _CACHE[key] = build(*key)
    nc = _CACHE[key]
    cp, sel = make_consts()
    NO1 = max(NO, 1)

    def pad_odd(a, shape):
        a = A(a)
        if a.shape[0] == 0:
            return np.zeros((1,) + tuple(shape), np.float32)
        return a

    shared = {
        "w_ada": A(w_ada),
        "b_adaT": fm(A(b_ada)),
        "ln_gT": fm(A(ln_g)), "ln_bT": fm(A(ln_b)),
        "w_in_even": A(w_in_even), "w_out_even": A(w_out_even),
        "conv_w_aT": np.ascontiguousarray(A(conv_w_a).reshape(NE, CONVW_A, 8, 128).transpose(0, 3, 2, 1)),
        "gn_gT": fm(A(gn_g_a)), "gn_bT": fm(A(gn_b_a)),
        "sb_biasB": np.ascontiguousarray(np.broadcast_to(A(sb_bias)[:, None, :], (NE, 128, SBH))),
        "cache_k": A(cache_k).reshape(NE, NPOOL * 128, 1024),
        "cache_v": A(cache_v).reshape(NE, NPOOL * 128, 1024),
        "cpack": cp, "selpack": sel, "cpack2": CP2,
    }
    if NO:
        shared.update({
            "w_in_odd": A(w_in_odd), "w_out_odd": A(w_out_odd),
            "conv_w_cT": np.ascontiguousarray(A(conv_w_c).reshape(NO, 4, 64, 128).transpose(0, 3, 2, 1)),
            "a_logB": np.ascontiguousarray(np.broadcast_to(A(a_log_c)[:, None, :], (NO, 128, GV))),
            "dt_biasB": np.ascontiguousarray(np.broadcast_to(A(dt_bias_c)[:, None, :], (NO, 128, GV))),
            "gnorm_wT": np.ascontiguousarray(A(gnorm_w_c)[:, :, None]),
        })
    else:
        shared.update({
            "w_in_odd": np.zeros((1, D, IN_ODD), np.float32), "w_out_odd": np.zeros((1, 2 * D, D), np.float32),
            "conv_w_cT": np.zeros((1, 128, 64, 4), np.float32), "a_logB": np.zeros((1, 128, GV), np.float32),
            "dt_biasB": np.zeros((1, 128, GV), np.float32), "gnorm_wT": np.zeros((1, 128, 1), np.float32),
        })
    in_maps = []
    for c in range(ncores):
        b = (c * B) // ncores
        s = c % NS
        m = dict(shared)
        m["xT"] = np.ascontiguousarray(x_prompt[b].T)
        m["xsT"] = fm(x_sample[s, 0])[:, :, None].copy()
        cc = np.stack([c_prompt[b], c_sample[s]], -1)
        m["cT"] = np.ascontiguousarray(cc.reshape(KC, 128, 2).transpose(1, 0, 2))
        m["ptB"] = np.ascontiguousarray(np.broadcast_to(A(page_table)[s][None, :], (128, NPG))).astype(np.int32)
        m["st_conv_aT"] = np.ascontiguousarray(A(state_conv_a)[:, s].transpose(0, 2, 1))
        if NO:
            m["st_conv_cT"] = np.ascontiguousarray(A(state_conv_c)[:, s].transpose(0, 2, 1))
            m["st_delta"] = np.ascontiguousarray(A(state_delta)[:, s])
        else:
            m["st_conv_cT"] = np.zeros((1, 8192, 3), np.float32)
            m["st_delta"] = np.zeros((1, GV, 128, 128), np.float32)
        in_maps.append(m)
    import os
    if os.environ.get("KTRACE"):
        res = run_bass_kernel_spmd(nc, in_maps, core_ids=list(range(ncores)), trace=True)
        print("EXEC_TIME_NS", res.exec_time_ns, flush=True)
    else:
        res = run_bass_kernel_spmd(nc, in_maps, core_ids=list(range(ncores)))
    R = res.results
    global LAST_R
    LAST_R = R
    cores_b = [(b * ncores) // B for b in range(B)]
    y_p = np.stack([R[c]["yT"].T for c in cores_b]).astype(np.float32)
    y_s = np.stack([R[s]["ysT"][:, :, 0].T.reshape(1, D) for s in range(NS)]).astype(np.float32)
    nk_p = np.stack([R[c]["nk_p"] for c in cores_b], 1).reshape(NE, B, T, SBH, HD)
    nv_p = np.stack([R[c]["nv_p"] for c in cores_b], 1).reshape(NE, B, T, SBH, HD)
    nk_s = np.stack([R[s]["nk_s"] for s in range(NS)], 1).reshape(NE, NS, 1, SBH, HD)
    nv_s = np.stack([R[s]["nv_s"] for s in range(NS)], 1).reshape(NE, NS, 1, SBH, HD)
    ca_p = np.stack([R[c]["ca_pT"].transpose(0, 2, 1) for c in cores_b], 1)
    ca_s = np.stack([R[s]["ca_sT"].transpose(0, 2, 1) for s in range(NS)], 1)
    cc_p = np.stack([R[c]["cc_pT"].transpose(0, 2, 1) for c in cores_b], 1)[:NO]
    cc_s = np.stack([R[s]["cc_sT"].transpose(0, 2, 1) for s in range(NS)], 1)[:NO]
    dl_p = np.stack([R[c]["dl_p"] for c in cores_b], 1)[:NO]
    dl_s = np.stack([R[s]["dl_s"] for s in range(NS)], 1)[:NO]
    f = lambda a: np.ascontiguousarray(a, dtype=np.float32)
    return tuple(f(a) for a in (y_p, y_s, nk_p, nv_p, nk_s, nv_s, ca_p, ca_s, cc_p, cc_s, dl_p, dl_s))
```

```python
import contextlib
import numpy as np
import concourse.bass as bass
import concourse.mybir as mybir
from concourse.bass_utils import run_bass_kernel_spmd

F32 = mybir.dt.float32
BF16 = mybir.dt.bfloat16
I32 = mybir.dt.int32
AF = mybir.ActivationFunctionType
ALU = mybir.AluOpType

D = 2048
KC = 16
W_A = 1024
W_B = 1024
IN_EVEN = 7168
IN_ODD = 12352
CONVW_A = 31
HD = 128
SBH = 8
GV = 32
GQ = 16
ALPHA = 8 ** 0.25
LN_EPS = 1e-5
RMS_EPS = 1e-6
NEG = -30000.0


class Tk:
    __slots__ = ("lastw", "readers")

    def __init__(self):
        self.lastw = []
        self.readers = []


class Prog:
    def __init__(self, nc, st, n_dma_ch=12):
        self.nc = nc
        self.eng = {"pe": nc.tensor, "act": nc.scalar, "dve": nc.vector, "pool": nc.gpsimd, "sp": nc.sync}
        self.cnt = {e: 0 for e in ("pe", "act", "dve", "pool")}
        self.known = {e: {} for e in self.eng}
        self.n_dma_ch = n_dma_ch
        self.chcnt = [0] * n_dma_ch
        self.chrr = {"sp": 0, "pool": 0}
        self.chown = {"sp": list(range(0, n_dma_ch - 4)), "pool": list(range(n_dma_ch - 4, n_dma_ch))}
        self.sems = {}
        for n in ["pe", "act", "dve", "pool"] + ["d%d" % c for c in range(n_dma_ch)]:
            self.sems[n] = st.enter_context(nc.semaphore("s_" + n))

    def _deps(self, eng, reads, writes):
        w = {}
        for t in reads:
            for s, v in t.lastw:
                if w.get(s, 0) < v:
                    w[s] = v
        for t in writes:
            for s, v in t.lastw:
                if w.get(s, 0) < v:
                    w[s] = v
            for s, v in t.readers:
                if w.get(s, 0) < v:
                    w[s] = v
        kn = self.known[eng]
        out = []
        for s, v in w.items():
            if eng == "pe" and s == "pe":
                continue
            if kn.get(s, 0) >= v:
                continue
            kn[s] = v
            out.append((s, v))
        return out

    def _mark(self, tok, reads, writes):
        for t in writes:
            t.lastw = [tok]
            t.readers = []
        for t in reads:
            if t not in writes:
                t.readers.append(tok)
                if len(t.readers) > 40:
                    m = {}
                    for s, v in t.readers:
                        if m.get(s, 0) < v:
                            m[s] = v
                    t.readers = list(m.items())

    def op(self, eng, fn, reads=(), writes=()):
        e = self.eng[eng]
        for s, v in self._deps(eng, reads, writes):
            e.wait_ge(self.sems[s], v)
        self.cnt[eng] += 1
        tok = (eng, self.cnt[eng])
        self._mark(tok, reads, writes)
        fn(e).then_inc(self.sems[eng], 1)
        return tok

    def dma(self, q, fn, reads=(), writes=()):
        e = self.eng[q]
        chs = self.chown[q]
        c = chs[self.chrr[q] % len(chs)]
        self.chrr[q] += 1
        waits = self._deps(q, reads, writes)
        sname = "d%d" % c
        prev = self.chcnt[c] * 16
        kn = self.known[q]
        if prev > 0 and kn.get(sname, 0) < prev:
            kn[sname] = prev
            waits.append((sname, prev))
        for s, v in waits:
            e.wait_ge(self.sems[s], v)
        self.chcnt[c] += 1
        tok = (sname, self.chcnt[c] * 16)
        self._mark(tok, reads, writes)
        fn(e).then_inc(self.sems[sname], 16)
        return tok

    def barrier(self):
        allw = []
        for c in range(self.n_dma_ch):
            if self.chcnt[c]:
                allw.append(("d%d" % c, self.chcnt[c] * 16))
        for en in ("pe", "act", "dve", "pool"):
            if self.cnt[en]:
                allw.append((en, self.cnt[en]))
        for en, e in self.eng.items():
            kn = self.known[en]
            for s, v in allw:
                if kn.get(s, 0) >= v:
                    continue
                kn[s] = v
                e.wait_ge(self.sems[s], v)


def build(T, NPG, NPOOL, DEPTH):
    NE = (DEPTH + 1) // 2
    NO = DEPTH // 2
    NTT = T // 512
    NB = T // 128
    assert T % 512 == 0
    nc = bass.Bass("TRN2", target_bir_lowering=False)

    def din(name, shape, dt=F32):
        return nc.dram_tensor(name, list(shape), dt, kind="ExternalInput").ap()

    def dout(name, shape, dt=F32):
        return nc.dram_tensor(name, list(shape), dt, kind="ExternalOutput").ap()

    def dscr(name, shape, dt=F32):
        return nc.dram_tensor(name, list(shape), dt, kind="Internal").ap()

    NO1 = max(NO, 1)
    xT_in = din("xT", [D, T])
    xsT_in = din("xsT", [128, KC, 1])
    cT_in = din("cT", [128, KC, 2])
    w_ada = din("w_ada", [DEPTH, D, 3 * D])
    b_adaT = din("b_adaT", [DEPTH, 128, 48])
    ln_gT = din("ln_gT", [DEPTH, 128, KC])
    ln_bT = din("ln_bT", [DEPTH, 128, KC])
    w_in_even = din("w_in_even", [NE, D, IN_EVEN])
    w_out_even = din("w_out_even", [NE, D, D])
    conv_w_aT = din("conv_w_aT", [NE, 128, 8, CONVW_A])
    gn_gT = din("gn_gT", [NE, 128, 8])
    gn_bT = din("gn_bT", [NE, 128, 8])
    sb_biasB = din("sb_biasB", [NE, 128, SBH])
    cache_k = din("cache_k", [NE, NPOOL * 128, 1024])
    cache_v = din("cache_v", [NE, NPOOL * 128, 1024])
    ptB = din("ptB", [128, NPG], I32)
    st_conv_aT = din("st_conv_aT", [NE, W_A, 30])
    w_in_odd = din("w_in_odd", [NO1, D, IN_ODD])
    w_out_odd = din("w_out_odd", [NO1, 2 * D, D])
    conv_w_cT = din("conv_w_cT", [NO1, 128, 64, 4])
    a_logB = din("a_logB", [NO1, 128, GV])
    dt_biasB = din("dt_biasB", [NO1, 128, GV])
    gnorm_wT = din("gnorm_wT", [NO1, 128, 1])
    st_conv_cT = din("st_conv_cT", [NO1, 8192, 3])
    st_delta = din("st_delta", [NO1, GV, 128, 128])
    cpack = din("cpack", [128, 2048])
    cpack2 = din("cpack2", [128, 7 * 128])
    selpack = din("selpack", [32, GV, 128])
    yT_out = dout("yT", [D, T])
    ysT_out = dout("ysT", [128, KC, 1])
    nk_p = dout("nk_p", [NE, T, 1024])
    nv_p = dout("nv_p", [NE, T, 1024])
    nk_s = dout("nk_s", [NE, 1, 1024])
    nv_s = dout("nv_s", [NE, 1, 1024])
    ca_pT = dout("ca_pT", [NE, W_A, 30])
    ca_sT = dout("ca_sT", [NE, W_A, 30])
    cc_pT = dout("cc_pT", [NO1, 8192, 3])
    cc_sT = dout("cc_sT", [NO1, 8192, 3])
    dl_p = dout("dl_p", [NO1, GV, 128, 128])
    dl_s = dout("dl_s", [NO1, GV, 128, 128])
    import os
    DBG = bool(os.environ.get("KDBG"))
    if DBG:
        dbg_mod = dout("dbg_mod", [DEPTH, 128, 48, 2])
        dbg32 = dout("dbg32", [2, 12, 128, 128])
        dbg16 = dout("dbg16", [2, 12, 128, 128], BF16)
    X = dscr("Xscr", [D, T])
    HT = dscr("HTscr", [IN_ODD if NO else IN_EVEN, T])
    YT = dscr("YTscr", [2 * D, T], BF16)
    Vb = dscr("Vbscr", [T, 1024], BF16)
    QKVn = dscr("QKVn", [8192, T], BF16)

    with contextlib.ExitStack() as st:
        P = Prog(nc, st)

        uid = [0]

        def sbt(stack, name, shape, dt):
            uid[0] += 1
            return stack.enter_context(nc.sbuf_tensor("%s_%d" % (name, uid[0]), list(shape), dt))

        pb = [st.enter_context(nc.psum_tensor("pb%d" % i, [128, 512], F32)) for i in range(8)]
        kb = [Tk() for _ in range(8)]

        def ACT(out, in_, func, reads, writes, bias=None, scale=None, accum=None):
            kw = {}
            if bias is not None:
                kw["bias"] = bias
            if scale is not None:
                kw["scale"] = scale
            if accum is not None:
                kw["accum_out"] = accum
            return P.op("act", lambda e: e.activation(out=out, in_=in_, func=func, **kw), reads, writes)

        def TT(eng, out, a, b, op, reads, writes):
            return P.op(eng, lambda e: e.tensor_tensor(out=out, in0=a, in1=b, op=op), reads, writes)

        def TS(eng, out, a, s1, s2, op0, op1, reads, writes):
            if s2 is None:
                return P.op(eng, lambda e: e.tensor_scalar(out=out, in0=a, scalar1=s1, scalar2=None, op0=op0),
                            reads, writes)
            return P.op(eng, lambda e: e.tensor_scalar(out=out, in0=a, scalar1=s1, scalar2=s2, op0=op0, op1=op1),
                        reads, writes)

        def STT(eng, out, in0, scalar, in1, op0, op1, reads, writes):
            return P.op(eng, lambda e: e.scalar_tensor_tensor(out=out, in0=in0, scalar=scalar, in1=in1,
                                                              op0=op0, op1=op1), reads, writes)

        def CP(eng, out, in_, reads, writes):
            if eng == "act":
                return P.op("act", lambda e: e.activation(out=out, in_=in_, func=AF.Copy), reads, writes)
            return P.op(eng, lambda e: e.tensor_copy(out=out, in_=in_), reads, writes)

        def MM(out, lhsT, rhs, start, stop, reads, writes):
            return P.op("pe", lambda e: e.matmul(out, lhsT=lhsT, rhs=rhs, start=start, stop=stop,
                                                 skip_group_check=True), reads, writes)

        def DMA(q, out, in_, reads=(), writes=()):
            return P.dma(q, lambda e: e.dma_start(out=out, in_=in_), reads, writes)

        def MEMSET(eng, ap, val, writes):
            return P.op(eng, lambda e: e.memset(ap, val), (), writes)

        evac_rr = [0]

        def EVAC(out, in_, reads, writes):
            evac_rr[0] += 1
            if evac_rr[0] % 2:
                return CP("act", out, in_, reads, writes)
            return CP("dve", out, in_, reads, writes)

        cpk = sbt(st, "cpk", [128, 2048], F32)
        k_c = Tk()
        identf = cpk[:, 0:128]
        trif = cpk[:, 1152:1280]
        sellast = cpk[:, 1408:1536]
        identb = sbt(st, "identb", [128, 128], BF16)
        nuincl = sbt(st, "nuincl", [128, 128], BF16)
        mkb = sbt(st, "mkb", [128, 896], BF16)
        nones = sbt(st, "nones", [128, 128], BF16)
        onesG = sbt(st, "onesG", [128, 128], BF16)
        onesD = sbt(st, "onesD", [128, 128], BF16)
        ones1 = sbt(st, "ones1", [128, 128], BF16)
        upbig = sbt(st, "upbig", [128, 128], BF16)
        lowstrict = sbt(st, "lowstrict", [128, 128], F32)
        DMA("sp", cpk[:], cpack[:, :], writes=[k_c])
        CP("dve", identb[:], cpk[:, 0:128], [k_c], [k_c])
        CP("dve", nuincl[:], cpk[:, 128:256], [k_c], [k_c])
        CP("dve", mkb[:], cpk[:, 256:1152], [k_c], [k_c])
        CP("dve", upbig[:], cpk[:, 1280:1408], [k_c], [k_c])
        CP("dve", lowstrict[:], cpk[:, 1536:1664], [k_c], [k_c])
        MEMSET("pool", nones[:], -1.0, [k_c])
        MEMSET("pool", onesG[:], 1.0 / 128, [k_c])
        MEMSET("pool", onesD[:], 1.0 / D, [k_c])
        MEMSET("pool", ones1[:], 1.0, [k_c])
        zerob = sbt(st, "zerob", [128, 128], BF16)
        MEMSET("pool", zerob[:], 0.0, [k_c])
        epsc = sbt(st, "epsc", [128, 2], F32)
        MEMSET("pool", epsc[:, 0:1], LN_EPS, [k_c])
        MEMSET("pool", epsc[:, 1:2], RMS_EPS, [k_c])
        modT = sbt(st, "modT", [128, 48, 2], F32)
        k_mod = Tk()
        XS = sbt(st, "XS", [128, KC, 1], F32)
        k_xs = Tk()
        HS = sbt(st, "HS", [128, 100], F32)
        k_hs = Tk()
        YS = sbt(st, "YS", [128, 32, 1], BF16)
        k_ys = Tk()
        SN = sbt(st, "SN", [128, 64], F32)
        k_sn = Tk()
        DMA("sp", XS[:], xsT_in[:, :, :], writes=[k_xs])
        P.barrier()

        def phase_M(l):
            with contextlib.ExitStack() as s2:
                wst = [sbt(s2, "mw%d" % i, [128, KC, 128], F32) for i in range(2)]
                kw = [Tk(), Tk()]
                ct = sbt(s2, "mct", [128, KC, 2], F32)
                sc = sbt(s2, "msc", [128, KC, 2], F32)
                bT = sbt(s2, "mbT", [128, 48], F32)
                k1 = Tk()
                DMA("sp", ct[:], cT_in[:, :, :], writes=[k1])
                DMA("sp", bT[:], b_adaT[l], writes=[k1])
                ACT(sc[:], ct[:], AF.Silu, [k1], [k1])
                for jj in range(48):
                    i = jj % 2
                    DMA("sp", wst[i][:], w_ada[l][:, jj * 128:(jj + 1) * 128].rearrange("(k p) c -> p k c", p=128),
                        writes=[kw[i]])
                    b = jj % 4
                    for k in range(KC):
                        MM(pb[b][:, 0:2], wst[i][:, k, :], sc[:, k, :], k == 0, k == KC - 1, [kw[i], k1], [kb[b]])
                    TT("dve", modT[:, jj, :], pb[b][:, 0:2], bT[:, jj:jj + 1].to_broadcast([128, 2]), ALU.add,
                       [kb[b], k1], [k_mod])
                TS("dve", modT[:, 16:48, :], modT[:, 16:48, :], 1.0, None, ALU.add, None, [k_mod], [k_mod])
                if DBG:
                    DMA("sp", dbg_mod[l], modT[:], reads=[k_mod])
                P.barrier()

        def phase_UP(l):
            even = (l % 2 == 0)
            j = l // 2
            w_in = w_in_even[j] if even else w_in_odd[j]
            Xsrc = xT_in if l == 0 else X
            TH = min(T, 2048)
            with contextlib.ExitStack() as s2:
                U = sbt(s2, "U", [128, KC, T], BF16)
                kU = Tk()
                Us = sbt(s2, "Us", [128, KC, 1], BF16)
                with contextlib.ExitStack() as s3:
                    xst = [sbt(s3, "xst%d" % i, [128, 512], F32) for i in range(3)]
                    kx = [Tk() for _ in range(3)]
                    n = 0
                    for k in range(KC):
                        for t in range(NTT):
                            i = n % 3
                            n += 1
                            DMA("sp", xst[i][:], Xsrc[k * 128:(k + 1) * 128, t * 512:(t + 1) * 512], writes=[kx[i]])
                            ACT(U[:, k, t * 512:(t + 1) * 512], xst[i][:], AF.Identity, [kx[i], k_mod], [kU],
                                bias=modT[:, k, 0:1], scale=modT[:, 16 + k, 0:1])
                        ACT(Us[:, k, :], XS[:, k, :], AF.Identity, [k_xs, k_mod], [kU],
                            bias=modT[:, k, 1:2], scale=modT[:, 16 + k, 1:2])
                    P.barrier()
                import os
                KUP = os.environ.get("KUP", "i,ii,s,s2,nk,vb").split(",")
                if even:
                    chunks = [(c, 128) for c in range(0, 5120, 128)] + [(c, 128) for c in range(6144, 7168, 128)]
                else:
                    chunks = [(c, 128) for c in range(0, 12288, 128)] + [(12288, 64)]
                if "i" not in KUP:
                    chunks = []
                with contextlib.ExitStack() as s3:
                    wst = [sbt(s3, "pw%d" % i, [128, KC, 128], F32) for i in range(2)]
                    wb = [sbt(s3, "pwb%d" % i, [128, KC, 128], BF16) for i in range(2)]
                    hst = [sbt(s3, "ph%d" % i, [128, TH], F32) for i in range(2)]
                    kws = [Tk(), Tk()]
                    kwb = [Tk(), Tk()]
                    kh = [Tk(), Tk()]
                    hi = 0
                    br = 0
                    for ci, (c0, M) in enumerate(chunks):
                        i = ci % 2
                        DMA("sp", wst[i][:, :, 0:M], w_in[:, c0:c0 + M].rearrange("(k p) c -> p k c", p=128),
                            writes=[kws[i]])
                        CP("pool", wb[i][:, :, 0:M], wst[i][:, :, 0:M], [kws[i]], [kwb[i]])
                        for half in range(T // TH):
                            hb = hi % 2
                            hi += 1
                            for tt in range(TH // 512):
                                t = half * (TH // 512) + tt
                                b = br % 4
                                br += 1
                                for k in range(KC):
                                    MM(pb[b][0:M, :], wb[i][:, k, 0:M], U[:, k, t * 512:(t + 1) * 512],
                                       k == 0, k == KC - 1, [kwb[i], kU], [kb[b]])
                                EVAC(hst[hb][0:M, tt * 512:(tt + 1) * 512], pb[b][0:M, :], [kb[b]], [kh[hb]])
                            DMA("pool", HT[c0:c0 + M, half * TH:(half + 1) * TH], hst[hb][0:M, :], reads=[kh[hb]])
                        if "s" in KUP:
                            for k in range(KC):
                                MM(pb[4][0:M, 0:1], wb[i][:, k, 0:M], Us[:, k, :], k == 0, k == KC - 1,
                                   [kwb[i], kU], [kb[4]])
                            CP("act", HS[0:M, c0 // 128:c0 // 128 + 1], pb[4][0:M, 0:1], [kb[4]], [k_hs])
                    P.barrier()
                if even and "ii" in KUP:
                    with contextlib.ExitStack() as s3:
                        W2 = sbt(s3, "W2", [128, KC, 512], BF16)
                        kW2 = Tk()
                        st2 = [sbt(s3, "st2%d" % i, [128, 512], F32) for i in range(2)]
                        ks2 = [Tk(), Tk()]
                        ev = [sbt(s3, "ev%d" % i, [128, 512], F32) for i in range(2)]
                        kev = [Tk(), Tk()]
                        evb = [sbt(s3, "evb%d" % i, [128, 512], BF16) for i in range(2)]
                        kevb = [Tk(), Tk()]
                        evs = sbt(s3, "evs", [1, 512], F32)
                        kevs = Tk()
                        n = 0
                        for ct in range(4):
                            c0 = 4096 + ct * 512
                            isv = ct >= 2
                            dst_p = nv_p if isv else nk_p
                            dst_s = nv_s if isv else nk_s
                            oc = (ct % 2) * 512
                            for k in range(KC):
                                DMA("sp", st2[k % 2][:], w_in[k * 128:(k + 1) * 128, c0:c0 + 512],
                                    writes=[ks2[k % 2]])
                                CP("pool" if k % 2 else "dve", W2[:, k, :], st2[k % 2][:], [ks2[k % 2]], [kW2])
                            for tb in range(NB):
                                b = n % 4
                                e_i = n % 2
                                n += 1
                                for k in range(KC):
                                    MM(pb[b][:, :], U[:, k, tb * 128:(tb + 1) * 128], W2[:, k, :],
                                       k == 0, k == KC - 1, [kU, kW2], [kb[b]])
                                CP("act", ev[e_i][:], pb[b][:, :], [kb[b]], [kev[e_i]])
                                if "nk" in KUP:
                                    DMA("pool", dst_p[j, tb * 128:(tb + 1) * 128, oc:oc + 512], ev[e_i][:],
                                        reads=[kev[e_i]])
                                if isv and "vb" in KUP:
                                    CP("dve", evb[e_i][:], ev[e_i][:], [kev[e_i]], [kevb[e_i]])
                                    DMA("sp", Vb[tb * 128:(tb + 1) * 128, oc:oc + 512], evb[e_i][:],
                                        reads=[kevb[e_i]])
                            if "s2" in KUP:
                                for k in range(KC):
                                    MM(pb[4][0:1, :], Us[:, k, 0:1], W2[:, k, :], k == 0, k == KC - 1,
                                       [kU, kW2], [kb[4]])
                                CP("act", evs[:], pb[4][0:1, :], [kb[4]], [kevs])
                                DMA("pool", dst_s[j, 0:1, oc:oc + 512], evs[:], reads=[kevs])
                        P.barrier()

        def phase_A(l):
            j = l // 2
            with contextlib.ExitStack() as s2:
                G = sbt(s2, "G", [128, 32 + T], BF16)
                kG = Tk()
                Gs = sbt(s2, "Gs", [128, 32], BF16)
                hs32 = sbt(s2, "hs32", [128, 32], F32)
                khs = Tk()
                cw = sbt(s2, "cw", [128, 8, CONVW_A], F32)
                gg = sbt(s2, "gg", [128, 8], F32)
                gb = sbt(s2, "gb", [128, 8], F32)
                kcw = Tk()
                Dg = sbt(s2, "Dg", [128, CONVW_A, 128], BF16)
                kDg = Tk()
                va = [sbt(s2, "va%d" % i, [128, 512], F32) for i in range(2)]
                gl = [sbt(s2, "gl%d" % i, [128, 512], F32) for i in range(2)]
                gt = [sbt(s2, "gt%d" % i, [128, 512], F32) for i in range(2)]
                kin = [Tk(), Tk()]
                kgt = [Tk(), Tk()]
                g32 = sbt(s2, "g32", [128, 512], F32)
                kg32 = Tk()
                yf = sbt(s2, "yf", [128, 512], F32)
                ybf = sbt(s2, "ybf", [128, 512], BF16)
                ysq = sbt(s2, "ysq", [128, 512], BF16)
                t1 = sbt(s2, "t1", [128, 512], F32)
                t2 = sbt(s2, "t2", [128, 512], F32)
                yo = [sbt(s2, "yo%d" % i, [128, 512], BF16) for i in range(2)]
                kyo = [Tk(), Tk()]
                kt = Tk()
                DMA("sp", cw[:], conv_w_aT[j], writes=[kcw])
                DMA("sp", gg[:], gn_gT[j], writes=[kcw])
                DMA("sp", gb[:], gn_bT[j], writes=[kcw])
                n = 0

                def post(N, ypsum, kyp, gate_ap, kgate, cc, out_bf, kout):
                    CP("act", yf[:, 0:N], ypsum, [kyp], [kt])
                    CP("dve", ybf[:, 0:N], yf[:, 0:N], [kt], [kt])
                    ACT(ysq[:, 0:N], yf[:, 0:N], AF.Square, [kt], [kt])
                    MM(pb[5][:, 0:N], onesG[:], ybf[:, 0:N], True, True, [kt, k_c], [kb[5]])
                    MM(pb[6][:, 0:N], onesG[:], ysq[:, 0:N], True, True, [kt, k_c], [kb[6]])
                    CP("act", t2[:, 0:N], pb[5][:, 0:N], [kb[5]], [kt])
                    TT("dve", t1[:, 0:N], yf[:, 0:N], t2[:, 0:N], ALU.subtract, [kt], [kt])
                    ACT(t2[:, 0:N], t2[:, 0:N], AF.Square, [kt], [kt])
                    TT("dve", t2[:, 0:N], pb[6][:, 0:N], t2[:, 0:N], ALU.subtract, [kt, kb[6]], [kt])
                    ACT(t2[:, 0:N], t2[:, 0:N], AF.Sqrt, [kt], [kt], bias=epsc[:, 0:1])
                    P.op("dve", lambda e: e.reciprocal(out=t2[:, 0:N], in_=t2[:, 0:N]), [kt], [kt])
                    TT("dve", t1[:, 0:N], t1[:, 0:N], t2[:, 0:N], ALU.mult, [kt], [kt])
                    ACT(t1[:, 0:N], t1[:, 0:N], AF.Silu, [kt, kcw], [kt], bias=gb[:, cc:cc + 1],
                        scale=gg[:, cc:cc + 1])
                    ACT(t2[:, 0:N], gate_ap, AF.Silu, [kgate], [kt])
                    TT("dve", out_bf, t1[:, 0:N], t2[:, 0:N], ALU.mult, [kt], [kout])

                for cc in range(8):
                    r0 = cc * 128
                    for kk in range(CONVW_A):
                        TS("dve" if kk % 2 else "pool", Dg[:, kk, :], identf, cw[:, cc, kk:kk + 1], None, ALU.mult,
                           None, [k_c, kcw], [kDg])
                    MEMSET("pool", G[:, 0:32], 0.0, [kG])
                    for t in range(NTT):
                        i = n % 2
                        n += 1
                        DMA("sp", va[i][:], HT[r0:r0 + 128, t * 512:(t + 1) * 512], writes=[kin[i]])
                        DMA("sp", gl[i][:], HT[1024 + r0:1024 + r0 + 128, t * 512:(t + 1) * 512], writes=[kin[i]])
                        ACT(gl[i][:], gl[i][:], AF.Sigmoid, [kin[i]], [kin[i]])
                        TT("dve", g32[:], va[i][:], gl[i][:], ALU.mult, [kin[i]], [kg32])
                        CP("pool", G[:, 32 + t * 512:32 + (t + 1) * 512], g32[:], [kg32], [kG])
                        if t == NTT - 1:
                            DMA("pool", ca_pT[j, r0:r0 + 128, :], g32[:, 482:512], reads=[kg32])
                    for t in range(NTT):
                        i = n % 2
                        n += 1
                        DMA("sp", gt[i][:], HT[2048 + r0:2048 + r0 + 128, t * 512:(t + 1) * 512], writes=[kgt[i]])
                        b = t % 2
                        for kk in range(CONVW_A):
                            MM(pb[b][:, :], Dg[:, kk, :], G[:, 2 + t * 512 + kk:2 + t * 512 + kk + 512],
                               kk == 0, kk == CONVW_A - 1, [kDg, kG], [kb[b]])
                        post(512, pb[b][:, :], kb[b], gt[i][:], kgt[i], cc, yo[i][:], kyo[i])
                        DMA("pool", YT[r0:r0 + 128, t * 512:(t + 1) * 512], yo[i][:], reads=[kyo[i]])
                    DMA("sp", hs32[:, 0:30], st_conv_aT[j, r0:r0 + 128, :], writes=[khs])
                    ACT(t2[:, 0:1], HS[:, 8 + cc:9 + cc], AF.Sigmoid, [k_hs], [kt])
                    TT("dve", hs32[:, 30:31], HS[:, cc:cc + 1], t2[:, 0:1], ALU.mult, [k_hs, kt], [khs])
                    CP("dve", Gs[:, 0:31], hs32[:, 0:31], [khs], [khs])
                    DMA("pool", ca_sT[j, r0:r0 + 128, :], hs32[:, 1:31], reads=[khs])
                    for kk in range(CONVW_A):
                        MM(pb[2][:, 0:1], Dg[:, kk, :], Gs[:, kk:kk + 1], kk == 0, kk == CONVW_A - 1,
                           [kDg, khs], [kb[2]])
                    post(1, pb[2][:, 0:1], kb[2], HS[:, 16 + cc:17 + cc], k_hs, cc, YS[:, cc, :], k_ys)
                P.barrier()

        def phase_B(l):
            j = l // 2
            with contextlib.ExitStack() as s2:
                sbb = sbt(s2, "sbb", [128, SBH], F32)
                ksbb = Tk()
                DMA("sp", sbb[:], sb_biasB[j], writes=[ksbb])

                class Str:
                    pass

                def mkstream(sid):
                    S_ = Str()
                    S_.QT = sbt(s2, "QT", [128, T], BF16)
                    S_.KT = sbt(s2, "KT", [128, T], BF16)
                    S_.Vh = sbt(s2, "Vh", [128, NB, 128], BF16)
                    S_.kq = Tk()
                    S_.stg = [sbt(s2, "stg%d" % i, [128, 512], F32) for i in range(2)]
                    S_.kstg = [Tk(), Tk()]
                    S_.bg = [sbt(s2, "bg%d" % i, [128, 512], F32) for i in range(2)]
                    S_.kbg = [Tk(), Tk()]
                    S_.E = [sbt(s2, "E%d" % i, [128, 512], F32) for i in range(1)]
                    S_.SP32 = [sbt(s2, "SP%d" % i, [128, 512], F32) for i in range(1)]
                    S_.SPb = [sbt(s2, "SPb%d" % i, [128, 512], BF16) for i in range(1)]
                    S_.Wt = [sbt(s2, "Wt%d" % i, [128, 512], BF16) for i in range(1)]
                    S_.kE = [Tk(), Tk()]
                    S_.kSP = [Tk(), Tk()]
                    S_.kSPb = [Tk(), Tk()]
                    S_.kWt = [Tk(), Tk()]
                    S_.CAR = sbt(s2, "CAR", [128, 512], F32)
                    S_.CARb = [sbt(s2, "CARb%d" % i, [128, 512], BF16) for i in range(2)]
                    S_.kCAR = Tk()
                    S_.kCARb = [Tk(), Tk()]
                    S_.yo = [sbt(s2, "byo%d" % i, [128, 512], BF16) for i in range(2)]
                    S_.kyo = [Tk(), Tk()]
                    S_.zb = [2 * sid]
                    S_.ob = [2 * sid + 1]
                    S_.n = 0
                    S_.nblk = 0
                    return S_

                def head_stream(S_, h):
                    for t in range(NTT):
                        for which in range(2):
                            i = S_.n % 2
                            S_.n += 1
                            row = (3072 if which == 0 else 4096) + h * 128
                            DMA("sp", S_.stg[i][:], HT[row:row + 128, t * 512:(t + 1) * 512], writes=[S_.kstg[i]])
                            if which == 0:
                                ACT(S_.QT[:, t * 512:(t + 1) * 512], S_.stg[i][:], AF.Copy, [S_.kstg[i]], [S_.kq],
                                    scale=float(HD ** -0.5))
                            else:
                                CP("dve", S_.KT[:, t * 512:(t + 1) * 512], S_.stg[i][:], [S_.kstg[i]], [S_.kq])
                        yield
                    DMA("sp", S_.Vh[:], Vb[:, h * 128:(h + 1) * 128].rearrange("(nb p) d -> p nb d", p=128),
                        writes=[S_.kq])
                    for qt in range(NTT):
                        q0 = qt * 512
                        gi = qt % 2
                        DMA("sp", S_.bg[gi][:], HT[6144 + h * 128:6144 + (h + 1) * 128, q0:q0 + 512],
                            writes=[S_.kbg[gi]])
                        ACT(S_.bg[gi][:], S_.bg[gi][:], AF.Silu, [S_.kbg[gi]], [S_.kbg[gi]])
                        ob = S_.ob[0]
                        blocks = list(range(q0 // 128 + 3, -1, -1))
                        for bi, kbk in enumerate(blocks):
                            r = kbk - q0 // 128
                            zi = 0
                            S_.nblk += 1
                            zb = S_.zb[zi]
                            diag = r >= 0
                            MM(pb[zb][:, :], S_.KT[:, kbk * 128:(kbk + 1) * 128], S_.QT[:, q0:q0 + 512], True, False,
                               [S_.kq], [kb[zb]])
                            if diag:
                                MM(pb[zb][:, :], identb[:], mkb[:, 384 - r * 128:384 - r * 128 + 512], False, False,
                                   [k_c], [kb[zb]])
                            ACT(S_.E[zi][:], pb[zb][:, :], AF.Exp, [kb[zb], ksbb], [S_.kE[zi]], bias=sbb[:, h:h + 1])
                            yield
                            ACT(S_.SP32[zi][:], S_.E[zi][:], AF.Ln, [S_.kE[zi]], [S_.kSP[zi]], bias=1.0)
                            CP("dve", S_.SPb[zi][:], S_.SP32[zi][:], [S_.kSP[zi]], [S_.kSPb[zi]])
                            MM(pb[zb][:, :], nuincl[:], S_.SPb[zi][:], False, bi == 0, [S_.kSPb[zi], k_c], [kb[zb]])
                            if bi > 0:
                                MM(pb[zb][:, :], nones[:], S_.CARb[(bi - 1) % 2][:], False, True,
                                   [S_.kCARb[(bi - 1) % 2], k_c], [kb[zb]])
                            yield
                            ACT(S_.Wt[zi][:], pb[zb][:, :], AF.Exp, [kb[zb], ksbb], [S_.kWt[zi]], bias=sbb[:, h:h + 1])
                            MM(pb[ob][:, :], S_.Vh[:, kbk, :], S_.Wt[zi][:], bi == 0, bi == len(blocks) - 1,
                               [S_.kq, S_.kWt[zi]], [kb[ob]])
                            if bi < len(blocks) - 1:
                                if bi == 0:
                                    CP("pool", S_.CAR[:], S_.SP32[zi][:], [S_.kSP[zi]], [S_.kCAR])
                                else:
                                    TT("pool", S_.CAR[:], S_.CAR[:], S_.SP32[zi][:], ALU.add, [S_.kSP[zi], S_.kCAR],
                                       [S_.kCAR])
                                CP("pool", S_.CARb[bi % 2][:], S_.CAR[:], [S_.kCAR], [S_.kCARb[bi % 2]])
                            yield
                        TT("dve", S_.yo[gi][:], pb[ob][:, :], S_.bg[gi][:], ALU.mult, [kb[ob], S_.kbg[gi]],
                           [S_.kyo[gi]])
                        DMA("pool", YT[1024 + h * 128:1024 + (h + 1) * 128, q0:q0 + 512], S_.yo[gi][:],
                            reads=[S_.kyo[gi]])
                        yield

                NSTR = 4
                streams = [mkstream(i) for i in range(NSTR)]
                for h0 in range(0, SBH, NSTR):
                    gens = [head_stream(streams[i], h0 + i) for i in range(NSTR)]
                    alive = [True] * NSTR
                    while any(alive):
                        for gi_ in range(NSTR):
                            if alive[gi_]:
                                try:
                                    next(gens[gi_])
                                except StopIteration:
                                    alive[gi_] = False
                P.barrier()

        def phase_Bs(l):
            j = l // 2
            NC8 = NPG * SBH
            with contextlib.ExitStack() as s2:
                sbb = sbt(s2, "ssbb", [128, SBH], F32)
                ptf = sbt(s2, "ptf", [128, NPG], F32)
                pti = sbt(s2, "pti", [128, NPG], I32)
                idx = sbt(s2, "idx", [128, NPG], I32)
                k0 = Tk()
                DMA("sp", sbb[:], sb_biasB[j], writes=[k0])
                DMA("sp", pti[:], ptB[:, :], writes=[k0])
                CP("dve", ptf[:], pti[:], [k0], [k0])
                TS("dve", ptf[:], ptf[:], 128.0, cpk[:, 1664:1665], ALU.mult, ALU.add, [k0, k_c], [k0])
                if j > 0:
                    TS("dve", ptf[:], ptf[:], float(j * NPOOL * 128), None, ALU.add, None, [k0], [k0])
                CP("dve", idx[:], ptf[:], [k0], [k0])
                qcol = sbt(s2, "qcol", [128, SBH], F32)
                TS("dve", qcol[:], HS[:, 24:32], float(HD ** -0.5), None, ALU.mult, None, [k_hs], [k0])
                qb = sbt(s2, "qb", [128, SBH, 128], F32)
                dq = sbt(s2, "dq", [128, 128], F32)
                for h in range(SBH):
                    TS("dve", dq[:], identf, qcol[:, h:h + 1], None, ALU.mult, None, [k0, k_c], [k0])
                    MM(pb[0][:, 0:128], cpk[:, 1792:1920], dq[:], True, True, [k0, k_c], [kb[0]])
                    CP("act", qb[:, h, :], pb[0][:, 0:128], [kb[0]], [k0])
                Z = sbt(s2, "Z", [128, NPG, SBH], F32)
                kZ = Tk()
                kpg = [sbt(s2, "kpg%d" % i, [128, 1024], F32) for i in range(2)]
                kkpg = [Tk(), Tk()]
                junk = sbt(s2, "junk", [128, 128], F32)
                kj = Tk()
                cflat_k = cache_k.rearrange("e r c -> (e r) c")
                cflat_v = cache_v.rearrange("e r c -> (e r) c")
                for pg in range(NPG):
                    i = pg % 2
                    P.dma("pool", lambda e, i=i, pg=pg: e.indirect_dma_start(
                        out=kpg[i][:], out_offset=None, in_=cflat_k,
                        in_offset=bass.IndirectOffsetOnAxis(ap=idx[:, pg:pg + 1], axis=0)),
                        reads=[k0], writes=[kkpg[i]])
                    for h in range(SBH):
                        STT("dve", junk[:], kpg[i][:, h * 128:(h + 1) * 128], 1.0, qb[:, h, :], ALU.mult, ALU.mult,
                            [kkpg[i], k0], [kj])
                        P.op("dve", lambda e, pg=pg, h=h: e.tensor_reduce(
                            out=Z[:, pg, h:h + 1], in_=junk[:], axis=mybir.AxisListType.X, op=ALU.add),
                            [kj], [kZ])
                Zf = Z[:].rearrange("p g h -> p (g h)")
                for pg in range(NPG):
                    TT("dve", Z[:, pg, :], Z[:, pg, :], sbb[:], ALU.add, [kZ, k0], [kZ])
                Ee = sbt(s2, "Ee", [128, NC8], F32)
                SPs = sbt(s2, "SPs", [128, NC8], F32)
                TOT = sbt(s2, "TOT", [128, NC8], F32)
                TO2 = sbt(s2, "TO2", [128, NC8], F32)
                ACT(Ee[:], Zf, AF.Exp, [kZ], [kZ])
                ACT(SPs[:], Ee[:], AF.Ln, [kZ], [kZ], bias=1.0)
                ARG = sbt(s2, "ARG", [128, NC8], F32)
                for c0 in range(0, NC8, 512):
                    c1 = min(NC8, c0 + 512)
                    MM(pb[1][:, 0:c1 - c0], cpk[:, 1920:2048], SPs[:, c0:c1], True, True, [kZ, k_c], [kb[1]])
                    TT("dve", ARG[:, c0:c1], Zf[:, c0:c1], pb[1][:, 0:c1 - c0], ALU.subtract, [kZ, kb[1]], [kZ])
                    MM(pb[2][:, 0:c1 - c0], cpk[:, 1792:1920], SPs[:, c0:c1], True, True, [kZ, k_c], [kb[2]])
                    CP("act", TOT[:, c0:c1], pb[2][:, 0:c1 - c0], [kb[2]], [kZ])
                T3 = TOT[:].rearrange("p (g h) -> p g h", h=SBH)
                T4 = TO2[:].rearrange("p (g h) -> p g h", h=SBH)
                src, dst = T3, T4
                sh = 1
                while sh < NPG:
                    CP("dve", dst[:, NPG - sh:NPG, :], src[:, NPG - sh:NPG, :], [kZ], [kZ])
                    TT("dve", dst[:, 0:NPG - sh, :], src[:, 0:NPG - sh, :], src[:, sh:NPG, :], ALU.add, [kZ], [kZ])
                    src, dst = dst, src
                    sh *= 2
                A3 = ARG[:].rearrange("p (g h) -> p g h", h=SBH)
                if NPG > 1:
                    TT("dve", A3[:, 0:NPG - 1, :], A3[:, 0:NPG - 1, :], src[:, 1:NPG, :], ALU.subtract, [kZ], [kZ])
                Wg = sbt(s2, "Wg", [128, NPG, SBH], F32)
                ACT(Wg[:].rearrange("p g h -> p (g h)"), ARG[:], AF.Exp, [kZ], [kZ])
                vpg = [sbt(s2, "vpg%d" % i, [128, 1024], F32) for i in range(2)]
                kvpg = [Tk(), Tk()]
                for pg in range(NPG):
                    i = pg % 2
                    P.dma("pool", lambda e, i=i, pg=pg: e.indirect_dma_start(
                        out=vpg[i][:], out_offset=None, in_=cflat_v,
                        in_offset=bass.IndirectOffsetOnAxis(ap=idx[:, pg:pg + 1], axis=0)),
                        reads=[k0], writes=[kvpg[i]])
                    if pg == 0:
                        MM(pb[3][:, 0:SBH], zerob[:], zerob[:, 0:SBH], True, False, [k_c], [kb[3]])
                    for h in range(SBH):
                        MM(pb[3][:, h:h + 1], vpg[i][:, h * 128:(h + 1) * 128], Wg[:, pg, h:h + 1],
                           False, pg == NPG - 1 and h == SBH - 1, [kvpg[i], kZ], [kb[3]])
                gsl = sbt(s2, "gsl", [128, SBH], F32)
                ACT(gsl[:], HS[:, 48:56], AF.Silu, [k_hs], [k0])
                TT("dve", YS[:, 8:16, :].rearrange("p h o -> p (h o)"), pb[3][:, 0:SBH], gsl[:], ALU.mult,
                   [kb[3], k0], [k_ys])
                P.barrier()

        def phase_C(l):
            j = l // 2
            with contextlib.ExitStack() as s2:
                cw = sbt(s2, "ccw", [128, 64, 4], F32)
                kcw = Tk()
                DMA("sp", cw[:], conv_w_cT[j], writes=[kcw])
                PRE = [sbt(s2, "PRE%d" % i, [128, 4 + T], F32) for i in range(2)]
                kpre = [Tk(), Tk()]
                acc = sbt(s2, "cacc", [128, T], F32)
                kacc = Tk()
                sq = sbt(s2, "csq", [128, T], BF16)
                ksq = Tk()
                rs = [sbt(s2, "crs%d" % i, [128, 512], F32) for i in range(2)]
                krs = [Tk(), Tk()]
                ob = [sbt(s2, "cob%d" % i, [128, T], BF16) for i in range(2)]
                kob = [Tk(), Tk()]
                for i in range(2):
                    MEMSET("pool", PRE[i][:, 0:4], 0.0, [kpre[i]])
                nr = 0
                for cc in range(64):
                    i = cc % 2
                    r0 = cc * 128
                    DMA("sp", PRE[i][:, 4:4 + T], HT[r0:r0 + 128, :], writes=[kpre[i]])
                    DMA("pool", cc_pT[j, r0:r0 + 128, :], PRE[i][:, 1 + T:4 + T], reads=[kpre[i]])
                    TS("dve", acc[:], PRE[i][:, 1:1 + T], cw[:, cc, 0:1], None, ALU.mult, None, [kpre[i], kcw], [kacc])
                    for kk in range(1, 4):
                        STT("dve", acc[:], PRE[i][:, 1 + kk:1 + kk + T], cw[:, cc, kk:kk + 1], acc[:], ALU.mult,
                            ALU.add, [kpre[i], kcw, kacc], [kacc])
                    ACT(acc[:], acc[:], AF.Silu, [kacc], [kacc])
                    if cc < 32:
                        ACT(sq[:], acc[:], AF.Square, [kacc], [ksq])
                        for t in range(NTT):
                            b = nr % 4
                            q = nr % 2
                            nr += 1
                            MM(pb[b][:, :], ones1[:], sq[:, t * 512:(t + 1) * 512], True, True, [ksq, k_c], [kb[b]])
                            ACT(rs[q][:], pb[b][:, :], AF.Sqrt, [kb[b]], [krs[q]], bias=epsc[:, 1:2])
                            P.op("dve", lambda e, q=q: e.reciprocal(out=rs[q][:], in_=rs[q][:]), [krs[q]], [krs[q]])
                            if cc < 16:
                                STT("dve", ob[i][:, t * 512:(t + 1) * 512], acc[:, t * 512:(t + 1) * 512],
                                    float(HD ** -0.5), rs[q][:], ALU.mult, ALU.mult, [kacc, krs[q]], [kob[i]])
                            else:
                                TT("dve", ob[i][:, t * 512:(t + 1) * 512], acc[:, t * 512:(t + 1) * 512], rs[q][:],
                                   ALU.mult, [kacc, krs[q]], [kob[i]])
                    else:
                        CP("pool", ob[i][:], acc[:], [kacc], [kob[i]])
                    DMA("pool", QKVn[r0:r0 + 128, :], ob[i][:], reads=[kob[i]])
                FS = sbt(s2, "FS", [128, 64, 4], F32)
                kfs = Tk()
                DMA("sp", FS[:, :, 0:3], st_conv_cT[j].rearrange("(c p) k -> p c k", p=128), writes=[kfs])
                CP("dve", FS[:, :, 3], HS[:, 0:64], [k_hs], [kfs])
                DMA("pool", cc_sT[j].rearrange("(c p) k -> p c k", p=128), FS[:, :, 1:4], reads=[kfs])
                pr = sbt(s2, "cpr", [128, 64, 4], F32)
                TT("dve", pr[:], FS[:], cw[:], ALU.mult, [kfs, kcw], [kfs])
                P.op("dve", lambda e: e.tensor_reduce(out=SN[:, 0:64], in_=pr[:], axis=mybir.AxisListType.X,
                                                      op=ALU.add), [kfs], [k_sn])
                ACT(SN[:, 0:64], SN[:, 0:64], AF.Silu, [k_sn], [k_sn])
                sqs = sbt(s2, "csqs", [128, 32], F32)
                ACT(sqs[:], SN[:, 0:32], AF.Square, [k_sn], [kfs])
                MM(pb[5][:, 0:32], cpk[:, 1792:1920], sqs[:], True, True, [kfs, k_c], [kb[5]])
                ACT(sqs[:], pb[5][:, 0:32], AF.Sqrt, [kb[5]], [kfs], bias=epsc[:, 1:2])
                P.op("dve", lambda e: e.reciprocal(out=sqs[:], in_=sqs[:]), [kfs], [kfs])
                TT("dve", SN[:, 0:32], SN[:, 0:32], sqs[:], ALU.mult, [kfs, k_sn], [k_sn])
                TS("dve", SN[:, 0:16], SN[:, 0:16], float(HD ** -0.5), None, ALU.mult, None, [k_sn], [k_sn])
                P.barrier()

        def phase_G(l):
            j = l // 2
            NCH = T // 128
            NCS = NCH + 1
            with contextlib.ExitStack() as s2:
                BETA = sbt(s2, "BETA", [128, NCS, GV], F32)
                GC = sbt(s2, "GC", [128, NCS, GV], F32)
                EGC = sbt(s2, "EGC", [128, NCS, GV], F32)
                BE = sbt(s2, "BE", [128, NCS, GV], F32)
                EKD = sbt(s2, "EKD", [128, NCS, GV], F32)
                EGL = sbt(s2, "EGL", [128, NCS, GV], F32)
                GCT = sbt(s2, "GCT", [32, NCS, 128], F32)
                nGCT = sbt(s2, "nGCT", [32, NCS, 128], F32)
                SEL = sbt(s2, "SEL", [32, GV, 128], F32)
                gw = sbt(s2, "gw", [128, 1], F32)
                ktab = Tk()
                DMA("sp", SEL[:], selpack[:, :, :], writes=[ktab])
                DMA("sp", gw[:], gnorm_wT[j], writes=[ktab])
                with contextlib.ExitStack() as s3:
                    BA = sbt(s3, "BA", [64, T + 128], F32)
                    kba = Tk()
                    nea = sbt(s3, "nea", [128, GV], F32)
                    dtb = sbt(s3, "dtb", [128, GV], F32)
                    kq = Tk()
                    DMA("sp", BA[:, 0:T], HT[12288:12352, :], writes=[kba])
                    MEMSET("pool", BA[:, T:T + 128], 0.0, [kba])
                    CP("dve", BA[:, T:T + 1], HS[0:64, 96:97], [k_hs], [kba])
                    DMA("sp", nea[:], a_logB[j], writes=[kq])
                    DMA("sp", dtb[:], dt_biasB[j], writes=[kq])
                    ACT(nea[:], nea[:], AF.Exp, [kq], [kq])
                    TS("dve", nea[:], nea[:], -1.0, None, ALU.mult, None, [kq], [kq])
                    tmp = [sbt(s3, "g0t%d" % i, [128, GV], F32) for i in range(2)]
                    g32 = [sbt(s3, "g0g%d" % i, [128, GV], F32) for i in range(2)]
                    gl = [sbt(s3, "g0l%d" % i, [128, GV], F32) for i in range(2)]
                    kt = [Tk(), Tk()]
                    for n in range(NCS):
                        q = n % 2
                        MM(pb[0 + q][:, 0:64], BA[:, n * 128:(n + 1) * 128], identf[0:64, 0:64], True, True,
                           [kba, k_c], [kb[0 + q]])
                        ACT(BETA[:, n, :], pb[0 + q][:, 0:32], AF.Sigmoid, [kb[0 + q]], [ktab])
                        CP("act", tmp[q][:], pb[0 + q][:, 32:64], [kb[0 + q]], [kt[q]])
                        TT("dve", tmp[q][:], tmp[q][:], dtb[:], ALU.add, [kt[q], kq], [kt[q]])
                        ACT(tmp[q][:], tmp[q][:], AF.Exp, [kt[q]], [kt[q]])
                        ACT(tmp[q][:], tmp[q][:], AF.Ln, [kt[q]], [kt[q]], bias=1.0)
                        TT("dve", g32[q][:], tmp[q][:], nea[:], ALU.mult, [kt[q], kq], [kt[q]])
                        if n == NCH:
                            TS("dve", g32[q][:], g32[q][:], identf[:, 0:1], None, ALU.mult, None, [kt[q], k_c],
                               [kt[q]])
                        MM(pb[2 + q][:, 0:32], trif, g32[q][:], True, True, [kt[q], k_c], [kb[2 + q]])
                        CP("act", GC[:, n, :], pb[2 + q][:, 0:32], [kb[2 + q]], [ktab])
                        MM(pb[4 + q][0:32, 0:128], g32[q][:], trif, True, True, [kt[q], k_c], [kb[4 + q]])
                        CP("act", GCT[:, n, :], pb[4 + q][0:32, 0:128], [kb[4 + q]], [ktab])
                        TS("dve", nGCT[:, n, :], GCT[:, n, :], -1.0, None, ALU.mult, None, [ktab], [ktab])
                        MM(pb[6 + q][:, 0:32], sellast, GC[:, n, :], True, True, [ktab, k_c], [kb[6 + q]])
                        ACT(EGL[:, n, :], pb[6 + q][:, 0:32], AF.Exp, [kb[6 + q]], [ktab])
                        CP("act", gl[q][:], pb[6 + q][:, 0:32], [kb[6 + q]], [kt[q]])
                        TT("dve", gl[q][:], gl[q][:], GC[:, n, :], ALU.subtract, [kt[q], ktab], [kt[q]])
                        ACT(EKD[:, n, :], gl[q][:], AF.Exp, [kt[q]], [ktab])
                        ACT(EGC[:, n, :], GC[:, n, :], AF.Exp, [ktab], [ktab])
                        TT("dve", BE[:, n, :], BETA[:, n, :], EGC[:, n, :], ALU.mult, [ktab], [ktab])
                    P.barrier()
                qT = sbt(s2, "gqT", [128, T], BF16)
                kT = sbt(s2, "gkT", [128, T], BF16)
                vT = sbt(s2, "gvT", [128, 2, T], BF16)
                yTh = sbt(s2, "gyT", [128, 2, T], BF16)
                zB = [sbt(s2, "gzB%d" % i, [128, 2, 256], F32) for i in range(2)]
                kzB = [Tk(), Tk()]
                kin = Tk()
                kyT = Tk()
                sqT = sbt(s2, "sqT", [128, 128], BF16)
                skT = sbt(s2, "skT", [128, 128], BF16)
                svT = sbt(s2, "svT", [128, 2, 128], BF16)
                szT = sbt(s2, "szT", [128, 2, 128], F32)
                syT = sbt(s2, "syT", [128, 2, 128], BF16)
                ksin = Tk()
                ksy = Tk()
                S32 = sbt(s2, "S32", [128, 2, 128], F32)
                Sb = sbt(s2, "Sb", [128, 2, 128], BF16)
                kS = Tk()
                kSb = Tk()

                class TL:
                    def __init__(self, name, shape, dt, n=1):
                        self.t = [sbt(s2, "%s%d" % (name, i), list(shape), dt) for i in range(n)]
                        self.k = [Tk() for _ in range(n)]

                def fl(t, w):
                    return t[:].rearrange("p m d -> p (m d)")[:, 0:w]

                msk4 = sbt(s2, "msk4", [128, 7, 512], BF16)
                id4 = sbt(s2, "id4", [128, 512], BF16)
                up4 = sbt(s2, "up4", [128, 512], BF16)
                low2 = sbt(s2, "low2", [128, 256], F32)
                with contextlib.ExitStack() as s3:
                    mskf = sbt(s3, "mskf", [128, 7 * 128], F32)
                    DMA("sp", mskf[:], cpack2[:, :], writes=[ktab])
                    for r in range(4):
                        CP("dve", msk4[:, :, r * 128:(r + 1) * 128], mskf[:].rearrange("p (a b) -> p a b", b=128),
                           [ktab], [ktab])
                        CP("dve", id4[:, r * 128:(r + 1) * 128], identb[:], [k_c], [ktab])
                        CP("dve", up4[:, r * 128:(r + 1) * 128], upbig[:], [k_c], [ktab])
                    for r in range(2):
                        CP("dve", low2[:, r * 128:(r + 1) * 128], lowstrict[:], [k_c], [ktab])
                    P.barrier()
                ktok2 = TL("ktok2", [128, 2, 128], F32)
                KKs2 = TL("KKs2", [128, 2, 128], F32)
                QKs2 = TL("QKs2", [128, 2, 128], F32)
                bv4 = TL("bv4", [128, 4, 128], BF16, 2)
                kbg4 = TL("kbg4", [128, 4, 128], BF16)
                kdec4 = TL("kdec4", [128, 4, 128], BF16, 2)
                dec4 = TL("dec4", [128, 4, 128], F32)
                L4 = TL("L4", [128, 4, 128], BF16)
                A4 = TL("A4", [128, 4, 128], BF16)
                Mf4 = TL("Mf4", [128, 4, 128], BF16)
                AT4 = TL("AT4", [128, 4, 128], BF16, 2)
                nwT4 = TL("nwT4", [128, 4, 128], BF16, 2)
                Lp = TL("Lp", [128, 4, 128], BF16, 2)
                Mp = TL("Mp", [128, 4, 128], BF16, 2)
                Tn = TL("Tn", [128, 4, 128], BF16, 2)
                Tt = TL("Tt", [128, 4, 128], BF16, 4)
                Cs = [TL("Cs%d" % i, [128, 4, 128], BF16) for i in range(3)]
                Cts = [TL("Cts%d" % i, [128, 4, 128], BF16) for i in range(3)]
                IL = TL("IL", [128, 4, 128], BF16)
                IM = TL("IM", [128, 4, 128], BF16)
                Xs = TL("Xs", [128, 4, 128], BF16)
                X2s = TL("X2s", [128, 4, 128], BF16)
                vnb2 = TL("vnb2", [128, 2, 128], BF16)
                qS2 = TL("qS2", [128, 2, 128], F32)
                o2 = TL("o2", [128, 2, 128], F32)
                junk2 = TL("junk2", [128, 2, 128], F32)
                onb2 = TL("onb2", [128, 2, 128], BF16)
                ss2 = TL("ss2", [128, 2], F32)
                zg2 = TL("zg2", [128, 2, 128], F32)
                bank = [0]

                def nb():
                    bank[0] = (bank[0] + 1) % 8
                    return bank[0]

                def bc_d(tab, n, hv0):
                    return tab[:, n, hv0:hv0 + 2].unsqueeze(2).to_broadcast([128, 2, 128])

                def bc_v(t3, ci):
                    return t3[:, ci:ci + 1, :].to_broadcast([128, 2, 128])

                def mm4(dst_bank, nm, lhs_fn, rhs_fn, reads):
                    for m in range(nm):
                        MM(pb[dst_bank][:, m * 128:(m + 1) * 128], lhs_fn(m), rhs_fn(m), True, True, reads,
                           [kb[dst_bank]])

                ttc = [0]

                def stage1(bp, hq, cbs, kI):
                    hv0 = 2 * hq
                    nc_ = len(cbs)
                    nm = 2 * nc_
                    W = nm * 128
                    Wk = nc_ * 128
                    b = nb()
                    for ci, (n, qc, kc, vc) in enumerate(cbs):
                        MM(pb[b][:, ci * 128:(ci + 1) * 128], kc, identb[:], True, True, [kI, k_c], [kb[b]])
                    CP("act", fl(ktok2.t[0], Wk), pb[b][:, 0:Wk], [kb[b]], [ktok2.k[0]])
                    yield
                    b = nb()
                    for ci, (n, qc, kc, vc) in enumerate(cbs):
                        MM(pb[b][:, ci * 128:(ci + 1) * 128], kc, kc, True, True, [kI], [kb[b]])
                    TT("dve", fl(KKs2.t[0], Wk), pb[b][:, 0:Wk], low2[:, 0:Wk], ALU.mult, [kb[b], ktab], [KKs2.k[0]])
                    yield
                    b = nb()
                    for ci, (n, qc, kc, vc) in enumerate(cbs):
                        MM(pb[b][:, ci * 128:(ci + 1) * 128], qc, kc, True, True, [kI], [kb[b]])
                    CP("act", fl(QKs2.t[0], Wk), pb[b][:, 0:Wk], [kb[b]], [QKs2.k[0]])
                    yield
                    b = nb()
                    for ci, (n, qc, kc, vc) in enumerate(cbs):
                        for vh in range(2):
                            m = 2 * ci + vh
                            MM(pb[b][:, m * 128:(m + 1) * 128], vc(vh), identb[:], True, True, [kI, k_c], [kb[b]])
                    for ci, (n, qc, kc, vc) in enumerate(cbs):
                        pv = pb[b][:, 2 * ci * 128:(2 * ci + 2) * 128].rearrange("p (v d) -> p v d", d=128)
                        TT("dve", bv4.t[bp][:, 2 * ci:2 * ci + 2, :], pv, bc_d(BETA, n, hv0), ALU.mult,
                           [kb[b], ktab], [bv4.k[bp]])
                        TT("pool", kbg4.t[0][:, 2 * ci:2 * ci + 2, :], bc_v(ktok2.t[0], ci), bc_d(BE, n, hv0),
                           ALU.mult, [ktok2.k[0], ktab], [kbg4.k[0]])
                        TT("pool", kdec4.t[bp][:, 2 * ci:2 * ci + 2, :], bc_v(ktok2.t[0], ci), bc_d(EKD, n, hv0),
                           ALU.mult, [ktok2.k[0], ktab], [kdec4.k[bp]])
                    b = nb()
                    MM(pb[b][:, 0:W], identb[:], up4[:, 0:W], True, False, [k_c, ktab], [kb[b]])
                    for ci, (n, qc, kc, vc) in enumerate(cbs):
                        for vh in range(2):
                            m = 2 * ci + vh
                            hv = hv0 + vh
                            MM(pb[b][:, m * 128:(m + 1) * 128], SEL[:, hv, :], GCT[:, n, :], False, False, [ktab],
                               [kb[b]])
                            MM(pb[b][:, m * 128:(m + 1) * 128], nGCT[:, n, :], SEL[:, hv, :], False,
                               m == nm - 1, [ktab], [kb[b]])
                    ACT(fl(dec4.t[0], W), pb[b][:, 0:W], AF.Exp, [kb[b]], [dec4.k[0]], scale=-1.0)
                    yield
                    for ci, (n, qc, kc, vc) in enumerate(cbs):
                        sl = slice(2 * ci, 2 * ci + 2)
                        TT("dve", L4.t[0][:, sl, :], dec4.t[0][:, sl, :], bc_d(BETA, n, hv0), ALU.mult,
                           [dec4.k[0], ktab], [L4.k[0]])
                        TT("dve", L4.t[0][:, sl, :], L4.t[0][:, sl, :], bc_v(KKs2.t[0], ci), ALU.mult,
                           [KKs2.k[0], L4.k[0]], [L4.k[0]])
                        TT("pool", A4.t[0][:, sl, :], dec4.t[0][:, sl, :], bc_v(QKs2.t[0], ci), ALU.mult,
                           [dec4.k[0], QKs2.k[0]], [A4.k[0]])
                    b = nb()
                    mm4(b, nm, lambda m: L4.t[0][:, m, :], lambda m: identb[:], [L4.k[0], k_c])
                    CP("act", fl(Mf4.t[0], W), pb[b][:, 0:W], [kb[b]], [Mf4.k[0]])
                    yield
                    b = nb()
                    mm4(b, nm, lambda m: A4.t[0][:, m, :], lambda m: identb[:], [A4.k[0], k_c])
                    CP("act", fl(AT4.t[bp], W), pb[b][:, 0:W], [kb[b]], [AT4.k[bp]])
                    yield
                    mk_ = lambda i: msk4[:, i, 0:W]
                    TT("pool", fl(Lp.t[0], W), fl(L4.t[0], W), mk_(0), ALU.mult, [L4.k[0], ktab], [Lp.k[0]])
                    TT("pool", fl(Mp.t[0], W), fl(Mf4.t[0], W), mk_(0), ALU.mult, [Mf4.k[0], ktab], [Mp.k[0]])
                    for si in range(3):
                        TT("pool", fl(Cs[si].t[0], W), fl(L4.t[0], W), mk_(1 + si), ALU.mult, [L4.k[0], ktab],
                           [Cs[si].k[0]])
                        TT("pool", fl(Cts[si].t[0], W), fl(Mf4.t[0], W), mk_(4 + si), ALU.mult, [Mf4.k[0], ktab],
                           [Cts[si].k[0]])
                    tb0 = 2 * bp
                    TT("pool", fl(Tn.t[0], W), id4[:, 0:W], fl(Lp.t[0], W), ALU.subtract, [Lp.k[0], ktab], [Tn.k[0]])
                    TT("pool", fl(Tt.t[tb0], W), id4[:, 0:W], fl(Mp.t[0], W), ALU.subtract, [Mp.k[0], ktab],
                       [Tt.k[tb0]])
                    cl, cm, ct_, cn = 0, 0, 0, 0
                    for lev in range(3):
                        nl, nm_ = 1 - cl, 1 - cm
                        bl = nb()
                        mm4(bl, nm, lambda m: Mp.t[cm][:, m, :], lambda m: Lp.t[cl][:, m, :], [Lp.k[cl], Mp.k[cm]])
                        bm = nb()
                        mm4(bm, nm, lambda m: Lp.t[cl][:, m, :], lambda m: Mp.t[cm][:, m, :], [Lp.k[cl], Mp.k[cm]])
                        CP("act", fl(Lp.t[nl], W), pb[bl][:, 0:W], [kb[bl]], [Lp.k[nl]])
                        CP("dve", fl(Mp.t[nm_], W), pb[bm][:, 0:W], [kb[bm]], [Mp.k[nm_]])
                        yield
                        TT("pool", fl(IL.t[0], W), fl(Lp.t[nl], W), id4[:, 0:W], ALU.add, [Lp.k[nl], ktab], [IL.k[0]])
                        TT("pool", fl(IM.t[0], W), fl(Mp.t[nm_], W), id4[:, 0:W], ALU.add, [Mp.k[nm_], ktab],
                           [IM.k[0]])
                        nt, nn = 1 - ct_, 1 - cn
                        bt_ = nb()
                        mm4(bt_, nm, lambda m: IL.t[0][:, m, :], lambda m: Tt.t[tb0 + ct_][:, m, :],
                            [IL.k[0], Tt.k[tb0 + ct_]])
                        CP("act", fl(Tt.t[tb0 + nt], W), pb[bt_][:, 0:W], [kb[bt_]], [Tt.k[tb0 + nt]])
                        bn_ = nb()
                        mm4(bn_, nm, lambda m: IM.t[0][:, m, :], lambda m: Tn.t[cn][:, m, :], [IM.k[0], Tn.k[cn]])
                        CP("dve", fl(Tn.t[nn], W), pb[bn_][:, 0:W], [kb[bn_]], [Tn.k[nn]])
                        yield
                        cl, cm, ct_, cn = nl, nm_, nt, nn
                    for si in range(3):
                        nt, nn = 1 - ct_, 1 - cn
                        bx2 = nb()
                        mm4(bx2, nm, lambda m: Cs[si].t[0][:, m, :], lambda m: Tt.t[tb0 + ct_][:, m, :],
                            [Cs[si].k[0], Tt.k[tb0 + ct_]])
                        CP("dve", fl(X2s.t[0], W), pb[bx2][:, 0:W], [kb[bx2]], [X2s.k[0]])
                        yield
                        if si < 2:
                            bx = nb()
                            mm4(bx, nm, lambda m: Cts[si].t[0][:, m, :], lambda m: Tn.t[cn][:, m, :],
                                [Cts[si].k[0], Tn.k[cn]])
                            CP("act", fl(Xs.t[0], W), pb[bx][:, 0:W], [kb[bx]], [Xs.k[0]])
                        by2 = nb()
                        mm4(by2, nm, lambda m: Tn.t[cn][:, m, :], lambda m: X2s.t[0][:, m, :], [Tn.k[cn], X2s.k[0]])
                        TT("dve", fl(Tt.t[tb0 + nt], W), fl(Tt.t[tb0 + ct_], W), pb[by2][:, 0:W], ALU.subtract,
                           [Tt.k[tb0 + ct_], kb[by2]], [Tt.k[tb0 + nt]])
                        yield
                        if si < 2:
                            by = nb()
                            mm4(by, nm, lambda m: Tt.t[tb0 + ct_][:, m, :], lambda m: Xs.t[0][:, m, :],
                                [Tt.k[tb0 + ct_], Xs.k[0]])
                            TT("dve", fl(Tn.t[nn], W), fl(Tn.t[cn], W), pb[by][:, 0:W], ALU.subtract,
                               [Tn.k[cn], kb[by]], [Tn.k[nn]])
                            cn = nn
                        ct_ = nt
                    ti = tb0 + ct_
                    b = nb()
                    mm4(b, nm, lambda m: kbg4.t[0][:, m, :], lambda m: Tt.t[ti][:, m, :], [kbg4.k[0], Tt.k[ti]])
                    ACT(fl(nwT4.t[bp], W), pb[b][:, 0:W], AF.Copy, [kb[b]], [nwT4.k[bp]], scale=-1.0)
                    yield

                def stage2(bp, ti, hq, ci, n, qc, zc, yout, kZ, kY, kI):
                    hv0 = 2 * hq
                    ti = 2 * bp
                    m0 = 2 * ci
                    TiT = Tt.t[ti]
                    kTi = Tt.k[ti]
                    b = nb()
                    for vh in range(2):
                        MM(pb[b][:, vh * 128:(vh + 1) * 128], TiT[:, m0 + vh, :], bv4.t[bp][:, m0 + vh, :], True, False,
                           [kTi, bv4.k[bp]], [kb[b]])
                        MM(pb[b][:, vh * 128:(vh + 1) * 128], nwT4.t[bp][:, m0 + vh, :], Sb[:, vh, :], False, True,
                           [nwT4.k[bp], kSb], [kb[b]])
                    CP("dve", fl(vnb2.t[0], 256), pb[b][:, 0:256], [kb[b]], [vnb2.k[0]])
                    yield
                    b = nb()
                    MM(pb[b][:, 0:256], qc, Sb[:].rearrange("p v d -> p (v d)"), True, True, [kI, kSb], [kb[b]])
                    TT("dve", qS2.t[0][:], pb[b][:, 0:256].rearrange("p (v d) -> p v d", d=128), bc_d(EGC, n, hv0),
                       ALU.mult, [kb[b], ktab], [qS2.k[0]])
                    b = nb()
                    for vh in range(2):
                        MM(pb[b][:, vh * 128:(vh + 1) * 128], AT4.t[bp][:, m0 + vh, :], vnb2.t[0][:, vh, :], True, True,
                           [AT4.k[bp], vnb2.k[0]], [kb[b]])
                    TT("dve", fl(o2.t[0], 256), pb[b][:, 0:256], fl(qS2.t[0], 256), ALU.add, [kb[b], qS2.k[0]],
                       [o2.k[0]])
                    yield
                    b = nb()
                    for vh in range(2):
                        MM(pb[b][:, vh * 128:(vh + 1) * 128], kdec4.t[bp][:, m0 + vh, :], vnb2.t[0][:, vh, :], True, True,
                           [kdec4.k[bp], vnb2.k[0]], [kb[b]])
                    TT("dve", S32[:], S32[:], bc_d(EGL, n, hv0), ALU.mult, [kS, kSb, ktab], [kS])
                    TT("dve", S32[:].rearrange("p v d -> p (v d)"), S32[:].rearrange("p v d -> p (v d)"),
                       pb[b][:, 0:256], ALU.add, [kS, kb[b]], [kS])
                    CP("pool", Sb[:], S32[:], [kS], [kSb])
                    yield
                    ACT(junk2.t[0][:], o2.t[0][:], AF.Square, [o2.k[0]], [junk2.k[0]])
                    P.op("dve", lambda e: e.tensor_reduce(out=ss2.t[0][:], in_=junk2.t[0][:],
                                                          axis=mybir.AxisListType.X, op=ALU.add),
                         [junk2.k[0]], [ss2.k[0]])
                    ACT(ss2.t[0][:], ss2.t[0][:], AF.Sqrt, [ss2.k[0]], [ss2.k[0]], bias=epsc[:, 1:2], scale=1.0 / HD)
                    P.op("dve", lambda e: e.reciprocal(out=ss2.t[0][:], in_=ss2.t[0][:]), [ss2.k[0]], [ss2.k[0]])
                    TT("pool", onb2.t[0][:], o2.t[0][:], ss2.t[0][:].unsqueeze(2).to_broadcast([128, 2, 128]), ALU.mult,
                       [o2.k[0], ss2.k[0]], [onb2.k[0]])
                    yield
                    b = nb()
                    for vh in range(2):
                        MM(pb[b][:, vh * 128:(vh + 1) * 128], onb2.t[0][:, vh, :], identb[:], True, True,
                           [onb2.k[0], k_c], [kb[b]])
                    ACT(zg2.t[0][:], zc, AF.Silu, [kZ], [zg2.k[0]])
                    STT("dve", yout, pb[b][:, 0:256].rearrange("p (v d) -> p v d", d=128), gw[:, 0:1], zg2.t[0][:],
                        ALU.mult, ALU.mult, [kb[b], ktab, zg2.k[0]], [kY])
                    yield

                nbatch = [0]

                def run_all(g):
                    for _ in g:
                        pass

                def interleave(gs):
                    gs = [g for g in gs if g is not None]
                    alive = [True] * len(gs)
                    while any(alive):
                        for i_, g in enumerate(gs):
                            if alive[i_]:
                                try:
                                    next(g)
                                except StopIteration:
                                    alive[i_] = False

                def chain(*gens):
                    for g in gens:
                        yield from g

                for hq in range(GQ):
                    DMA("sp", qT[:], QKVn[hq * 128:(hq + 1) * 128, :], writes=[kin])
                    DMA("sp", kT[:], QKVn[2048 + hq * 128:2048 + (hq + 1) * 128, :], writes=[kin])
                    DMA("sp", vT[:], QKVn[4096 + 2 * hq * 128:4096 + (2 * hq + 2) * 128, :].rearrange(
                        "(v p) t -> p v t", p=128), writes=[kin])
                    MEMSET("pool", S32[:], 0.0, [kS])
                    MEMSET("pool", Sb[:], 0.0, [kSb])
                    MEMSET("pool", sqT[:], 0.0, [ksin])
                    MEMSET("pool", skT[:], 0.0, [ksin])
                    MEMSET("pool", svT[:], 0.0, [ksin])
                    MEMSET("pool", szT[:], 0.0, [ksin])
                    CP("dve", sqT[:, 0:1], SN[:, hq:hq + 1], [k_sn], [ksin])
                    CP("dve", skT[:, 0:1], SN[:, 16 + hq:17 + hq], [k_sn], [ksin])
                    for vh in range(2):
                        CP("dve", svT[:, vh, 0:1], SN[:, 32 + 2 * hq + vh:33 + 2 * hq + vh], [k_sn], [ksin])
                        CP("dve", szT[:, vh, 0:1], HS[:, 64 + 2 * hq + vh:65 + 2 * hq + vh], [k_hs], [ksin])
                    batches = []
                    for n0 in range(0, NCH, 2):
                        cbs = []
                        for n in range(n0, min(n0 + 2, NCH)):
                            c0 = n * 128
                            cbs.append((n, qT[:, c0:c0 + 128], kT[:, c0:c0 + 128],
                                        (lambda vh, c0=c0: vT[:, vh, c0:c0 + 128])))
                        batches.append((n0, cbs, kin))
                    batches.append((NCH, [(NCH, sqT[:], skT[:], (lambda vh: svT[:, vh, :]))], ksin))

                    def start1(bi):
                        n0, cbs, kI = batches[bi]
                        bp = (nbatch[0] + bi) % 2
                        if bi < len(batches) - 1:
                            wz = len(cbs) * 128
                            DMA("sp", zB[bp][:, :, 0:wz],
                                HT[8192 + 2 * hq * 128:8192 + (2 * hq + 2) * 128, n0 * 128:n0 * 128 + wz].rearrange(
                                    "(v p) t -> p v t", p=128), writes=[kzB[bp]])
                        return stage1(bp, hq, cbs, kI)

                    def make2(bi):
                        n0, cbs, kI = batches[bi]
                        bp = (nbatch[0] + bi) % 2
                        gs = []
                        if bi < len(batches) - 1:
                            for ci, (n, qc, kc, vc) in enumerate(cbs):
                                c0 = n * 128
                                gs.append(stage2(bp, 0, hq, ci, n, qc, zB[bp][:, :, ci * 128:(ci + 1) * 128],
                                                 yTh[:, :, c0:c0 + 128], kzB[bp], kyT, kI))
                        else:
                            gs.append(stage2(bp, 0, hq, 0, NCH, sqT[:], szT[:], syT[:], ksin, ksy, ksin))
                        return chain(*gs)

                    run_all(start1(0))
                    for bi in range(len(batches)):
                        g1 = start1(bi + 1) if bi + 1 < len(batches) else None
                        if bi == len(batches) - 1:
                            DMA("pool", YT[2 * hq * 128:(2 * hq + 2) * 128, :].rearrange("(v p) t -> p v t", p=128),
                                yTh[:], reads=[kyT])
                            DMA("pool", dl_p[j, 2 * hq:2 * hq + 2].rearrange("v k d -> k v d"), S32[:], reads=[kS])
                            DMA("sp", S32[:], st_delta[j, 2 * hq:2 * hq + 2].rearrange("v k d -> k v d"), writes=[kS])
                            CP("pool", Sb[:], S32[:], [kS], [kSb])
                        interleave([g1, make2(bi)])
                    nbatch[0] += len(batches)
                    CP("dve", YS[:, 2 * hq:2 * hq + 2, :].rearrange("p v o -> p (v o)"), syT[:, :, 0], [ksy], [k_ys])
                    DMA("pool", dl_s[j, 2 * hq:2 * hq + 2].rearrange("v k d -> k v d"), S32[:], reads=[kS])
                P.barrier()

        def phase_GDN(l):
            import os
            KG = os.environ.get("KG", "c,g").split(",")
            if "c" in KG:
                phase_C(l)
            if "g" in KG:
                phase_G(l)

        def phase_O(l):
            even = (l % 2 == 0)
            j = l // 2
            KY = 16 if even else 32
            w_out = w_out_even[j] if even else w_out_odd[j]
            TO = 256 if even else 128
            last = (l == DEPTH - 1)
            Xsrc = xT_in if l == 0 else X
            Xdst = yT_out if last else X
            with contextlib.ExitStack() as s2:
                Wo = sbt(s2, "Wo", [128, KY, D], BF16)
                kWo = Tk()
                wst = [sbt(s2, "ow%d" % i, [128, D], F32) for i in range(2)]
                kws = [Tk(), Tk()]
                lg = sbt(s2, "lg", [128, KC], F32)
                lb = sbt(s2, "lb", [128, KC], F32)
                klg = Tk()
                DMA("sp", lg[:], ln_gT[l], writes=[klg])
                DMA("sp", lb[:], ln_bT[l], writes=[klg])
                for k in range(KY):
                    DMA("sp", wst[k % 2][:], w_out[k * 128:(k + 1) * 128, :], writes=[kws[k % 2]])
                    CP("pool" if k % 2 else "dve", Wo[:, k, :], wst[k % 2][:], [kws[k % 2]], [kWo])
                Yt = [sbt(s2, "Yt%d" % i, [128, KY, TO], BF16) for i in range(2)]
                kYt = [Tk(), Tk()]
                Xt = [sbt(s2, "Xt%d" % i, [128, KC, TO], F32) for i in range(2)]
                kXt = [Tk(), Tk()]
                rb = sbt(s2, "rb", [128, KC, TO], BF16)
                rsq = sbt(s2, "rsq", [128, KC, TO], BF16)
                krb = Tk()
                mean = sbt(s2, "mean", [128, TO], F32)
                rstd = sbt(s2, "rstd", [128, TO], F32)
                kst = Tk()
                tmp = [sbt(s2, "otmp%d" % i, [128, TO], F32) for i in range(2)]
                ktmp = [Tk(), Tk()]
                ntile = T // TO
                br = 0
                for ti in range(ntile + 1):
                    samp = (ti == ntile)
                    N = 1 if samp else TO
                    i = ti % 2
                    r = 1 if samp else 0
                    if samp:
                        yt_ap = YS[:, 0:KY, :]
                        kyt = k_ys
                        xt_t = XS
                        kxt = k_xs
                    else:
                        t0 = ti * TO
                        DMA("sp", Yt[i][:], YT[0:KY * 128, t0:t0 + TO].rearrange("(k p) t -> p k t", p=128),
                            writes=[kYt[i]])
                        DMA("sp", Xt[i][:], Xsrc[:, t0:t0 + TO].rearrange("(k p) t -> p k t", p=128),
                            writes=[kXt[i]])
                        yt_ap = Yt[i][:]
                        kyt = kYt[i]
                        xt_t = Xt[i]
                        kxt = kXt[i]
                    for d in range(KC):
                        b = br % 4
                        br += 1
                        for k in range(KY):
                            MM(pb[b][:, 0:N], Wo[:, k, d * 128:(d + 1) * 128], yt_ap[:, k, :], k == 0, k == KY - 1,
                               [kWo, kyt], [kb[b]])
                        ACT(xt_t[:, d, :], xt_t[:, d, :], AF.Copy, [kxt], [kxt], scale=float(ALPHA))
                        STT("dve", xt_t[:, d, :], pb[b][:, 0:N], modT[:, 32 + d, r:r + 1], xt_t[:, d, :],
                            ALU.mult, ALU.add, [kb[b], k_mod, kxt], [kxt])
                        CP("pool", rb[:, d, 0:N], xt_t[:, d, :], [kxt], [krb])
                        ACT(rsq[:, d, 0:N], xt_t[:, d, :], AF.Square, [kxt], [krb])
                    for d in range(KC):
                        MM(pb[4][:, 0:N], onesD[:], rb[:, d, 0:N], d == 0, d == KC - 1, [krb, k_c], [kb[4]])
                    for d in range(KC):
                        MM(pb[5][:, 0:N], onesD[:], rsq[:, d, 0:N], d == 0, d == KC - 1, [krb, k_c], [kb[5]])
                    CP("act", mean[:, 0:N], pb[4][:, 0:N], [kb[4]], [kst])
                    ACT(rstd[:, 0:N], pb[4][:, 0:N], AF.Square, [kb[4]], [kst])
                    TT("dve", rstd[:, 0:N], pb[5][:, 0:N], rstd[:, 0:N], ALU.subtract, [kb[5], kst], [kst])
                    ACT(rstd[:, 0:N], rstd[:, 0:N], AF.Sqrt, [kst], [kst], bias=epsc[:, 0:1])
                    P.op("dve", lambda e: e.reciprocal(out=rstd[:, 0:N], in_=rstd[:, 0:N]), [kst], [kst])
                    for d in range(KC):
                        q = d % 2
                        TT("pool", tmp[q][:, 0:N], xt_t[:, d, :], mean[:, 0:N], ALU.subtract, [kxt, kst], [ktmp[q]])
                        TT("dve", tmp[q][:, 0:N], tmp[q][:, 0:N], rstd[:, 0:N], ALU.mult, [ktmp[q], kst], [ktmp[q]])
                        ACT(xt_t[:, d, :], tmp[q][:, 0:N], AF.Identity, [ktmp[q], klg], [kxt],
                            bias=lb[:, d:d + 1], scale=lg[:, d:d + 1])
                    if samp:
                        if last:
                            DMA("pool", ysT_out[:, :, :], XS[:], reads=[k_xs])
                    else:
                        DMA("pool", Xdst[:, t0:t0 + TO].rearrange("(k p) t -> p k t", p=128), Xt[i][:],
                            reads=[kXt[i]])
                P.barrier()

        import os
        PH = os.environ.get("KPH", "M,UP,A,B,Bs,G,O").split(",")
        for l in range(DEPTH):
            if "M" in PH:
                phase_M(l)
            if "UP" in PH:
                phase_UP(l)
            if l % 2 == 0:
                if "A" in PH:
                    phase_A(l)
                if "B" in PH:
                    phase_B(l)
                if "Bs" in PH:
                    phase_Bs(l)
            else:
                if "G" in PH:
                    phase_GDN(l)
            if "O" in PH:
                phase_O(l)
        P.barrier()
    return nc


def make_consts():
    cp = np.zeros((128, 2048), np.float32)
    m = np.arange(128)[:, None]
    jj = np.arange(128)[None, :]
    cp[:, 0:128] = np.eye(128, dtype=np.float32)
    cp[:, 128:256] = np.where(m >= jj, -1.0, 0.0)
    i9 = np.arange(896)[None, :]
    cp[:, 256:1152] = np.where(m >= (i9 - 384), NEG, 0.0)
    cp[:, 1152:1280] = np.where(m <= jj, 1.0, 0.0)
    cp[:, 1280:1408] = np.where(jj > m, -NEG, 0.0)
    cp[:, 1408:1536] = np.where(m == 127, 1.0, 0.0)
    cp[:, 1536:1664] = np.where(m > jj, 1.0, 0.0)
    cp[:, 1664] = np.arange(128)
    cp[:, 1792:1920] = 1.0
    cp[:, 1920:2048] = np.where(m >= jj, 1.0, 0.0)
    cp2 = np.zeros((128, 7, 128), np.float32)
    cp2[:, 0] = (m // 16 == jj // 16)
    for si, sz in enumerate((16, 32, 64)):
        off = ((m // (2 * sz) == jj // (2 * sz)) & (m % (2 * sz) >= sz) & (jj % (2 * sz) < sz)).astype(np.float32)
        cp2[:, 1 + si] = off
        cp2[:, 4 + si] = off.T
    global CP2
    CP2 = cp2.reshape(128, 7 * 128)
    sel = np.zeros((32, GV, 128), np.float32)
    for h in range(GV):
        sel[h, h, :] = 1.0
    return cp, sel


def fm(v):
    n = v.shape[-1] // 128
    return np.ascontiguousarray(np.moveaxis(v.reshape(v.shape[:-1] + (n, 128)), -1, -2))


_CACHE = {}


def kernel(x_prompt, x_sample, c_prompt, c_sample, cache_k, cache_v, page_table,
           state_conv_a, state_conv_c, state_delta, w_ada, b_ada, ln_g, ln_b,
           w_in_even, conv_w_a, gn_g_a, gn_b_a, sb_bias, w_out_even,
           w_in_odd, conv_w_c, a_log_c, dt_bias_c, gnorm_w_c, w_out_odd):
    A = lambda v: np.ascontiguousarray(np.asarray(v))
    x_prompt = A(x_prompt); x_sample = A(x_sample); c_prompt = A(c_prompt); c_sample = A(c_sample)
    B, T, _ = x_prompt.shape
    NS = x_sample.shape[0]
    DEPTH = w_ada.shape[0]
    NE = (DEPTH + 1) // 2
    NO = DEPTH // 2
    NPOOL = cache_k.shape[1]
    NPG = page_table.shape[1]
    ncores = 8
    key = (T, NPG, NPOOL, DEPTH)
    if key not in _CACHE:
        _CACHE[key] = build(*key)
    nc = _CACHE[key]
    cp, sel = make_consts()
    NO1 = max(NO, 1)

    def pad_odd(a, shape):
        a = A(a)
        if a.shape[0] == 0:
            return np.zeros((1,) + tuple(shape), np.float32)
        return a

    shared = {
        "w_ada": A(w_ada),
        "b_adaT": fm(A(b_ada)),
        "ln_gT": fm(A(ln_g)), "ln_bT": fm(A(ln_b)),
        "w_in_even": A(w_in_even), "w_out_even": A(w_out_even),
        "conv_w_aT": np.ascontiguousarray(A(conv_w_a).reshape(NE, CONVW_A, 8, 128).transpose(0, 3, 2, 1)),
        "gn_gT": fm(A(gn_g_a)), "gn_bT": fm(A(gn_b_a)),
        "sb_biasB": np.ascontiguousarray(np.broadcast_to(A(sb_bias)[:, None, :], (NE, 128, SBH))),
        "cache_k": A(cache_k).reshape(NE, NPOOL * 128, 1024),
        "cache_v": A(cache_v).reshape(NE, NPOOL * 128, 1024),
        "cpack": cp, "selpack": sel, "cpack2": CP2,
    }
    if NO:
        shared.update({
            "w_in_odd": A(w_in_odd), "w_out_odd": A(w_out_odd),
            "conv_w_cT": np.ascontiguousarray(A(conv_w_c).reshape(NO, 4, 64, 128).transpose(0, 3, 2, 1)),
            "a_logB": np.ascontiguousarray(np.broadcast_to(A(a_log_c)[:, None, :], (NO, 128, GV))),
            "dt_biasB": np.ascontiguousarray(np.broadcast_to(A(dt_bias_c)[:, None, :], (NO, 128, GV))),
            "gnorm_wT": np.ascontiguousarray(A(gnorm_w_c)[:, :, None]),
        })
    else:
        shared.update({
            "w_in_odd": np.zeros((1, D, IN_ODD), np.float32), "w_out_odd": np.zeros((1, 2 * D, D), np.float32),
            "conv_w_cT": np.zeros((1, 128, 64, 4), np.float32), "a_logB": np.zeros((1, 128, GV), np.float32),
            "dt_biasB": np.zeros((1, 128, GV), np.float32), "gnorm_wT": np.zeros((1, 128, 1), np.float32),
        })
    in_maps = []
    for c in range(ncores):
        b = (c * B) // ncores
        s = c % NS
        m = dict(shared)
        m["xT"] = np.ascontiguousarray(x_prompt[b].T)
        m["xsT"] = fm(x_sample[s, 0])[:, :, None].copy()
        cc = np.stack([c_prompt[b], c_sample[s]], -1)
        m["cT"] = np.ascontiguousarray(cc.reshape(KC, 128, 2).transpose(1, 0, 2))
        m["ptB"] = np.ascontiguousarray(np.broadcast_to(A(page_table)[s][None, :], (128, NPG))).astype(np.int32)
        m["st_conv_aT"] = np.ascontiguousarray(A(state_conv_a)[:, s].transpose(0, 2, 1))
        if NO:
            m["st_conv_cT"] = np.ascontiguousarray(A(state_conv_c)[:, s].transpose(0, 2, 1))
            m["st_delta"] = np.ascontiguousarray(A(state_delta)[:, s])
        else:
            m["st_conv_cT"] = np.zeros((1, 8192, 3), np.float32)
            m["st_delta"] = np.zeros((1, GV, 128, 128), np.float32)
        in_maps.append(m)
    import os
    if os.environ.get("KTRACE"):
        res = run_bass_kernel_spmd(nc, in_maps, core_ids=list(range(ncores)), trace=True)
        print("EXEC_TIME_NS", res.exec_time_ns, flush=True)
    else:
        res = run_bass_kernel_spmd(nc, in_maps, core_ids=list(range(ncores)))
    R = res.results
    global LAST_R
    LAST_R = R
    cores_b = [(b * ncores) // B for b in range(B)]
    y_p = np.stack([R[c]["yT"].T for c in cores_b]).astype(np.float32)
    y_s = np.stack([R[s]["ysT"][:, :, 0].T.reshape(1, D) for s in range(NS)]).astype(np.float32)
    nk_p = np.stack([R[c]["nk_p"] for c in cores_b], 1).reshape(NE, B, T, SBH, HD)
    nv_p = np.stack([R[c]["nv_p"] for c in cores_b], 1).reshape(NE, B, T, SBH, HD)
    nk_s = np.stack([R[s]["nk_s"] for s in range(NS)], 1).reshape(NE, NS, 1, SBH, HD)
    nv_s = np.stack([R[s]["nv_s"] for s in range(NS)], 1).reshape(NE, NS, 1, SBH, HD)
    ca_p = np.stack([R[c]["ca_pT"].transpose(0, 2, 1) for c in cores_b], 1)
    ca_s = np.stack([R[s]["ca_sT"].transpose(0, 2, 1) for s in range(NS)], 1)
    cc_p = np.stack([R[c]["cc_pT"].transpose(0, 2, 1) for c in cores_b], 1)[:NO]
    cc_s = np.stack([R[s]["cc_sT"].transpose(0, 2, 1) for s in range(NS)], 1)[:NO]
    dl_p = np.stack([R[c]["dl_p"] for c in cores_b], 1)[:NO]
    dl_s = np.stack([R[s]["dl_s"] for s in range(NS)], 1)[:NO]
    f = lambda a: np.ascontiguousarray(a, dtype=np.float32)
    return tuple(f(a) for a in (y_p, y_s, nk_p, nv_p, nk_s, nv_s, ca_p, ca_s, cc_p, cc_s, dl_p, dl_s))
```

```python
import contextlib
import numpy as np
import concourse.bass as bass
import concourse.mybir as mybir
from concourse.bass_utils import run_bass_kernel_spmd

F32 = mybir.dt.float32
BF16 = mybir.dt.bfloat16
I32 = mybir.dt.int32
AF = mybir.ActivationFunctionType
ALU = mybir.AluOpType

D = 2048
KC = 16
W_A = 1024
W_B = 1024
IN_EVEN = 7168
IN_ODD = 12352
CONVW_A = 31
HD = 128
SBH = 8
GV = 32
GQ = 16
ALPHA = 8 ** 0.25
LN_EPS = 1e-5
RMS_EPS = 1e-6
NEG = -30000.0


class Tk:
    __slots__ = ("lastw", "readers")

    def __init__(self):
        self.lastw = []
        self.readers = []


class Prog:
    def __init__(self, nc, st, n_dma_ch=12):
        self.nc = nc
        self.eng = {"pe": nc.tensor, "act": nc.scalar, "dve": nc.vector, "pool": nc.gpsimd, "sp": nc.sync}
        self.cnt = {e: 0 for e in ("pe", "act", "dve", "pool")}
        self.known = {e: {} for e in self.eng}
        self.n_dma_ch = n_dma_ch
        self.chcnt = [0] * n_dma_ch
        self.chrr = {"sp": 0, "pool": 0}
        self.chown = {"sp": list(range(0, n_dma_ch - 4)), "pool": list(range(n_dma_ch - 4, n_dma_ch))}
        self.sems = {}
        for n in ["pe", "act", "dve", "pool"] + ["d%d" % c for c in range(n_dma_ch)]:
            self.sems[n] = st.enter_context(nc.semaphore("s_" + n))

    def _deps(self, eng, reads, writes):
        w = {}
        for t in reads:
            for s, v in t.lastw:
                if w.get(s, 0) < v:
                    w[s] = v
        for t in writes:
            for s, v in t.lastw:
                if w.get(s, 0) < v:
                    w[s] = v
            for s, v in t.readers:
                if w.get(s, 0) < v:
                    w[s] = v
        kn = self.known[eng]
        out = []
        for s, v in w.items():
            if eng == "pe" and s == "pe":
                continue
            if kn.get(s, 0) >= v:
                continue
            kn[s] = v
            out.append((s, v))
        return out

    def _mark(self, tok, reads, writes):
        for t in writes:
            t.lastw = [tok]
            t.readers = []
        for t in reads:
            if t not in writes:
                t.readers.append(tok)
                if len(t.readers) > 40:
                    m = {}
                    for s, v in t.readers:
                        if m.get(s, 0) < v:
                            m[s] = v
                    t.readers = list(m.items())

    def op(self, eng, fn, reads=(), writes=()):
        e = self.eng[eng]
        for s, v in self._deps(eng, reads, writes):
            e.wait_ge(self.sems[s], v)
        self.cnt[eng] += 1
        tok = (eng, self.cnt[eng])
        self._mark(tok, reads, writes)
        fn(e).then_inc(self.sems[eng], 1)
        return tok

    def dma(self, q, fn, reads=(), writes=()):
        e = self.eng[q]
        chs = self.chown[q]
        c = chs[self.chrr[q] % len(chs)]
        self.chrr[q] += 1
        waits = self._deps(q, reads, writes)
        sname = "d%d" % c
        prev = self.chcnt[c] * 16
        kn = self.known[q]
        if prev > 0 and kn.get(sname, 0) < prev:
            kn[sname] = prev
            waits.append((sname, prev))
        for s, v in waits:
            e.wait_ge(self.sems[s], v)
        self.chcnt[c] += 1
        tok = (sname, self.chcnt[c] * 16)
        self._mark(tok, reads, writes)
        fn(e).then_inc(self.sems[sname], 16)
        return tok

    def barrier(self):
        allw = []
        for c in range(self.n_dma_ch):
            if self.chcnt[c]:
                allw.append(("d%d" % c, self.chcnt[c] * 16))
        for en in ("pe", "act", "dve", "pool"):
            if self.cnt[en]:
                allw.append((en, self.cnt[en]))
        for en, e in self.eng.items():
            kn = self.known[en]
            for s, v in allw:
                if kn.get(s, 0) >= v:
                    continue
                kn[s] = v
                e.wait_ge(self.sems[s], v)


def build(T, NPG, NPOOL, DEPTH):
    NE = (DEPTH + 1) // 2
    NO = DEPTH // 2
    NTT = T // 512
    NB = T // 128
    assert T % 512 == 0
    nc = bass.Bass("TRN2", target_bir_lowering=False)

    def din(name, shape, dt=F32):
        return nc.dram_tensor(name, list(shape), dt, kind="ExternalInput").ap()

    def dout(name, shape, dt=F32):
        return nc.dram_tensor(name, list(shape), dt, kind="ExternalOutput").ap()

    def dscr(name, shape, dt=F32):
        return nc.dram_tensor(name, list(shape), dt, kind="Internal").ap()

    NO1 = max(NO, 1)
    xT_in = din("xT", [D, T])
    xsT_in = din("xsT", [128, KC, 1])
    cT_in = din("cT", [128, KC, 2])
    w_ada = din("w_ada", [DEPTH, D, 3 * D])
    b_adaT = din("b_adaT", [DEPTH, 128, 48])
    ln_gT = din("ln_gT", [DEPTH, 128, KC])
    ln_bT = din("ln_bT", [DEPTH, 128, KC])
    w_in_even = din("w_in_even", [NE, D, IN_EVEN])
    w_out_even = din("w_out_even", [NE, D, D])
    conv_w_aT = din("conv_w_aT", [NE, 128, 8, CONVW_A])
    gn_gT = din("gn_gT", [NE, 128, 8])
    gn_bT = din("gn_bT", [NE, 128, 8])
    sb_biasB = din("sb_biasB", [NE, 128, SBH])
    cache_k = din("cache_k", [NE, NPOOL * 128, 1024])
    cache_v = din("cache_v", [NE, NPOOL * 128, 1024])
    ptB = din("ptB", [128, NPG], I32)
    st_conv_aT = din("st_conv_aT", [NE, W_A, 30])
    w_in_odd = din("w_in_odd", [NO1, D, IN_ODD])
    w_out_odd = din("w_out_odd", [NO1, 2 * D, D])
    conv_w_cT = din("conv_w_cT", [NO1, 128, 64, 4])
    a_logB = din("a_logB", [NO1, 128, GV])
    dt_biasB = din("dt_biasB", [NO1, 128, GV])
    gnorm_wT = din("gnorm_wT", [NO1, 128, 1])
    st_conv_cT = din("st_conv_cT", [NO1, 8192, 3])
    st_delta = din("st_delta", [NO1, GV, 128, 128])
    cpack = din("cpack", [128, 2048])
    cpack2 = din("cpack2", [128, 7 * 128])
    selpack = din("selpack", [32, GV, 128])
    yT_out = dout("yT", [D, T])
    ysT_out = dout("ysT", [128, KC, 1])
    nk_p = dout("nk_p", [NE, T, 1024])
    nv_p = dout("nv_p", [NE, T, 1024])
    nk_s = dout("nk_s", [NE, 1, 1024])
    nv_s = dout("nv_s", [NE, 1, 1024])
    ca_pT = dout("ca_pT", [NE, W_A, 30])
    ca_sT = dout("ca_sT", [NE, W_A, 30])
    cc_pT = dout("cc_pT", [NO1, 8192, 3])
    cc_sT = dout("cc_sT", [NO1, 8192, 3])
    dl_p = dout("dl_p", [NO1, GV, 128, 128])
    dl_s = dout("dl_s", [NO1, GV, 128, 128])
    import os
    DBG = bool(os.environ.get("KDBG"))
    if DBG:
        dbg_mod = dout("dbg_mod", [DEPTH, 128, 48, 2])
        dbg32 = dout("dbg32", [2, 12, 128, 128])
        dbg16 = dout("dbg16", [2, 12, 128, 128], BF16)
    X = dscr("Xscr", [D, T])
    HT = dscr("HTscr", [IN_ODD if NO else IN_EVEN, T])
    YT = dscr("YTscr", [2 * D, T], BF16)
    Vb = dscr("Vbscr", [T, 1024], BF16)
    QKVn = dscr("QKVn", [8192, T], BF16)

    with contextlib.ExitStack() as st:
        P = Prog(nc, st)

        uid = [0]

        def sbt(stack, name, shape, dt):
            uid[0] += 1
            return stack.enter_context(nc.sbuf_tensor("%s_%d" % (name, uid[0]), list(shape), dt))

        pb = [st.enter_context(nc.psum_tensor("pb%d" % i, [128, 512], F32)) for i in range(8)]
        kb = [Tk() for _ in range(8)]

        def ACT(out, in_, func, reads, writes, bias=None, scale=None, accum=None):
            kw = {}
            if bias is not None:
                kw["bias"] = bias
            if scale is not None:
                kw["scale"] = scale
            if accum is not None:
                kw["accum_out"] = accum
            return P.op("act", lambda e: e.activation(out=out, in_=in_, func=func, **kw), reads, writes)

        def TT(eng, out, a, b, op, reads, writes):
            return P.op(eng, lambda e: e.tensor_tensor(out=out, in0=a, in1=b, op=op), reads, writes)

        def TS(eng, out, a, s1, s2, op0, op1, reads, writes):
            if s2 is None:
                return P.op(eng, lambda e: e.tensor_scalar(out=out, in0=a, scalar1=s1, scalar2=None, op0=op0),
                            reads, writes)
            return P.op(eng, lambda e: e.tensor_scalar(out=out, in0=a, scalar1=s1, scalar2=s2, op0=op0, op1=op1),
                        reads, writes)

        def STT(eng, out, in0, scalar, in1, op0, op1, reads, writes):
            return P.op(eng, lambda e: e.scalar_tensor_tensor(out=out, in0=in0, scalar=scalar, in1=in1,
                                                              op0=op0, op1=op1), reads, writes)

        def CP(eng, out, in_, reads, writes):
            if eng == "act":
                return P.op("act", lambda e: e.activation(out=out, in_=in_, func=AF.Copy), reads, writes)
            return P.op(eng, lambda e: e.tensor_copy(out=out, in_=in_), reads, writes)

        def MM(out, lhsT, rhs, start, stop, reads, writes):
            return P.op("pe", lambda e: e.matmul(out, lhsT=lhsT, rhs=rhs, start=start, stop=stop,
                                                 skip_group_check=True), reads, writes)

        def DMA(q, out, in_, reads=(), writes=()):
            return P.dma(q, lambda e: e.dma_start(out=out, in_=in_), reads, writes)

        def MEMSET(eng, ap, val, writes):
            return P.op(eng, lambda e: e.memset(ap, val), (), writes)

        evac_rr = [0]

        def EVAC(out, in_, reads, writes):
            evac_rr[0] += 1
            if evac_rr[0] % 2:
                return CP("act", out, in_, reads, writes)
            return CP("dve", out, in_, reads, writes)

        cpk = sbt(st, "cpk", [128, 2048], F32)
        k_c = Tk()
        identf = cpk[:, 0:128]
        trif = cpk[:, 1152:1280]
        sellast = cpk[:, 1408:1536]
        identb = sbt(st, "identb", [128, 128], BF16)
        nuincl = sbt(st, "nuincl", [128, 128], BF16)
        mkb = sbt(st, "mkb", [128, 896], BF16)
        nones = sbt(st, "nones", [128, 128], BF16)
        onesG = sbt(st, "onesG", [128, 128], BF16)
        onesD = sbt(st, "onesD", [128, 128], BF16)
        ones1 = sbt(st, "ones1", [128, 128], BF16)
        upbig = sbt(st, "upbig", [128, 128], BF16)
        lowstrict = sbt(st, "lowstrict", [128, 128], F32)
        DMA("sp", cpk[:], cpack[:, :], writes=[k_c])
        CP("dve", identb[:], cpk[:, 0:128], [k_c], [k_c])
        CP("dve", nuincl[:], cpk[:, 128:256], [k_c], [k_c])
        CP("dve", mkb[:], cpk[:, 256:1152], [k_c], [k_c])
        CP("dve", upbig[:], cpk[:, 1280:1408], [k_c], [k_c])
        CP("dve", lowstrict[:], cpk[:, 1536:1664], [k_c], [k_c])
        MEMSET("pool", nones[:], -1.0, [k_c])
        MEMSET("pool", onesG[:], 1.0 / 128, [k_c])
        MEMSET("pool", onesD[:], 1.0 / D, [k_c])
        MEMSET("pool", ones1[:], 1.0, [k_c])
        zerob = sbt(st, "zerob", [128, 128], BF16)
        MEMSET("pool", zerob[:], 0.0, [k_c])
        epsc = sbt(st, "epsc", [128, 2], F32)
        MEMSET("pool", epsc[:, 0:1], LN_EPS, [k_c])
        MEMSET("pool", epsc[:, 1:2], RMS_EPS, [k_c])
        modT = sbt(st, "modT", [128, 48, 2], F32)
        k_mod = Tk()
        XS = sbt(st, "XS", [128, KC, 1], F32)
        k_xs = Tk()
        HS = sbt(st, "HS", [128, 100], F32)
        k_hs = Tk()
        YS = sbt(st, "YS", [128, 32, 1], BF16)
        k_ys = Tk()
        SN = sbt(st, "SN", [128, 64], F32)
        k_sn = Tk()
        DMA("sp", XS[:], xsT_in[:, :, :], writes=[k_xs])
        P.barrier()

        def phase_M(l):
            with contextlib.ExitStack() as s2:
                wst = [sbt(s2, "mw%d" % i, [128, KC, 128], F32) for i in range(2)]
                kw = [Tk(), Tk()]
                ct = sbt(s2, "mct", [128, KC, 2], F32)
                sc = sbt(s2, "msc", [128, KC, 2], F32)
                bT = sbt(s2, "mbT", [128, 48], F32)
                k1 = Tk()
                DMA("sp", ct[:], cT_in[:, :, :], writes=[k1])
                DMA("sp", bT[:], b_adaT[l], writes=[k1])
                ACT(sc[:], ct[:], AF.Silu, [k1], [k1])
                for jj in range(48):
                    i = jj % 2
                    DMA("sp", wst[i][:], w_ada[l][:, jj * 128:(jj + 1) * 128].rearrange("(k p) c -> p k c", p=128),
                        writes=[kw[i]])
                    b = jj % 4
                    for k in range(KC):
                        MM(pb[b][:, 0:2], wst[i][:, k, :], sc[:, k, :], k == 0, k == KC - 1, [kw[i], k1], [kb[b]])
                    TT("dve", modT[:, jj, :], pb[b][:, 0:2], bT[:, jj:jj + 1].to_broadcast([128, 2]), ALU.add,
                       [kb[b], k1], [k_mod])
                TS("dve", modT[:, 16:48, :], modT[:, 16:48, :], 1.0, None, ALU.add, None, [k_mod], [k_mod])
                if DBG:
                    DMA("sp", dbg_mod[l], modT[:], reads=[k_mod])
                P.barrier()

        def phase_UP(l):
            even = (l % 2 == 0)
            j = l // 2
            w_in = w_in_even[j] if even else w_in_odd[j]
            Xsrc = xT_in if l == 0 else X
            TH = min(T, 2048)
            with contextlib.ExitStack() as s2:
                U = sbt(s2, "U", [128, KC, T], BF16)
                kU = Tk()
                Us = sbt(s2, "Us", [128, KC, 1], BF16)
                with contextlib.ExitStack() as s3:
                    xst = [sbt(s3, "xst%d" % i, [128, 512], F32) for i in range(3)]
                    kx = [Tk() for _ in range(3)]
                    n = 0
                    for k in range(KC):
                        for t in range(NTT):
                            i = n % 3
                            n += 1
                            DMA("sp", xst[i][:], Xsrc[k * 128:(k + 1) * 128, t * 512:(t + 1) * 512], writes=[kx[i]])
                            ACT(U[:, k, t * 512:(t + 1) * 512], xst[i][:], AF.Identity, [kx[i], k_mod], [kU],
                                bias=modT[:, k, 0:1], scale=modT[:, 16 + k, 0:1])
                        ACT(Us[:, k, :], XS[:, k, :], AF.Identity, [k_xs, k_mod], [kU],
                            bias=modT[:, k, 1:2], scale=modT[:, 16 + k, 1:2])
                    P.barrier()
                import os
                KUP = os.environ.get("KUP", "i,ii,s,s2,nk,vb").split(",")
                if even:
                    chunks = [(c, 128) for c in range(0, 5120, 128)] + [(c, 128) for c in range(6144, 7168, 128)]
                else:
                    chunks = [(c, 128) for c in range(0, 12288, 128)] + [(12288, 64)]
                if "i" not in KUP:
                    chunks = []
                with contextlib.ExitStack() as s3:
                    wst = [sbt(s3, "pw%d" % i, [128, KC, 128], F32) for i in range(2)]
                    wb = [sbt(s3, "pwb%d" % i, [128, KC, 128], BF16) for i in range(2)]
                    hst = [sbt(s3, "ph%d" % i, [128, TH], F32) for i in range(2)]
                    kws = [Tk(), Tk()]
                    kwb = [Tk(), Tk()]
                    kh = [Tk(), Tk()]
                    hi = 0
                    br = 0
                    for ci, (c0, M) in enumerate(chunks):
                        i = ci % 2
                        DMA("sp", wst[i][:, :, 0:M], w_in[:, c0:c0 + M].rearrange("(k p) c -> p k c", p=128),
                            writes=[kws[i]])
                        CP("pool", wb[i][:, :, 0:M], wst[i][:, :, 0:M], [kws[i]], [kwb[i]])
                        for half in range(T // TH):
                            hb = hi % 2
                            hi += 1
                            for tt in range(TH // 512):
                                t = half * (TH // 512) + tt
                                b = (0, 1, 2, 3, 5, 6, 7)[br % 7]
                                br += 1
                                for k in range(KC):
                                    MM(pb[b][0:M, :], wb[i][:, k, 0:M], U[:, k, t * 512:(t + 1) * 512],
                                       k == 0, k == KC - 1, [kwb[i], kU], [kb[b]])
                                EVAC(hst[hb][0:M, tt * 512:(tt + 1) * 512], pb[b][0:M, :], [kb[b]], [kh[hb]])
                            DMA("pool", HT[c0:c0 + M, half * TH:(half + 1) * TH], hst[hb][0:M, :], reads=[kh[hb]])
                        if "s" in KUP:
                            for k in range(KC):
                                MM(pb[4][0:M, 0:1], wb[i][:, k, 0:M], Us[:, k, :], k == 0, k == KC - 1,
                                   [kwb[i], kU], [kb[4]])
                            CP("act", HS[0:M, c0 // 128:c0 // 128 + 1], pb[4][0:M, 0:1], [kb[4]], [k_hs])
                    P.barrier()
                if even and "ii" in KUP:
                    with contextlib.ExitStack() as s3:
                        W2 = sbt(s3, "W2", [128, KC, 512], BF16)
                        kW2 = Tk()
                        st2 = [sbt(s3, "st2%d" % i, [128, 512], F32) for i in range(2)]
                        ks2 = [Tk(), Tk()]
                        ev = [sbt(s3, "ev%d" % i, [128, 512], F32) for i in range(2)]
                        kev = [Tk(), Tk()]
                        evb = [sbt(s3, "evb%d" % i, [128, 512], BF16) for i in range(2)]
                        kevb = [Tk(), Tk()]
                        evs = sbt(s3, "evs", [1, 512], F32)
                        kevs = Tk()
                        n = 0
                        for ct in range(4):
                            c0 = 4096 + ct * 512
                            isv = ct >= 2
                            dst_p = nv_p if isv else nk_p
                            dst_s = nv_s if isv else nk_s
                            oc = (ct % 2) * 512
                            for k in range(KC):
                                DMA("sp", st2[k % 2][:], w_in[k * 128:(k + 1) * 128, c0:c0 + 512],
                                    writes=[ks2[k % 2]])
                                CP("pool" if k % 2 else "dve", W2[:, k, :], st2[k % 2][:], [ks2[k % 2]], [kW2])
                            for tb in range(NB):
                                b = (0, 1, 2, 3, 5, 6, 7)[n % 7]
                                e_i = n % 2
                                n += 1
                                for k in range(KC):
                                    MM(pb[b][:, :], U[:, k, tb * 128:(tb + 1) * 128], W2[:, k, :],
                                       k == 0, k == KC - 1, [kU, kW2], [kb[b]])
                                CP("act", ev[e_i][:], pb[b][:, :], [kb[b]], [kev[e_i]])
                                if "nk" in KUP:
                                    DMA("pool", dst_p[j, tb * 128:(tb + 1) * 128, oc:oc + 512], ev[e_i][:],
                                        reads=[kev[e_i]])
                                if isv and "vb" in KUP:
                                    CP("dve", evb[e_i][:], ev[e_i][:], [kev[e_i]], [kevb[e_i]])
                                    DMA("sp", Vb[tb * 128:(tb + 1) * 128, oc:oc + 512], evb[e_i][:],
                                        reads=[kevb[e_i]])
                            if "s2" in KUP:
                                for k in range(KC):
                                    MM(pb[4][0:1, :], Us[:, k, 0:1], W2[:, k, :], k == 0, k == KC - 1,
                                       [kU, kW2], [kb[4]])
                                CP("act", evs[:], pb[4][0:1, :], [kb[4]], [kevs])
                                DMA("pool", dst_s[j, 0:1, oc:oc + 512], evs[:], reads=[kevs])
                        P.barrier()

        def phase_A(l):
            j = l // 2
            with contextlib.ExitStack() as s2:
                G = sbt(s2, "G", [128, 32 + T], BF16)
                kG = Tk()
                Gs = sbt(s2, "Gs", [128, 32], BF16)
                hs32 = sbt(s2, "hs32", [128, 32], F32)
                khs = Tk()
                cw = sbt(s2, "cw", [128, 8, CONVW_A], F32)
                gg = sbt(s2, "gg", [128, 8], F32)
                gb = sbt(s2, "gb", [128, 8], F32)
                kcw = Tk()
                Dg = sbt(s2, "Dg", [128, CONVW_A, 128], BF16)
                kDg = Tk()
                va = [sbt(s2, "va%d" % i, [128, 512], F32) for i in range(2)]
                gl = [sbt(s2, "gl%d" % i, [128, 512], F32) for i in range(2)]
                gt = [sbt(s2, "gt%d" % i, [128, 512], F32) for i in range(2)]
                kin = [Tk(), Tk()]
                kgt = [Tk(), Tk()]
                g32 = sbt(s2, "g32", [128, 512], F32)
                kg32 = Tk()
                yf = sbt(s2, "yf", [128, 512], F32)
                ybf = sbt(s2, "ybf", [128, 512], BF16)
                ysq = sbt(s2, "ysq", [128, 512], BF16)
                t1 = sbt(s2, "t1", [128, 512], F32)
                t2 = sbt(s2, "t2", [128, 512], F32)
                yo = [sbt(s2, "yo%d" % i, [128, 512], BF16) for i in range(2)]
                kyo = [Tk(), Tk()]
                kt = Tk()
                DMA("sp", cw[:], conv_w_aT[j], writes=[kcw])
                DMA("sp", gg[:], gn_gT[j], writes=[kcw])
                DMA("sp", gb[:], gn_bT[j], writes=[kcw])
                n = 0

                def post(N, ypsum, kyp, gate_ap, kgate, cc, out_bf, kout):
                    CP("act", yf[:, 0:N], ypsum, [kyp], [kt])
                    CP("dve", ybf[:, 0:N], yf[:, 0:N], [kt], [kt])
                    ACT(ysq[:, 0:N], yf[:, 0:N], AF.Square, [kt], [kt])
                    MM(pb[5][:, 0:N], onesG[:], ybf[:, 0:N], True, True, [kt, k_c], [kb[5]])
                    MM(pb[6][:, 0:N], onesG[:], ysq[:, 0:N], True, True, [kt, k_c], [kb[6]])
                    CP("act", t2[:, 0:N], pb[5][:, 0:N], [kb[5]], [kt])
                    TT("dve", t1[:, 0:N], yf[:, 0:N], t2[:, 0:N], ALU.subtract, [kt], [kt])
                    ACT(t2[:, 0:N], t2[:, 0:N], AF.Square, [kt], [kt])
                    TT("dve", t2[:, 0:N], pb[6][:, 0:N], t2[:, 0:N], ALU.subtract, [kt, kb[6]], [kt])
                    ACT(t2[:, 0:N], t2[:, 0:N], AF.Sqrt, [kt], [kt], bias=epsc[:, 0:1])
                    P.op("dve", lambda e: e.reciprocal(out=t2[:, 0:N], in_=t2[:, 0:N]), [kt], [kt])
                    TT("dve", t1[:, 0:N], t1[:, 0:N], t2[:, 0:N], ALU.mult, [kt], [kt])
                    ACT(t1[:, 0:N], t1[:, 0:N], AF.Silu, [kt, kcw], [kt], bias=gb[:, cc:cc + 1],
                        scale=gg[:, cc:cc + 1])
                    ACT(t2[:, 0:N], gate_ap, AF.Silu, [kgate], [kt])
                    TT("dve", out_bf, t1[:, 0:N], t2[:, 0:N], ALU.mult, [kt], [kout])

                for cc in range(8):
                    r0 = cc * 128
                    for kk in range(CONVW_A):
                        TS("dve" if kk % 2 else "pool", Dg[:, kk, :], identf, cw[:, cc, kk:kk + 1], None, ALU.mult,
                           None, [k_c, kcw], [kDg])
                    MEMSET("pool", G[:, 0:32], 0.0, [kG])
                    for t in range(NTT):
                        i = n % 2
                        n += 1
                        DMA("sp", va[i][:], HT[r0:r0 + 128, t * 512:(t + 1) * 512], writes=[kin[i]])
                        DMA("sp", gl[i][:], HT[1024 + r0:1024 + r0 + 128, t * 512:(t + 1) * 512], writes=[kin[i]])
                        ACT(gl[i][:], gl[i][:], AF.Sigmoid, [kin[i]], [kin[i]])
                        TT("dve", g32[:], va[i][:], gl[i][:], ALU.mult, [kin[i]], [kg32])
                        CP("pool", G[:, 32 + t * 512:32 + (t + 1) * 512], g32[:], [kg32], [kG])
                        if t == NTT - 1:
                            DMA("pool", ca_pT[j, r0:r0 + 128, :], g32[:, 482:512], reads=[kg32])
                    for t in range(NTT):
                        i = n % 2
                        n += 1
                        DMA("sp", gt[i][:], HT[2048 + r0:2048 + r0 + 128, t * 512:(t + 1) * 512], writes=[kgt[i]])
                        b = t % 2
                        for kk in range(CONVW_A):
                            MM(pb[b][:, :], Dg[:, kk, :], G[:, 2 + t * 512 + kk:2 + t * 512 + kk + 512],
                               kk == 0, kk == CONVW_A - 1, [kDg, kG], [kb[b]])
                        post(512, pb[b][:, :], kb[b], gt[i][:], kgt[i], cc, yo[i][:], kyo[i])
                        DMA("pool", YT[r0:r0 + 128, t * 512:(t + 1) * 512], yo[i][:], reads=[kyo[i]])
                    DMA("sp", hs32[:, 0:30], st_conv_aT[j, r0:r0 + 128, :], writes=[khs])
                    ACT(t2[:, 0:1], HS[:, 8 + cc:9 + cc], AF.Sigmoid, [k_hs], [kt])
                    TT("dve", hs32[:, 30:31], HS[:, cc:cc + 1], t2[:, 0:1], ALU.mult, [k_hs, kt], [khs])
                    CP("dve", Gs[:, 0:31], hs32[:, 0:31], [khs], [khs])
                    DMA("pool", ca_sT[j, r0:r0 + 128, :], hs32[:, 1:31], reads=[khs])
                    for kk in range(CONVW_A):
                        MM(pb[2][:, 0:1], Dg[:, kk, :], Gs[:, kk:kk + 1], kk == 0, kk == CONVW_A - 1,
                           [kDg, khs], [kb[2]])
                    post(1, pb[2][:, 0:1], kb[2], HS[:, 16 + cc:17 + cc], k_hs, cc, YS[:, cc, :], k_ys)
                P.barrier()

        def phase_B(l):
            j = l // 2
            with contextlib.ExitStack() as s2:
                sbb = sbt(s2, "sbb", [128, SBH], F32)
                ksbb = Tk()
                DMA("sp", sbb[:], sb_biasB[j], writes=[ksbb])

                class Str:
                    pass

                def mkstream(sid):
                    S_ = Str()
                    S_.QT = sbt(s2, "QT", [128, T], BF16)
                    S_.KT = sbt(s2, "KT", [128, T], BF16)
                    S_.Vh = sbt(s2, "Vh", [128, NB, 128], BF16)
                    S_.kq = Tk()
                    S_.stg = [sbt(s2, "stg%d" % i, [128, 512], F32) for i in range(2)]
                    S_.kstg = [Tk(), Tk()]
                    S_.bg = [sbt(s2, "bg%d" % i, [128, 512], F32) for i in range(2)]
                    S_.kbg = [Tk(), Tk()]
                    S_.E = [sbt(s2, "E%d" % i, [128, 512], F32) for i in range(1)]
                    S_.SP32 = [sbt(s2, "SP%d" % i, [128, 512], F32) for i in range(1)]
                    S_.SPb = [sbt(s2, "SPb%d" % i, [128, 512], BF16) for i in range(1)]
                    S_.Wt = [sbt(s2, "Wt%d" % i, [128, 512], BF16) for i in range(1)]
                    S_.kE = [Tk(), Tk()]
                    S_.kSP = [Tk(), Tk()]
                    S_.kSPb = [Tk(), Tk()]
                    S_.kWt = [Tk(), Tk()]
                    S_.CAR = sbt(s2, "CAR", [128, 512], F32)
                    S_.CARb = [sbt(s2, "CARb%d" % i, [128, 512], BF16) for i in range(2)]
                    S_.kCAR = Tk()
                    S_.kCARb = [Tk(), Tk()]
                    S_.yo = [sbt(s2, "byo%d" % i, [128, 512], BF16) for i in range(2)]
                    S_.kyo = [Tk(), Tk()]
                    S_.zb = [2 * sid]
                    S_.ob = [2 * sid + 1]
                    S_.n = 0
                    S_.nblk = 0
                    return S_

                def head_stream(S_, h):
                    for t in range(NTT):
                        for which in range(2):
                            i = S_.n % 2
                            S_.n += 1
                            row = (3072 if which == 0 else 4096) + h * 128
                            DMA("sp", S_.stg[i][:], HT[row:row + 128, t * 512:(t + 1) * 512], writes=[S_.kstg[i]])
                            if which == 0:
                                ACT(S_.QT[:, t * 512:(t + 1) * 512], S_.stg[i][:], AF.Copy, [S_.kstg[i]], [S_.kq],
                                    scale=float(HD ** -0.5))
                            else:
                                CP("dve", S_.KT[:, t * 512:(t + 1) * 512], S_.stg[i][:], [S_.kstg[i]], [S_.kq])
                        yield
                    DMA("sp", S_.Vh[:], Vb[:, h * 128:(h + 1) * 128].rearrange("(nb p) d -> p nb d", p=128),
                        writes=[S_.kq])
                    for qt in range(NTT):
                        q0 = qt * 512
                        gi = qt % 2
                        DMA("sp", S_.bg[gi][:], HT[6144 + h * 128:6144 + (h + 1) * 128, q0:q0 + 512],
                            writes=[S_.kbg[gi]])
                        ACT(S_.bg[gi][:], S_.bg[gi][:], AF.Silu, [S_.kbg[gi]], [S_.kbg[gi]])
                        ob = S_.ob[0]
                        blocks = list(range(q0 // 128 + 3, -1, -1))
                        for bi, kbk in enumerate(blocks):
                            r = kbk - q0 // 128
                            zi = 0
                            S_.nblk += 1
                            zb = S_.zb[zi]
                            diag = r >= 0
                            MM(pb[zb][:, :], S_.KT[:, kbk * 128:(kbk + 1) * 128], S_.QT[:, q0:q0 + 512], True, False,
                               [S_.kq], [kb[zb]])
                            if diag:
                                MM(pb[zb][:, :], identb[:], mkb[:, 384 - r * 128:384 - r * 128 + 512], False, False,
                                   [k_c], [kb[zb]])
                            ACT(S_.E[zi][:], pb[zb][:, :], AF.Exp, [kb[zb], ksbb], [S_.kE[zi]], bias=sbb[:, h:h + 1])
                            yield
                            ACT(S_.SP32[zi][:], S_.E[zi][:], AF.Ln, [S_.kE[zi]], [S_.kSP[zi]], bias=1.0)
                            CP("dve", S_.SPb[zi][:], S_.SP32[zi][:], [S_.kSP[zi]], [S_.kSPb[zi]])
                            MM(pb[zb][:, :], nuincl[:], S_.SPb[zi][:], False, bi == 0, [S_.kSPb[zi], k_c], [kb[zb]])
                            if bi > 0:
                                MM(pb[zb][:, :], nones[:], S_.CARb[(bi - 1) % 2][:], False, True,
                                   [S_.kCARb[(bi - 1) % 2], k_c], [kb[zb]])
                            yield
                            ACT(S_.Wt[zi][:], pb[zb][:, :], AF.Exp, [kb[zb], ksbb], [S_.kWt[zi]], bias=sbb[:, h:h + 1])
                            MM(pb[ob][:, :], S_.Vh[:, kbk, :], S_.Wt[zi][:], bi == 0, bi == len(blocks) - 1,
                               [S_.kq, S_.kWt[zi]], [kb[ob]])
                            if bi < len(blocks) - 1:
                                if bi == 0:
                                    CP("pool", S_.CAR[:], S_.SP32[zi][:], [S_.kSP[zi]], [S_.kCAR])
                                else:
                                    TT("pool", S_.CAR[:], S_.CAR[:], S_.SP32[zi][:], ALU.add, [S_.kSP[zi], S_.kCAR],
                                       [S_.kCAR])
                                CP("pool", S_.CARb[bi % 2][:], S_.CAR[:], [S_.kCAR], [S_.kCARb[bi % 2]])
                            yield
                        TT("dve", S_.yo[gi][:], pb[ob][:, :], S_.bg[gi][:], ALU.mult, [kb[ob], S_.kbg[gi]],
                           [S_.kyo[gi]])
                        DMA("pool", YT[1024 + h * 128:1024 + (h + 1) * 128, q0:q0 + 512], S_.yo[gi][:],
                            reads=[S_.kyo[gi]])
                        yield

                NSTR = 4
                streams = [mkstream(i) for i in range(NSTR)]
                for h0 in range(0, SBH, NSTR):
                    gens = [head_stream(streams[i], h0 + i) for i in range(NSTR)]
                    alive = [True] * NSTR
                    while any(alive):
                        for gi_ in range(NSTR):
                            if alive[gi_]:
                                try:
                                    next(gens[gi_])
                                except StopIteration:
                                    alive[gi_] = False
                P.barrier()

        def phase_Bs(l):
            j = l // 2
            NC8 = NPG * SBH
            with contextlib.ExitStack() as s2:
                sbb = sbt(s2, "ssbb", [128, SBH], F32)
                ptf = sbt(s2, "ptf", [128, NPG], F32)
                pti = sbt(s2, "pti", [128, NPG], I32)
                idx = sbt(s2, "idx", [128, NPG], I32)
                k0 = Tk()
                DMA("sp", sbb[:], sb_biasB[j], writes=[k0])
                DMA("sp", pti[:], ptB[:, :], writes=[k0])
                CP("dve", ptf[:], pti[:], [k0], [k0])
                TS("dve", ptf[:], ptf[:], 128.0, cpk[:, 1664:1665], ALU.mult, ALU.add, [k0, k_c], [k0])
                if j > 0:
                    TS("dve", ptf[:], ptf[:], float(j * NPOOL * 128), None, ALU.add, None, [k0], [k0])
                CP("dve", idx[:], ptf[:], [k0], [k0])
                qcol = sbt(s2, "qcol", [128, SBH], F32)
                TS("dve", qcol[:], HS[:, 24:32], float(HD ** -0.5), None, ALU.mult, None, [k_hs], [k0])
                qb = sbt(s2, "qb", [128, SBH, 128], F32)
                dq = sbt(s2, "dq", [128, 128], F32)
                for h in range(SBH):
                    TS("dve", dq[:], identf, qcol[:, h:h + 1], None, ALU.mult, None, [k0, k_c], [k0])
                    MM(pb[0][:, 0:128], cpk[:, 1792:1920], dq[:], True, True, [k0, k_c], [kb[0]])
                    CP("act", qb[:, h, :], pb[0][:, 0:128], [kb[0]], [k0])
                Z = sbt(s2, "Z", [128, NPG, SBH], F32)
                kZ = Tk()
                kpg = [sbt(s2, "kpg%d" % i, [128, 1024], F32) for i in range(2)]
                kkpg = [Tk(), Tk()]
                junk = sbt(s2, "junk", [128, 128], F32)
                kj = Tk()
                cflat_k = cache_k.rearrange("e r c -> (e r) c")
                cflat_v = cache_v.rearrange("e r c -> (e r) c")
                for pg in range(NPG):
                    i = pg % 2
                    P.dma("pool", lambda e, i=i, pg=pg: e.indirect_dma_start(
                        out=kpg[i][:], out_offset=None, in_=cflat_k,
                        in_offset=bass.IndirectOffsetOnAxis(ap=idx[:, pg:pg + 1], axis=0)),
                        reads=[k0], writes=[kkpg[i]])
                    for h in range(SBH):
                        STT("dve", junk[:], kpg[i][:, h * 128:(h + 1) * 128], 1.0, qb[:, h, :], ALU.mult, ALU.mult,
                            [kkpg[i], k0], [kj])
                        P.op("dve", lambda e, pg=pg, h=h: e.tensor_reduce(
                            out=Z[:, pg, h:h + 1], in_=junk[:], axis=mybir.AxisListType.X, op=ALU.add),
                            [kj], [kZ])
                Zf = Z[:].rearrange("p g h -> p (g h)")
                for pg in range(NPG):
                    TT("dve", Z[:, pg, :], Z[:, pg, :], sbb[:], ALU.add, [kZ, k0], [kZ])
                Ee = sbt(s2, "Ee", [128, NC8], F32)
                SPs = sbt(s2, "SPs", [128, NC8], F32)
                TOT = sbt(s2, "TOT", [128, NC8], F32)
                TO2 = sbt(s2, "TO2", [128, NC8], F32)
                ACT(Ee[:], Zf, AF.Exp, [kZ], [kZ])
                ACT(SPs[:], Ee[:], AF.Ln, [kZ], [kZ], bias=1.0)
                ARG = sbt(s2, "ARG", [128, NC8], F32)
                for c0 in range(0, NC8, 512):
                    c1 = min(NC8, c0 + 512)
                    MM(pb[1][:, 0:c1 - c0], cpk[:, 1920:2048], SPs[:, c0:c1], True, True, [kZ, k_c], [kb[1]])
                    TT("dve", ARG[:, c0:c1], Zf[:, c0:c1], pb[1][:, 0:c1 - c0], ALU.subtract, [kZ, kb[1]], [kZ])
                    MM(pb[2][:, 0:c1 - c0], cpk[:, 1792:1920], SPs[:, c0:c1], True, True, [kZ, k_c], [kb[2]])
                    CP("act", TOT[:, c0:c1], pb[2][:, 0:c1 - c0], [kb[2]], [kZ])
                T3 = TOT[:].rearrange("p (g h) -> p g h", h=SBH)
                T4 = TO2[:].rearrange("p (g h) -> p g h", h=SBH)
                src, dst = T3, T4
                sh = 1
                while sh < NPG:
                    CP("dve", dst[:, NPG - sh:NPG, :], src[:, NPG - sh:NPG, :], [kZ], [kZ])
                    TT("dve", dst[:, 0:NPG - sh, :], src[:, 0:NPG - sh, :], src[:, sh:NPG, :], ALU.add, [kZ], [kZ])
                    src, dst = dst, src
                    sh *= 2
                A3 = ARG[:].rearrange("p (g h) -> p g h", h=SBH)
                if NPG > 1:
                    TT("dve", A3[:, 0:NPG - 1, :], A3[:, 0:NPG - 1, :], src[:, 1:NPG, :], ALU.subtract, [kZ], [kZ])
                Wg = sbt(s2, "Wg", [128, NPG, SBH], F32)
                ACT(Wg[:].rearrange("p g h -> p (g h)"), ARG[:], AF.Exp, [kZ], [kZ])
                vpg = [sbt(s2, "vpg%d" % i, [128, 1024], F32) for i in range(2)]
                kvpg = [Tk(), Tk()]
                for pg in range(NPG):
                    i = pg % 2
                    P.dma("pool", lambda e, i=i, pg=pg: e.indirect_dma_start(
                        out=vpg[i][:], out_offset=None, in_=cflat_v,
                        in_offset=bass.IndirectOffsetOnAxis(ap=idx[:, pg:pg + 1], axis=0)),
                        reads=[k0], writes=[kvpg[i]])
                    if pg == 0:
                        MM(pb[3][:, 0:SBH], zerob[:], zerob[:, 0:SBH], True, False, [k_c], [kb[3]])
                    for h in range(SBH):
                        MM(pb[3][:, h:h + 1], vpg[i][:, h * 128:(h + 1) * 128], Wg[:, pg, h:h + 1],
                           False, pg == NPG - 1 and h == SBH - 1, [kvpg[i], kZ], [kb[3]])
                gsl = sbt(s2, "gsl", [128, SBH], F32)
                ACT(gsl[:], HS[:, 48:56], AF.Silu, [k_hs], [k0])
                TT("dve", YS[:, 8:16, :].rearrange("p h o -> p (h o)"), pb[3][:, 0:SBH], gsl[:], ALU.mult,
                   [kb[3], k0], [k_ys])
                P.barrier()

        def phase_C(l):
            j = l // 2
            with contextlib.ExitStack() as s2:
                cw = sbt(s2, "ccw", [128, 64, 4], F32)
                kcw = Tk()
                DMA("sp", cw[:], conv_w_cT[j], writes=[kcw])
                PRE = [sbt(s2, "PRE%d" % i, [128, 4 + T], F32) for i in range(2)]
                kpre = [Tk(), Tk()]
                acc = sbt(s2, "cacc", [128, T], F32)
                kacc = Tk()
                sq = sbt(s2, "csq", [128, T], BF16)
                ksq = Tk()
                rs = [sbt(s2, "crs%d" % i, [128, 512], F32) for i in range(2)]
                krs = [Tk(), Tk()]
                ob = [sbt(s2, "cob%d" % i, [128, T], BF16) for i in range(2)]
                kob = [Tk(), Tk()]
                for i in range(2):
                    MEMSET("pool", PRE[i][:, 0:4], 0.0, [kpre[i]])
                nr = 0
                for cc in range(64):
                    i = cc % 2
                    r0 = cc * 128
                    DMA("sp", PRE[i][:, 4:4 + T], HT[r0:r0 + 128, :], writes=[kpre[i]])
                    DMA("pool", cc_pT[j, r0:r0 + 128, :], PRE[i][:, 1 + T:4 + T], reads=[kpre[i]])
                    TS("dve", acc[:], PRE[i][:, 1:1 + T], cw[:, cc, 0:1], None, ALU.mult, None, [kpre[i], kcw], [kacc])
                    for kk in range(1, 4):
                        STT("dve", acc[:], PRE[i][:, 1 + kk:1 + kk + T], cw[:, cc, kk:kk + 1], acc[:], ALU.mult,
                            ALU.add, [kpre[i], kcw, kacc], [kacc])
                    ACT(acc[:], acc[:], AF.Silu, [kacc], [kacc])
                    if cc < 32:
                        ACT(sq[:], acc[:], AF.Square, [kacc], [ksq])
                        for t in range(NTT):
                            b = nr % 4
                            q = nr % 2
                            nr += 1
                            MM(pb[b][:, :], ones1[:], sq[:, t * 512:(t + 1) * 512], True, True, [ksq, k_c], [kb[b]])
                            ACT(rs[q][:], pb[b][:, :], AF.Sqrt, [kb[b]], [krs[q]], bias=epsc[:, 1:2])
                            P.op("dve", lambda e, q=q: e.reciprocal(out=rs[q][:], in_=rs[q][:]), [krs[q]], [krs[q]])
                            if cc < 16:
                                STT("dve", ob[i][:, t * 512:(t + 1) * 512], acc[:, t * 512:(t + 1) * 512],
                                    float(HD ** -0.5), rs[q][:], ALU.mult, ALU.mult, [kacc, krs[q]], [kob[i]])
                            else:
                                TT("dve", ob[i][:, t * 512:(t + 1) * 512], acc[:, t * 512:(t + 1) * 512], rs[q][:],
                                   ALU.mult, [kacc, krs[q]], [kob[i]])
                    else:
                        CP("pool", ob[i][:], acc[:], [kacc], [kob[i]])
                    DMA("pool", QKVn[r0:r0 + 128, :], ob[i][:], reads=[kob[i]])
                FS = sbt(s2, "FS", [128, 64, 4], F32)
                kfs = Tk()
                DMA("sp", FS[:, :, 0:3], st_conv_cT[j].rearrange("(c p) k -> p c k", p=128), writes=[kfs])
                CP("dve", FS[:, :, 3], HS[:, 0:64], [k_hs], [kfs])
                DMA("pool", cc_sT[j].rearrange("(c p) k -> p c k", p=128), FS[:, :, 1:4], reads=[kfs])
                pr = sbt(s2, "cpr", [128, 64, 4], F32)
                TT("dve", pr[:], FS[:], cw[:], ALU.mult, [kfs, kcw], [kfs])
                P.op("dve", lambda e: e.tensor_reduce(out=SN[:, 0:64], in_=pr[:], axis=mybir.AxisListType.X,
                                                      op=ALU.add), [kfs], [k_sn])
                ACT(SN[:, 0:64], SN[:, 0:64], AF.Silu, [k_sn], [k_sn])
                sqs = sbt(s2, "csqs", [128, 32], F32)
                ACT(sqs[:], SN[:, 0:32], AF.Square, [k_sn], [kfs])
                MM(pb[5][:, 0:32], cpk[:, 1792:1920], sqs[:], True, True, [kfs, k_c], [kb[5]])
                ACT(sqs[:], pb[5][:, 0:32], AF.Sqrt, [kb[5]], [kfs], bias=epsc[:, 1:2])
                P.op("dve", lambda e: e.reciprocal(out=sqs[:], in_=sqs[:]), [kfs], [kfs])
                TT("dve", SN[:, 0:32], SN[:, 0:32], sqs[:], ALU.mult, [kfs, k_sn], [k_sn])
                TS("dve", SN[:, 0:16], SN[:, 0:16], float(HD ** -0.5), None, ALU.mult, None, [k_sn], [k_sn])
                P.barrier()

        def phase_G(l):
            j = l // 2
            NCH = T // 128
            NCS = NCH + 1
            with contextlib.ExitStack() as s2:
                BETA = sbt(s2, "BETA", [128, NCS, GV], F32)
                GC = sbt(s2, "GC", [128, NCS, GV], F32)
                EGC = sbt(s2, "EGC", [128, NCS, GV], F32)
                BE = sbt(s2, "BE", [128, NCS, GV], F32)
                EKD = sbt(s2, "EKD", [128, NCS, GV], F32)
                EGL = sbt(s2, "EGL", [128, NCS, GV], F32)
                GCT = sbt(s2, "GCT", [32, NCS, 128], F32)
                nGCT = sbt(s2, "nGCT", [32, NCS, 128], F32)
                SEL = sbt(s2, "SEL", [32, GV, 128], F32)
                gw = sbt(s2, "gw", [128, 1], F32)
                ktab = Tk()
                DMA("sp", SEL[:], selpack[:, :, :], writes=[ktab])
                DMA("sp", gw[:], gnorm_wT[j], writes=[ktab])
                with contextlib.ExitStack() as s3:
                    BA = sbt(s3, "BA", [64, T + 128], F32)
                    kba = Tk()
                    nea = sbt(s3, "nea", [128, GV], F32)
                    dtb = sbt(s3, "dtb", [128, GV], F32)
                    kq = Tk()
                    DMA("sp", BA[:, 0:T], HT[12288:12352, :], writes=[kba])
                    MEMSET("pool", BA[:, T:T + 128], 0.0, [kba])
                    CP("dve", BA[:, T:T + 1], HS[0:64, 96:97], [k_hs], [kba])
                    DMA("sp", nea[:], a_logB[j], writes=[kq])
                    DMA("sp", dtb[:], dt_biasB[j], writes=[kq])
                    ACT(nea[:], nea[:], AF.Exp, [kq], [kq])
                    TS("dve", nea[:], nea[:], -1.0, None, ALU.mult, None, [kq], [kq])
                    tmp = [sbt(s3, "g0t%d" % i, [128, GV], F32) for i in range(2)]
                    g32 = [sbt(s3, "g0g%d" % i, [128, GV], F32) for i in range(2)]
                    gl = [sbt(s3, "g0l%d" % i, [128, GV], F32) for i in range(2)]
                    kt = [Tk(), Tk()]
                    for n in range(NCS):
                        q = n % 2
                        MM(pb[0 + q][:, 0:64], BA[:, n * 128:(n + 1) * 128], identf[0:64, 0:64], True, True,
                           [kba, k_c], [kb[0 + q]])
                        ACT(BETA[:, n, :], pb[0 + q][:, 0:32], AF.Sigmoid, [kb[0 + q]], [ktab])
                        CP("act", tmp[q][:], pb[0 + q][:, 32:64], [kb[0 + q]], [kt[q]])
                        TT("dve", tmp[q][:], tmp[q][:], dtb[:], ALU.add, [kt[q], kq], [kt[q]])
                        ACT(tmp[q][:], tmp[q][:], AF.Exp, [kt[q]], [kt[q]])
                        ACT(tmp[q][:], tmp[q][:], AF.Ln, [kt[q]], [kt[q]], bias=1.0)
                        TT("dve", g32[q][:], tmp[q][:], nea[:], ALU.mult, [kt[q], kq], [kt[q]])
                        if n == NCH:
                            TS("dve", g32[q][:], g32[q][:], identf[:, 0:1], None, ALU.mult, None, [kt[q], k_c],
                               [kt[q]])
                        MM(pb[2 + q][:, 0:32], trif, g32[q][:], True, True, [kt[q], k_c], [kb[2 + q]])
                        CP("act", GC[:, n, :], pb[2 + q][:, 0:32], [kb[2 + q]], [ktab])
                        MM(pb[4 + q][0:32, 0:128], g32[q][:], trif, True, True, [kt[q], k_c], [kb[4 + q]])
                        CP("act", GCT[:, n, :], pb[4 + q][0:32, 0:128], [kb[4 + q]], [ktab])
                        TS("dve", nGCT[:, n, :], GCT[:, n, :], -1.0, None, ALU.mult, None, [ktab], [ktab])
                        MM(pb[6 + q][:, 0:32], sellast, GC[:, n, :], True, True, [ktab, k_c], [kb[6 + q]])
                        ACT(EGL[:, n, :], pb[6 + q][:, 0:32], AF.Exp, [kb[6 + q]], [ktab])
                        CP("act", gl[q][:], pb[6 + q][:, 0:32], [kb[6 + q]], [kt[q]])
                        TT("dve", gl[q][:], gl[q][:], GC[:, n, :], ALU.subtract, [kt[q], ktab], [kt[q]])
                        ACT(EKD[:, n, :], gl[q][:], AF.Exp, [kt[q]], [ktab])
                        ACT(EGC[:, n, :], GC[:, n, :], AF.Exp, [ktab], [ktab])
                        TT("dve", BE[:, n, :], BETA[:, n, :], EGC[:, n, :], ALU.mult, [ktab], [ktab])
                    P.barrier()
                qT = sbt(s2, "gqT", [128, T], BF16)
                kT = sbt(s2, "gkT", [128, T], BF16)
                vT = sbt(s2, "gvT", [128, 2, T], BF16)
                yTh = sbt(s2, "gyT", [128, 2, T], BF16)
                zB = [sbt(s2, "gzB%d" % i, [128, 2, 256], F32) for i in range(2)]
                kzB = [Tk(), Tk()]
                kin = Tk()
                kyT = Tk()
                sqT = sbt(s2, "sqT", [128, 128], BF16)
                skT = sbt(s2, "skT", [128, 128], BF16)
                svT = sbt(s2, "svT", [128, 2, 128], BF16)
                szT = sbt(s2, "szT", [128, 2, 128], F32)
                syT = sbt(s2, "syT", [128, 2, 128], BF16)
                ksin = Tk()
                ksy = Tk()
                S32 = sbt(s2, "S32", [128, 2, 128], F32)
                Sb = sbt(s2, "Sb", [128, 2, 128], BF16)
                kS = Tk()
                kSb = Tk()

                class TL:
                    def __init__(self, name, shape, dt, n=1):
                        self.t = [sbt(s2, "%s%d" % (name, i), list(shape), dt) for i in range(n)]
                        self.k = [Tk() for _ in range(n)]

                def fl(t, w):
                    return t[:].rearrange("p m d -> p (m d)")[:, 0:w]

                msk4 = sbt(s2, "msk4", [128, 7, 512], BF16)
                id4 = sbt(s2, "id4", [128, 512], BF16)
                up4 = sbt(s2, "up4", [128, 512], BF16)
                low2 = sbt(s2, "low2", [128, 256], F32)
                with contextlib.ExitStack() as s3:
                    mskf = sbt(s3, "mskf", [128, 7 * 128], F32)
                    DMA("sp", mskf[:], cpack2[:, :], writes=[ktab])
                    for r in range(4):
                        CP("dve", msk4[:, :, r * 128:(r + 1) * 128], mskf[:].rearrange("p (a b) -> p a b", b=128),
                           [ktab], [ktab])
                        CP("dve", id4[:, r * 128:(r + 1) * 128], identb[:], [k_c], [ktab])
                        CP("dve", up4[:, r * 128:(r + 1) * 128], upbig[:], [k_c], [ktab])
                    for r in range(2):
                        CP("dve", low2[:, r * 128:(r + 1) * 128], lowstrict[:], [k_c], [ktab])
                    P.barrier()
                ktok2 = TL("ktok2", [128, 2, 128], F32)
                KKs2 = TL("KKs2", [128, 2, 128], F32)
                QKs2 = TL("QKs2", [128, 2, 128], F32)
                bv4 = TL("bv4", [128, 4, 128], BF16, 2)
                kbg4 = TL("kbg4", [128, 4, 128], BF16)
                kdec4 = TL("kdec4", [128, 4, 128], BF16, 2)
                dec4 = TL("dec4", [128, 4, 128], F32)
                L4 = TL("L4", [128, 4, 128], BF16)
                A4 = TL("A4", [128, 4, 128], BF16)
                Mf4 = TL("Mf4", [128, 4, 128], BF16)
                AT4 = TL("AT4", [128, 4, 128], BF16, 2)
                nwT4 = TL("nwT4", [128, 4, 128], BF16, 2)
                Lp = TL("Lp", [128, 4, 128], BF16, 2)
                Mp = TL("Mp", [128, 4, 128], BF16, 2)
                Tn = TL("Tn", [128, 4, 128], BF16, 2)
                Tt = TL("Tt", [128, 4, 128], BF16, 4)
                Cs = [TL("Cs%d" % i, [128, 4, 128], BF16) for i in range(3)]
                Cts = [TL("Cts%d" % i, [128, 4, 128], BF16) for i in range(3)]
                IL = TL("IL", [128, 4, 128], BF16)
                IM = TL("IM", [128, 4, 128], BF16)
                Xs = TL("Xs", [128, 4, 128], BF16)
                X2s = TL("X2s", [128, 4, 128], BF16)
                vnb2 = TL("vnb2", [128, 2, 128], BF16)
                qS2 = TL("qS2", [128, 2, 128], F32)
                o2 = TL("o2", [128, 2, 128], F32)
                junk2 = TL("junk2", [128, 2, 128], F32)
                onb2 = TL("onb2", [128, 2, 128], BF16)
                ss2 = TL("ss2", [128, 2], F32)
                zg2 = TL("zg2", [128, 2, 128], F32)
                bank = [0]

                def nb():
                    bank[0] = (bank[0] + 1) % 8
                    return bank[0]

                def bc_d(tab, n, hv0):
                    return tab[:, n, hv0:hv0 + 2].unsqueeze(2).to_broadcast([128, 2, 128])

                def bc_v(t3, ci):
                    return t3[:, ci:ci + 1, :].to_broadcast([128, 2, 128])

                def mm4(dst_bank, nm, lhs_fn, rhs_fn, reads):
                    for m in range(nm):
                        MM(pb[dst_bank][:, m * 128:(m + 1) * 128], lhs_fn(m), rhs_fn(m), True, True, reads,
                           [kb[dst_bank]])

                ttc = [0]

                def stage1(bp, hq, cbs, kI):
                    hv0 = 2 * hq
                    nc_ = len(cbs)
                    nm = 2 * nc_
                    W = nm * 128
                    Wk = nc_ * 128
                    b = nb()
                    for ci, (n, qc, kc, vc) in enumerate(cbs):
                        MM(pb[b][:, ci * 128:(ci + 1) * 128], kc, identb[:], True, True, [kI, k_c], [kb[b]])
                    CP("act", fl(ktok2.t[0], Wk), pb[b][:, 0:Wk], [kb[b]], [ktok2.k[0]])
                    yield
                    b = nb()
                    for ci, (n, qc, kc, vc) in enumerate(cbs):
                        MM(pb[b][:, ci * 128:(ci + 1) * 128], kc, kc, True, True, [kI], [kb[b]])
                    TT("dve", fl(KKs2.t[0], Wk), pb[b][:, 0:Wk], low2[:, 0:Wk], ALU.mult, [kb[b], ktab], [KKs2.k[0]])
                    yield
                    b = nb()
                    for ci, (n, qc, kc, vc) in enumerate(cbs):
                        MM(pb[b][:, ci * 128:(ci + 1) * 128], qc, kc, True, True, [kI], [kb[b]])
                    CP("act", fl(QKs2.t[0], Wk), pb[b][:, 0:Wk], [kb[b]], [QKs2.k[0]])
                    yield
                    b = nb()
                    for ci, (n, qc, kc, vc) in enumerate(cbs):
                        for vh in range(2):
                            m = 2 * ci + vh
                            MM(pb[b][:, m * 128:(m + 1) * 128], vc(vh), identb[:], True, True, [kI, k_c], [kb[b]])
                    for ci, (n, qc, kc, vc) in enumerate(cbs):
                        pv = pb[b][:, 2 * ci * 128:(2 * ci + 2) * 128].rearrange("p (v d) -> p v d", d=128)
                        TT("dve", bv4.t[bp][:, 2 * ci:2 * ci + 2, :], pv, bc_d(BETA, n, hv0), ALU.mult,
                           [kb[b], ktab], [bv4.k[bp]])
                        TT("pool", kbg4.t[0][:, 2 * ci:2 * ci + 2, :], bc_v(ktok2.t[0], ci), bc_d(BE, n, hv0),
                           ALU.mult, [ktok2.k[0], ktab], [kbg4.k[0]])
                        TT("pool", kdec4.t[bp][:, 2 * ci:2 * ci + 2, :], bc_v(ktok2.t[0], ci), bc_d(EKD, n, hv0),
                           ALU.mult, [ktok2.k[0], ktab], [kdec4.k[bp]])
                    b = nb()
                    MM(pb[b][:, 0:W], identb[:], up4[:, 0:W], True, False, [k_c, ktab], [kb[b]])
                    for ci, (n, qc, kc, vc) in enumerate(cbs):
                        for vh in range(2):
                            m = 2 * ci + vh
                            hv = hv0 + vh
                            MM(pb[b][:, m * 128:(m + 1) * 128], SEL[:, hv, :], GCT[:, n, :], False, False, [ktab],
                               [kb[b]])
                            MM(pb[b][:, m * 128:(m + 1) * 128], nGCT[:, n, :], SEL[:, hv, :], False,
                               m == nm - 1, [ktab], [kb[b]])
                    ACT(fl(dec4.t[0], W), pb[b][:, 0:W], AF.Exp, [kb[b]], [dec4.k[0]], scale=-1.0)
                    yield
                    for ci, (n, qc, kc, vc) in enumerate(cbs):
                        sl = slice(2 * ci, 2 * ci + 2)
                        TT("dve", L4.t[0][:, sl, :], dec4.t[0][:, sl, :], bc_d(BETA, n, hv0), ALU.mult,
                           [dec4.k[0], ktab], [L4.k[0]])
                        TT("dve", L4.t[0][:, sl, :], L4.t[0][:, sl, :], bc_v(KKs2.t[0], ci), ALU.mult,
                           [KKs2.k[0], L4.k[0]], [L4.k[0]])
                        TT("pool", A4.t[0][:, sl, :], dec4.t[0][:, sl, :], bc_v(QKs2.t[0], ci), ALU.mult,
                           [dec4.k[0], QKs2.k[0]], [A4.k[0]])
                    b = nb()
                    mm4(b, nm, lambda m: L4.t[0][:, m, :], lambda m: identb[:], [L4.k[0], k_c])
                    CP("act", fl(Mf4.t[0], W), pb[b][:, 0:W], [kb[b]], [Mf4.k[0]])
                    yield
                    b = nb()
                    mm4(b, nm, lambda m: A4.t[0][:, m, :], lambda m: identb[:], [A4.k[0], k_c])
                    CP("act", fl(AT4.t[bp], W), pb[b][:, 0:W], [kb[b]], [AT4.k[bp]])
                    yield
                    mk_ = lambda i: msk4[:, i, 0:W]
                    TT("pool", fl(Lp.t[0], W), fl(L4.t[0], W), mk_(0), ALU.mult, [L4.k[0], ktab], [Lp.k[0]])
                    TT("pool", fl(Mp.t[0], W), fl(Mf4.t[0], W), mk_(0), ALU.mult, [Mf4.k[0], ktab], [Mp.k[0]])
                    for si in range(3):
                        TT("pool", fl(Cs[si].t[0], W), fl(L4.t[0], W), mk_(1 + si), ALU.mult, [L4.k[0], ktab],
                           [Cs[si].k[0]])
                        TT("pool", fl(Cts[si].t[0], W), fl(Mf4.t[0], W), mk_(4 + si), ALU.mult, [Mf4.k[0], ktab],
                           [Cts[si].k[0]])
                    tb0 = 2 * bp
                    TT("pool", fl(Tn.t[0], W), id4[:, 0:W], fl(Lp.t[0], W), ALU.subtract, [Lp.k[0], ktab], [Tn.k[0]])
                    TT("pool", fl(Tt.t[tb0], W), id4[:, 0:W], fl(Mp.t[0], W), ALU.subtract, [Mp.k[0], ktab],
                       [Tt.k[tb0]])
                    cl, cm, ct_, cn = 0, 0, 0, 0
                    for lev in range(3):
                        nl, nm_ = 1 - cl, 1 - cm
                        bl = nb()
                        mm4(bl, nm, lambda m: Mp.t[cm][:, m, :], lambda m: Lp.t[cl][:, m, :], [Lp.k[cl], Mp.k[cm]])
                        bm = nb()
                        mm4(bm, nm, lambda m: Lp.t[cl][:, m, :], lambda m: Mp.t[cm][:, m, :], [Lp.k[cl], Mp.k[cm]])
                        CP("act", fl(Lp.t[nl], W), pb[bl][:, 0:W], [kb[bl]], [Lp.k[nl]])
                        CP("dve", fl(Mp.t[nm_], W), pb[bm][:, 0:W], [kb[bm]], [Mp.k[nm_]])
                        yield
                        TT("dve", fl(IL.t[0], W), fl(Lp.t[nl], W), id4[:, 0:W], ALU.add, [Lp.k[nl], ktab], [IL.k[0]])
                        TT("pool", fl(IM.t[0], W), fl(Mp.t[nm_], W), id4[:, 0:W], ALU.add, [Mp.k[nm_], ktab],
                           [IM.k[0]])
                        nt, nn = 1 - ct_, 1 - cn
                        bt_ = nb()
                        mm4(bt_, nm, lambda m: IL.t[0][:, m, :], lambda m: Tt.t[tb0 + ct_][:, m, :],
                            [IL.k[0], Tt.k[tb0 + ct_]])
                        CP("act", fl(Tt.t[tb0 + nt], W), pb[bt_][:, 0:W], [kb[bt_]], [Tt.k[tb0 + nt]])
                        bn_ = nb()
                        mm4(bn_, nm, lambda m: IM.t[0][:, m, :], lambda m: Tn.t[cn][:, m, :], [IM.k[0], Tn.k[cn]])
                        CP("dve", fl(Tn.t[nn], W), pb[bn_][:, 0:W], [kb[bn_]], [Tn.k[nn]])
                        yield
                        cl, cm, ct_, cn = nl, nm_, nt, nn
                    for si in range(3):
                        nt, nn = 1 - ct_, 1 - cn
                        bx2 = nb()
                        mm4(bx2, nm, lambda m: Cs[si].t[0][:, m, :], lambda m: Tt.t[tb0 + ct_][:, m, :],
                            [Cs[si].k[0], Tt.k[tb0 + ct_]])
                        CP("dve", fl(X2s.t[0], W), pb[bx2][:, 0:W], [kb[bx2]], [X2s.k[0]])
                        yield
                        if si < 2:
                            bx = nb()
                            mm4(bx, nm, lambda m: Cts[si].t[0][:, m, :], lambda m: Tn.t[cn][:, m, :],
                                [Cts[si].k[0], Tn.k[cn]])
                            CP("act", fl(Xs.t[0], W), pb[bx][:, 0:W], [kb[bx]], [Xs.k[0]])
                        by2 = nb()
                        mm4(by2, nm, lambda m: Tn.t[cn][:, m, :], lambda m: X2s.t[0][:, m, :], [Tn.k[cn], X2s.k[0]])
                        TT("dve", fl(Tt.t[tb0 + nt], W), fl(Tt.t[tb0 + ct_], W), pb[by2][:, 0:W], ALU.subtract,
                           [Tt.k[tb0 + ct_], kb[by2]], [Tt.k[tb0 + nt]])
                        yield
                        if si < 2:
                            by = nb()
                            mm4(by, nm, lambda m: Tt.t[tb0 + ct_][:, m, :], lambda m: Xs.t[0][:, m, :],
                                [Tt.k[tb0 + ct_], Xs.k[0]])
                            TT("dve", fl(Tn.t[nn], W), fl(Tn.t[cn], W), pb[by][:, 0:W], ALU.subtract,
                               [Tn.k[cn], kb[by]], [Tn.k[nn]])
                            cn = nn
                        ct_ = nt
                    ti = tb0 + ct_
                    b = nb()
                    mm4(b, nm, lambda m: kbg4.t[0][:, m, :], lambda m: Tt.t[ti][:, m, :], [kbg4.k[0], Tt.k[ti]])
                    ACT(fl(nwT4.t[bp], W), pb[b][:, 0:W], AF.Copy, [kb[b]], [nwT4.k[bp]], scale=-1.0)
                    yield

                def stage2(bp, ti, hq, ci, n, qc, zc, yout, kZ, kY, kI):
                    hv0 = 2 * hq
                    ti = 2 * bp
                    m0 = 2 * ci
                    TiT = Tt.t[ti]
                    kTi = Tt.k[ti]
                    b = nb()
                    for vh in range(2):
                        MM(pb[b][:, vh * 128:(vh + 1) * 128], TiT[:, m0 + vh, :], bv4.t[bp][:, m0 + vh, :], True, False,
                           [kTi, bv4.k[bp]], [kb[b]])
                        MM(pb[b][:, vh * 128:(vh + 1) * 128], nwT4.t[bp][:, m0 + vh, :], Sb[:, vh, :], False, True,
                           [nwT4.k[bp], kSb], [kb[b]])
                    CP("dve", fl(vnb2.t[0], 256), pb[b][:, 0:256], [kb[b]], [vnb2.k[0]])
                    yield
                    b = nb()
                    MM(pb[b][:, 0:256], qc, Sb[:].rearrange("p v d -> p (v d)"), True, True, [kI, kSb], [kb[b]])
                    TT("dve", qS2.t[0][:], pb[b][:, 0:256].rearrange("p (v d) -> p v d", d=128), bc_d(EGC, n, hv0),
                       ALU.mult, [kb[b], ktab], [qS2.k[0]])
                    b = nb()
                    for vh in range(2):
                        MM(pb[b][:, vh * 128:(vh + 1) * 128], AT4.t[bp][:, m0 + vh, :], vnb2.t[0][:, vh, :], True, True,
                           [AT4.k[bp], vnb2.k[0]], [kb[b]])
                    TT("dve", fl(o2.t[0], 256), pb[b][:, 0:256], fl(qS2.t[0], 256), ALU.add, [kb[b], qS2.k[0]],
                       [o2.k[0]])
                    yield
                    b = nb()
                    for vh in range(2):
                        MM(pb[b][:, vh * 128:(vh + 1) * 128], kdec4.t[bp][:, m0 + vh, :], vnb2.t[0][:, vh, :], True, True,
                           [kdec4.k[bp], vnb2.k[0]], [kb[b]])
                    TT("dve", S32[:], S32[:], bc_d(EGL, n, hv0), ALU.mult, [kS, kSb, ktab], [kS])
                    TT("dve", S32[:].rearrange("p v d -> p (v d)"), S32[:].rearrange("p v d -> p (v d)"),
                       pb[b][:, 0:256], ALU.add, [kS, kb[b]], [kS])
                    CP("pool", Sb[:], S32[:], [kS], [kSb])
                    yield
                    ACT(junk2.t[0][:], o2.t[0][:], AF.Square, [o2.k[0]], [junk2.k[0]])
                    P.op("dve", lambda e: e.tensor_reduce(out=ss2.t[0][:], in_=junk2.t[0][:],
                                                          axis=mybir.AxisListType.X, op=ALU.add),
                         [junk2.k[0]], [ss2.k[0]])
                    ACT(ss2.t[0][:], ss2.t[0][:], AF.Sqrt, [ss2.k[0]], [ss2.k[0]], bias=epsc[:, 1:2], scale=1.0 / HD)
                    P.op("dve", lambda e: e.reciprocal(out=ss2.t[0][:], in_=ss2.t[0][:]), [ss2.k[0]], [ss2.k[0]])
                    TT("pool", onb2.t[0][:], o2.t[0][:], ss2.t[0][:].unsqueeze(2).to_broadcast([128, 2, 128]), ALU.mult,
                       [o2.k[0], ss2.k[0]], [onb2.k[0]])
                    yield
                    b = nb()
                    for vh in range(2):
                        MM(pb[b][:, vh * 128:(vh + 1) * 128], onb2.t[0][:, vh, :], identb[:], True, True,
                           [onb2.k[0], k_c], [kb[b]])
                    ACT(zg2.t[0][:], zc, AF.Silu, [kZ], [zg2.k[0]])
                    STT("dve", yout, pb[b][:, 0:256].rearrange("p (v d) -> p v d", d=128), gw[:, 0:1], zg2.t[0][:],
                        ALU.mult, ALU.mult, [kb[b], ktab, zg2.k[0]], [kY])
                    yield

                nbatch = [0]

                def run_all(g):
                    for _ in g:
                        pass

                def interleave(gs):
                    gs = [g for g in gs if g is not None]
                    alive = [True] * len(gs)
                    while any(alive):
                        for i_, g in enumerate(gs):
                            if alive[i_]:
                                try:
                                    next(g)
                                except StopIteration:
                                    alive[i_] = False

                def chain(*gens):
                    for g in gens:
                        yield from g

                for hq in range(GQ):
                    DMA("sp", qT[:], QKVn[hq * 128:(hq + 1) * 128, :], writes=[kin])
                    DMA("sp", kT[:], QKVn[2048 + hq * 128:2048 + (hq + 1) * 128, :], writes=[kin])
                    DMA("sp", vT[:], QKVn[4096 + 2 * hq * 128:4096 + (2 * hq + 2) * 128, :].rearrange(
                        "(v p) t -> p v t", p=128), writes=[kin])
                    MEMSET("pool", S32[:], 0.0, [kS])
                    MEMSET("pool", Sb[:], 0.0, [kSb])
                    MEMSET("pool", sqT[:], 0.0, [ksin])
                    MEMSET("pool", skT[:], 0.0, [ksin])
                    MEMSET("pool", svT[:], 0.0, [ksin])
                    MEMSET("pool", szT[:], 0.0, [ksin])
                    CP("dve", sqT[:, 0:1], SN[:, hq:hq + 1], [k_sn], [ksin])
                    CP("dve", skT[:, 0:1], SN[:, 16 + hq:17 + hq], [k_sn], [ksin])
                    for vh in range(2):
                        CP("dve", svT[:, vh, 0:1], SN[:, 32 + 2 * hq + vh:33 + 2 * hq + vh], [k_sn], [ksin])
                        CP("dve", szT[:, vh, 0:1], HS[:, 64 + 2 * hq + vh:65 + 2 * hq + vh], [k_hs], [ksin])
                    batches = []
                    for n0 in range(0, NCH, 2):
                        cbs = []
                        for n in range(n0, min(n0 + 2, NCH)):
                            c0 = n * 128
                            cbs.append((n, qT[:, c0:c0 + 128], kT[:, c0:c0 + 128],
                                        (lambda vh, c0=c0: vT[:, vh, c0:c0 + 128])))
                        batches.append((n0, cbs, kin))
                    batches.append((NCH, [(NCH, sqT[:], skT[:], (lambda vh: svT[:, vh, :]))], ksin))

                    def start1(bi):
                        n0, cbs, kI = batches[bi]
                        bp = (nbatch[0] + bi) % 2
                        if bi < len(batches) - 1:
                            wz = len(cbs) * 128
                            DMA("sp", zB[bp][:, :, 0:wz],
                                HT[8192 + 2 * hq * 128:8192 + (2 * hq + 2) * 128, n0 * 128:n0 * 128 + wz].rearrange(
                                    "(v p) t -> p v t", p=128), writes=[kzB[bp]])
                        return stage1(bp, hq, cbs, kI)

                    def make2(bi):
                        n0, cbs, kI = batches[bi]
                        bp = (nbatch[0] + bi) % 2
                        gs = []
                        if bi < len(batches) - 1:
                            for ci, (n, qc, kc, vc) in enumerate(cbs):
                                c0 = n * 128
                                gs.append(stage2(bp, 0, hq, ci, n, qc, zB[bp][:, :, ci * 128:(ci + 1) * 128],
                                                 yTh[:, :, c0:c0 + 128], kzB[bp], kyT, kI))
                        else:
                            gs.append(stage2(bp, 0, hq, 0, NCH, sqT[:], szT[:], syT[:], ksin, ksy, ksin))
                        return chain(*gs)

                    run_all(start1(0))
                    for bi in range(len(batches)):
                        g1 = start1(bi + 1) if bi + 1 < len(batches) else None
                        if bi == len(batches) - 1:
                            DMA("pool", YT[2 * hq * 128:(2 * hq + 2) * 128, :].rearrange("(v p) t -> p v t", p=128),
                                yTh[:], reads=[kyT])
                            DMA("pool", dl_p[j, 2 * hq:2 * hq + 2].rearrange("v k d -> k v d"), S32[:], reads=[kS])
                            DMA("sp", S32[:], st_delta[j, 2 * hq:2 * hq + 2].rearrange("v k d -> k v d"), writes=[kS])
                            CP("pool", Sb[:], S32[:], [kS], [kSb])
                        interleave([g1, make2(bi)])
                    nbatch[0] += len(batches)
                    CP("dve", YS[:, 2 * hq:2 * hq + 2, :].rearrange("p v o -> p (v o)"), syT[:, :, 0], [ksy], [k_ys])
                    DMA("pool", dl_s[j, 2 * hq:2 * hq + 2].rearrange("v k d -> k v d"), S32[:], reads=[kS])
                P.barrier()

        def phase_GDN(l):
            import os
            KG = os.environ.get("KG", "c,g").split(",")
            if "c" in KG:
                phase_C(l)
            if "g" in KG:
                phase_G(l)

        def phase_O(l):
            even = (l % 2 == 0)
            j = l // 2
            KY = 16 if even else 32
            w_out = w_out_even[j] if even else w_out_odd[j]
            TO = 256 if even else 128
            last = (l == DEPTH - 1)
            Xsrc = xT_in if l == 0 else X
            Xdst = yT_out if last else X
            with contextlib.ExitStack() as s2:
                Wo = sbt(s2, "Wo", [128, KY, D], BF16)
                kWo = Tk()
                wst = [sbt(s2, "ow%d" % i, [128, D], F32) for i in range(2)]
                kws = [Tk(), Tk()]
                lg = sbt(s2, "lg", [128, KC], F32)
                lb = sbt(s2, "lb", [128, KC], F32)
                klg = Tk()
                DMA("sp", lg[:], ln_gT[l], writes=[klg])
                DMA("sp", lb[:], ln_bT[l], writes=[klg])
                for k in range(KY):
                    DMA("sp", wst[k % 2][:], w_out[k * 128:(k + 1) * 128, :], writes=[kws[k % 2]])
                    CP("pool" if k % 2 else "dve", Wo[:, k, :], wst[k % 2][:], [kws[k % 2]], [kWo])
                Yt = [sbt(s2, "Yt%d" % i, [128, KY, TO], BF16) for i in range(2)]
                kYt = [Tk(), Tk()]
                Xt = [sbt(s2, "Xt%d" % i, [128, KC, TO], F32) for i in range(2)]
                kXt = [Tk(), Tk()]
                rb = sbt(s2, "rb", [128, KC, TO], BF16)
                rsq = sbt(s2, "rsq", [128, KC, TO], BF16)
                krb = Tk()
                mean = sbt(s2, "mean", [128, TO], F32)
                rstd = sbt(s2, "rstd", [128, TO], F32)
                kst = Tk()
                tmp = [sbt(s2, "otmp%d" % i, [128, TO], F32) for i in range(2)]
                ktmp = [Tk(), Tk()]
                ntile = T // TO
                br = 0
                for ti in range(ntile + 1):
                    samp = (ti == ntile)
                    N = 1 if samp else TO
                    i = ti % 2
                    r = 1 if samp else 0
                    if samp:
                        yt_ap = YS[:, 0:KY, :]
                        kyt = k_ys
                        xt_t = XS
                        kxt = k_xs
                    else:
                        t0 = ti * TO
                        DMA("sp", Yt[i][:], YT[0:KY * 128, t0:t0 + TO].rearrange("(k p) t -> p k t", p=128),
                            writes=[kYt[i]])
                        DMA("sp", Xt[i][:], Xsrc[:, t0:t0 + TO].rearrange("(k p) t -> p k t", p=128),
                            writes=[kXt[i]])
                        yt_ap = Yt[i][:]
                        kyt = kYt[i]
                        xt_t = Xt[i]
                        kxt = kXt[i]
                    for d in range(KC):
                        b = (0, 1, 2, 3, 6, 7)[br % 6]
                        br += 1
                        for k in range(KY):
                            MM(pb[b][:, 0:N], Wo[:, k, d * 128:(d + 1) * 128], yt_ap[:, k, :], k == 0, k == KY - 1,
                               [kWo, kyt], [kb[b]])
                        ACT(xt_t[:, d, :], xt_t[:, d, :], AF.Copy, [kxt], [kxt], scale=float(ALPHA))
                        STT("dve", xt_t[:, d, :], pb[b][:, 0:N], modT[:, 32 + d, r:r + 1], xt_t[:, d, :],
                            ALU.mult, ALU.add, [kb[b], k_mod, kxt], [kxt])
                        CP("pool", rb[:, d, 0:N], xt_t[:, d, :], [kxt], [krb])
                        ACT(rsq[:, d, 0:N], xt_t[:, d, :], AF.Square, [kxt], [krb])
                    for d in range(KC):
                        MM(pb[4][:, 0:N], onesD[:], rb[:, d, 0:N], d == 0, d == KC - 1, [krb, k_c], [kb[4]])
                    for d in range(KC):
                        MM(pb[5][:, 0:N], onesD[:], rsq[:, d, 0:N], d == 0, d == KC - 1, [krb, k_c], [kb[5]])
                    CP("act", mean[:, 0:N], pb[4][:, 0:N], [kb[4]], [kst])
                    ACT(rstd[:, 0:N], pb[4][:, 0:N], AF.Square, [kb[4]], [kst])
                    TT("dve", rstd[:, 0:N], pb[5][:, 0:N], rstd[:, 0:N], ALU.subtract, [kb[5], kst], [kst])
                    ACT(rstd[:, 0:N], rstd[:, 0:N], AF.Sqrt, [kst], [kst], bias=epsc[:, 0:1])
                    P.op("dve", lambda e: e.reciprocal(out=rstd[:, 0:N], in_=rstd[:, 0:N]), [kst], [kst])
                    for d in range(KC):
                        q = d % 2
                        TT("pool", tmp[q][:, 0:N], xt_t[:, d, :], mean[:, 0:N], ALU.subtract, [kxt, kst], [ktmp[q]])
                        TT("dve", tmp[q][:, 0:N], tmp[q][:, 0:N], rstd[:, 0:N], ALU.mult, [ktmp[q], kst], [ktmp[q]])
                        ACT(xt_t[:, d, :], tmp[q][:, 0:N], AF.Identity, [ktmp[q], klg], [kxt],
                            bias=lb[:, d:d + 1], scale=lg[:, d:d + 1])
                    if samp:
                        if last:
                            DMA("pool", ysT_out[:, :, :], XS[:], reads=[k_xs])
                    else:
                        DMA("pool", Xdst[:, t0:t0 + TO].rearrange("(k p) t -> p k t", p=128), Xt[i][:],
                            reads=[kXt[i]])
                P.barrier()

        import os
        PH = os.environ.get("KPH", "M,UP,A,B,Bs,G,O").split(",")
        for l in range(DEPTH):
            if "M" in PH:
                phase_M(l)
            if "UP" in PH:
                phase_UP(l)
            if l % 2 == 0:
                if "A" in PH:
                    phase_A(l)
                if "B" in PH:
                    phase_B(l)
                if "Bs" in PH:
                    phase_Bs(l)
            else:
                if "G" in PH:
                    phase_GDN(l)
            if "O" in PH:
                phase_O(l)
        P.barrier()
    return nc


def make_consts():
    cp = np.zeros((128, 2048), np.float32)
    m = np.arange(128)[:, None]
    jj = np.arange(128)[None, :]
    cp[:, 0:128] = np.eye(128, dtype=np.float32)
    cp[:, 128:256] = np.where(m >= jj, -1.0, 0.0)
    i9 = np.arange(896)[None, :]
    cp[:, 256:1152] = np.where(m >= (i9 - 384), NEG, 0.0)
    cp[:, 1152:1280] = np.where(m <= jj, 1.0, 0.0)
    cp[:, 1280:1408] = np.where(jj > m, -NEG, 0.0)
    cp[:, 1408:1536] = np.where(m == 127, 1.0, 0.0)
    cp[:, 1536:1664] = np.where(m > jj, 1.0, 0.0)
    cp[:, 1664] = np.arange(128)
    cp[:, 1792:1920] = 1.0
    cp[:, 1920:2048] = np.where(m >= jj, 1.0, 0.0)
    cp2 = np.zeros((128, 7, 128), np.float32)
    cp2[:, 0] = (m // 16 == jj // 16)
    for si, sz in enumerate((16, 32, 64)):
        off = ((m // (2 * sz) == jj // (2 * sz)) & (m % (2 * sz) >= sz) & (jj % (2 * sz) < sz)).astype(np.float32)
        cp2[:, 1 + si] = off
        cp2[:, 4 + si] = off.T
    global CP2
    CP2 = cp2.reshape(128, 7 * 128)
    sel = np.zeros((32, GV, 128), np.float32)
    for h in range(GV):
        sel[h, h, :] = 1.0
    return cp, sel


def fm(v):
    n = v.shape[-1] // 128
    return np.ascontiguousarray(np.moveaxis(v.reshape(v.shape[:-1] + (n, 128)), -1, -2))


_CACHE = {}


def kernel(x_prompt, x_sample, c_prompt, c_sample, cache_k, cache_v, page_table,
           state_conv_a, state_conv_c, state_delta, w_ada, b_ada, ln_g, ln_b,
           w_in_even, conv_w_a, gn_g_a, gn_b_a, sb_bias, w_out_even,
           w_in_odd, conv_w_c, a_log_c, dt_bias_c, gnorm_w_c, w_out_odd):
    A = lambda v: np.ascontiguousarray(np.asarray(v))
    x_prompt = A(x_prompt); x_sample = A(x_sample); c_prompt = A(c_prompt); c_sample = A(c_sample)
    B, T, _ = x_prompt.shape
    NS = x_sample.shape[0]
    DEPTH = w_ada.shape[0]
    NE = (DEPTH + 1) // 2
    NO = DEPTH // 2
    NPOOL = cache_k.shape[1]
    NPG = page_table.shape[1]
    ncores = 8
    key = (T, NPG, NPOOL, DEPTH)
    if key not in _CACHE:
        _CACHE[key] = build(*key)
    nc = _CACHE[key]
    cp, sel = make_consts()
    NO1 = max(NO, 1)

    def pad_odd(a, shape):
        a = A(a)
        if a.shape[0] == 0:
            return np.zeros((1,) + tuple(shape), np.float32)
        return a

    shared = {
        "w_ada": A(w_ada),
        "b_adaT": fm(A(b_ada)),
        "ln_gT": fm(A(ln_g)), "ln_bT": fm(A(ln_b)),
        "w_in_even": A(w_in_even), "w_out_even": A(w_out_even),
        "conv_w_aT": np.ascontiguousarray(A(conv_w_a).reshape(NE, CONVW_A, 8, 128).transpose(0, 3, 2, 1)),
        "gn_gT": fm(A(gn_g_a)), "gn_bT": fm(A(gn_b_a)),
        "sb_biasB": np.ascontiguousarray(np.broadcast_to(A(sb_bias)[:, None, :], (NE, 128, SBH))),
        "cache_k": A(cache_k).reshape(NE, NPOOL * 128, 1024),
        "cache_v": A(cache_v).reshape(NE, NPOOL * 128, 1024),
        "cpack": cp, "selpack": sel, "cpack2": CP2,
    }
    if NO:
        shared.update({
            "w_in_odd": A(w_in_odd), "w_out_odd": A(w_out_odd),
            "conv_w_cT": np.ascontiguousarray(A(conv_w_c).reshape(NO, 4, 64, 128).transpose(0, 3, 2, 1)),
            "a_logB": np.ascontiguousarray(np.broadcast_to(A(a_log_c)[:, None, :], (NO, 128, GV))),
            "dt_biasB": np.ascontiguousarray(np.broadcast_to(A(dt_bias_c)[:, None, :], (NO, 128, GV))),
            "gnorm_wT": np.ascontiguousarray(A(gnorm_w_c)[:, :, None]),
        })
    else:
        shared.update({
            "w_in_odd": np.zeros((1, D, IN_ODD), np.float32), "w_out_odd": np.zeros((1, 2 * D, D), np.float32),
            "conv_w_cT": np.zeros((1, 128, 64, 4), np.float32), "a_logB": np.zeros((1, 128, GV), np.float32),
            "dt_biasB": np.zeros((1, 128, GV), np.float32), "gnorm_wT": np.zeros((1, 128, 1), np.float32),
        })
    in_maps = []
    for c in range(ncores):
        b = (c * B) // ncores
        s = c % NS
        m = dict(shared)
        m["xT"] = np.ascontiguousarray(x_prompt[b].T)
        m["xsT"] = fm(x_sample[s, 0])[:, :, None].copy()
        cc = np.stack([c_prompt[b], c_sample[s]], -1)
        m["cT"] = np.ascontiguousarray(cc.reshape(KC, 128, 2).transpose(1, 0, 2))
        m["ptB"] = np.ascontiguousarray(np.broadcast_to(A(page_table)[s][None, :], (128, NPG))).astype(np.int32)
        m["st_conv_aT"] = np.ascontiguousarray(A(state_conv_a)[:, s].transpose(0, 2, 1))
        if NO:
            m["st_conv_cT"] = np.ascontiguousarray(A(state_conv_c)[:, s].transpose(0, 2, 1))
            m["st_delta"] = np.ascontiguousarray(A(state_delta)[:, s])
        else:
            m["st_conv_cT"] = np.zeros((1, 8192, 3), np.float32)
            m["st_delta"] = np.zeros((1, GV, 128, 128), np.float32)
        in_maps.append(m)
    import os
    if os.environ.get("KTRACE"):
        res = run_bass_kernel_spmd(nc, in_maps, core_ids=list(range(ncores)), trace=True)
        print("EXEC_TIME_NS", res.exec_time_ns, flush=True)
    else:
        res = run_bass_kernel_spmd(nc, in_maps, core_ids=list(range(ncores)))
    R = res.results
    global LAST_R
    LAST_R = R
    cores_b = [(b * ncores) // B for b in range(B)]
    y_p = np.stack([R[c]["yT"].T for c in cores_b]).astype(np.float32)
    y_s = np.stack([R[s]["ysT"][:, :, 0].T.reshape(1, D) for s in range(NS)]).astype(np.float32)
    nk_p = np.stack([R[c]["nk_p"] for c in cores_b], 1).reshape(NE, B, T, SBH, HD)
    nv_p = np.stack([R[c]["nv_p"] for c in cores_b], 1).reshape(NE, B, T, SBH, HD)
    nk_s = np.stack([R[s]["nk_s"] for s in range(NS)], 1).reshape(NE, NS, 1, SBH, HD)
    nv_s = np.stack([R[s]["nv_s"] for s in range(NS)], 1).reshape(NE, NS, 1, SBH, HD)
    ca_p = np.stack([R[c]["ca_pT"].transpose(0, 2, 1) for c in cores_b], 1)
    ca_s = np.stack([R[s]["ca_sT"].transpose(0, 2, 1) for s in range(NS)], 1)
    cc_p = np.stack([R[c]["cc_pT"].transpose(0, 2, 1) for c in cores_b], 1)[:NO]
    cc_s = np.stack([R[s]["cc_sT"].transpose(0, 2, 1) for s in range(NS)], 1)[:NO]
    dl_p = np.stack([R[c]["dl_p"] for c in cores_b], 1)[:NO]
    dl_s = np.stack([R[s]["dl_s"] for s in range(NS)], 1)[:NO]
    f = lambda a: np.ascontiguousarray(a, dtype=np.float32)
    return tuple(f(a) for a in (y_p, y_s, nk_p, nv_p, nk_s, nv_s, ca_p, ca_s, cc_p, cc_s, dl_p, dl_s))
```
